# Optimizing a Trainium2 kernel written in Bass

```python
import math
import jax, jax.numpy as jnp
from jax import lax
import numpy as np

D_MODEL = 1024
BATCH = 16
SEQ = 2048
DEPTH = 1
DEC_BATCH = 128
DEC_SEQ = 8
PAST_LEN = 8192
PAGE_SIZE = 128

MIX_WIDTH = D_MODEL
ATT_WIDTH = MIX_WIDTH // 2
LRU_WIDTH = MIX_WIDTH - ATT_WIDTH
HEAD_DIM = 64
N_ATT_HEADS = ATT_WIDTH // HEAD_DIM
N_LRU_BLOCKS = 8
LRU_BLOCK = LRU_WIDTH // N_LRU_BLOCKS
CONV_WIDTH = 4
LRU_C = 8.0
D_FF = 4 * D_MODEL
DILATED_CONFIGS = ((128, 1), (512, 4), (2048, 16))
WIN_MAX = max(w for w, _ in DILATED_CONFIGS)
N_BUCKETS = 32
BUCKET_MAX_DIST = WIN_MAX
NORM_EPS = 1e-6
NEG_INF = -1e30
IN_COLS = 3 * ATT_WIDTH + 2 * LRU_WIDTH

kernel_name = "hymba_rglru_dilated_swa_step"


def rms_norm(x, g):
    x32 = x.astype(jnp.float32)
    y = x32 * lax.rsqrt(jnp.mean(x32 * x32, axis=-1, keepdims=True) + NORM_EPS)
    return (y * g.astype(jnp.float32)).astype(x.dtype)


def t5_bucket(dist):
    max_exact = N_BUCKETS // 2
    d_f = jnp.maximum(dist, max_exact).astype(jnp.float32)
    large = max_exact + (jnp.log(d_f / max_exact) / math.log(BUCKET_MAX_DIST / max_exact)
                         * (N_BUCKETS - max_exact)).astype(jnp.int32)
    large = jnp.minimum(large, N_BUCKETS - 1)
    return jnp.where(dist < max_exact, dist, large)


def branch_bias(rel_bias, dilation, span):
    dist = jnp.arange(span + 1, dtype=jnp.int32) * dilation
    return rel_bias[t5_bucket(dist)].astype(jnp.float32).T


def dilated_branch_prompt(q, k, v, bias, dilation, span):
    b, s, h, dh = q.shape
    L = s // dilation
    blk = span
    nb = -(-L // blk)
    lp = nb * blk

    def to_sub(t):
        t = t.reshape(b, L, dilation, h, dh).transpose(0, 2, 3, 1, 4)
        t = jnp.pad(t, ((0, 0), (0, 0), (0, 0), (0, lp - L), (0, 0)))
        return t.reshape(b, dilation, h, nb, blk, dh)

    def with_prev(t):
        prev = jnp.pad(t, ((0, 0), (0, 0), (0, 0), (1, 0), (0, 0), (0, 0)))[:, :, :, :-1]
        return jnp.concatenate([prev, t], axis=4)

    qb = to_sub(q)
    kk = with_prev(to_sub(k))
    vv = with_prev(to_sub(v))
    qi = jnp.arange(blk)[:, None]
    ki = jnp.arange(2 * blk)[None, :]
    rel = blk + qi - ki
    in_band = (rel >= 0) & (rel <= span)
    first_blk = (jnp.arange(nb)[:, None, None] == 0) & (ki[None] < blk)
    mask = in_band[None] & ~first_blk
    bias_m = bias[:, jnp.clip(rel, 0, span)]
    logits = jnp.einsum('bchnqd,bchnkd->bchnqk', qb, kk).astype(jnp.float32) * (dh ** -0.5)
    logits = logits + bias_m[None, None, :, None]
    logits = jnp.where(mask[None, None, None], logits, NEG_INF)
    m = jnp.max(logits, axis=-1, keepdims=True)
    p = jnp.exp(logits - m)
    den = jnp.sum(p, axis=-1, keepdims=True)
    o = jnp.einsum('bchnqk,bchnkd->bchnqd', p, vv.astype(jnp.float32)) / den
    lse = (m + jnp.log(den))[..., 0]
    o = o.reshape(b, dilation, h, lp, dh)[:, :, :, :L].transpose(0, 3, 1, 2, 4).reshape(b, s, h, dh)
    lse = lse.reshape(b, dilation, h, lp)[:, :, :, :L].transpose(0, 3, 1, 2).reshape(b, s, h)
    return o, lse


def dilated_branch_sample(q, k_all, v_all, bias, dilation, span, n_buf):
    t, dh = q.shape[1], q.shape[3]
    idx = n_buf + jnp.arange(t)[:, None] - dilation * jnp.arange(span + 1)[None, :]
    valid = idx >= 0
    idx = jnp.maximum(idx, 0)
    kg = k_all[:, idx]
    vg = v_all[:, idx]
    logits = jnp.einsum('bthd,btjhd->bthj', q, kg).astype(jnp.float32) * (dh ** -0.5)
    logits = logits + bias[None, None]
    logits = jnp.where(valid[None, :, None, :], logits, NEG_INF)
    m = jnp.max(logits, axis=-1, keepdims=True)
    p = jnp.exp(logits - m)
    den = jnp.sum(p, axis=-1, keepdims=True)
    o = jnp.einsum('bthj,btjhd->bthd', p, vg.astype(jnp.float32)) / den[..., 0][..., None]
    lse = (m + jnp.log(den))[..., 0]
    return o, lse


def combine_branches(outs, lses):
    w = jax.nn.softmax(jnp.stack(lses, axis=0), axis=0)
    return jnp.einsum('gbth,gbthd->bthd', w, jnp.stack(outs, axis=0))


def causal_conv(x, buf, w, b):
    t = x.shape[1]
    xp = jnp.concatenate([buf.astype(x.dtype), x], axis=1)
    y = b
    for j in range(CONV_WIDTH):
        y = y + w[j] * xp[:, j:j + t]
    return y, xp[:, xp.shape[1] - (CONV_WIDTH - 1):]


def rg_lru(xc, h0, wa, ba, wx, bx, lam):
    b, t, c = xc.shape
    x32 = xc.astype(jnp.float32)
    xb = x32.reshape(b, t, N_LRU_BLOCKS, LRU_BLOCK)
    r = jax.nn.sigmoid(jnp.einsum('btgi,gio->btgo', xb, wa) + ba).reshape(b, t, c)
    gi = jax.nn.sigmoid(jnp.einsum('btgi,gio->btgo', xb, wx) + bx).reshape(b, t, c)
    log_a = -LRU_C * r * jax.nn.softplus(-lam.astype(jnp.float32))
    a = jnp.exp(log_a)
    u = jnp.sqrt(-jnp.expm1(2.0 * log_a)) * (gi * x32)

    def step(h, au):
        a_t, u_t = au
        h = a_t * h + u_t
        return h, h

    h_last, hs = lax.scan(step, h0.astype(jnp.float32), (a.transpose(1, 0, 2), u.transpose(1, 0, 2)))
    return hs.transpose(1, 0, 2), h_last


def hybrid_layer(x, conv_buf, h0, k_buf, v_buf, norm1_g, w_in, rel_bias, conv_w, conv_b,
                 gate_a_w, gate_a_b, gate_x_w, gate_x_b, lru_lambda, att_out_g, rnn_out_g,
                 w_out, norm2_g, w_mlp_in, w_mlp_out):
    b, t, _ = x.shape
    n = rms_norm(x, norm1_g)
    proj = n @ w_in
    q, k, v, xr, gr = jnp.split(
        proj, [ATT_WIDTH, 2 * ATT_WIDTH, 3 * ATT_WIDTH, 3 * ATT_WIDTH + LRU_WIDTH], axis=-1)
    q = q.reshape(b, t, N_ATT_HEADS, HEAD_DIM)
    k = k.reshape(b, t, N_ATT_HEADS, HEAD_DIM)
    v = v.reshape(b, t, N_ATT_HEADS, HEAD_DIM)

    outs, lses = [], []
    if k_buf is None:
        for (win, dil) in DILATED_CONFIGS:
            span = win // dil
            o, l = dilated_branch_prompt(q, k, v, branch_bias(rel_bias, dil, span), dil, span)
            outs.append(o)
            lses.append(l)
        keep = min(WIN_MAX, t)
        new_k, new_v = k[:, t - keep:], v[:, t - keep:]
        conv_buf = jnp.zeros((b, CONV_WIDTH - 1, LRU_WIDTH), x.dtype)
        h0 = jnp.zeros((b, LRU_WIDTH), jnp.float32)
        state_dtype = x.dtype
    else:
        n_buf = k_buf.shape[1]
        k_all = jnp.concatenate([k_buf.astype(k.dtype), k], axis=1)
        v_all = jnp.concatenate([v_buf.astype(v.dtype), v], axis=1)
        for (win, dil) in DILATED_CONFIGS:
            span = win // dil
            o, l = dilated_branch_sample(q, k_all, v_all, branch_bias(rel_bias, dil, span), dil, span, n_buf)
            outs.append(o)
            lses.append(l)
        new_k, new_v = k, v
        state_dtype = h0.dtype
    att = combine_branches(outs, lses).astype(x.dtype).reshape(b, t, ATT_WIDTH)

    xc, new_conv = causal_conv(xr, conv_buf, conv_w, conv_b)
    hs, h_last = rg_lru(xc, h0, gate_a_w, gate_a_b, gate_x_w, gate_x_b, lru_lambda)
    rnn = (hs * jax.nn.gelu(gr.astype(jnp.float32))).astype(x.dtype)

    mixed = jnp.concatenate([rms_norm(att, att_out_g), rms_norm(rnn, rnn_out_g)], axis=-1)
    x = x + mixed @ w_out
    hmid = rms_norm(x, norm2_g) @ w_mlp_in
    x = x + jnp.square(jax.nn.relu(hmid)) @ w_mlp_out
    return x, new_k, new_v, new_conv, h_last.astype(state_dtype)


def setup_inputs(seed: int = 0) -> dict:
    key = jax.random.key(seed)
    ks = jax.random.split(key, 24)
    n_buf = min(WIN_MAX, PAST_LEN)

    def nrm(k, shape, scale):
        return jax.random.normal(k, shape, jnp.float32) * scale

    a0 = jax.random.uniform(ks[14], (DEPTH, LRU_WIDTH), jnp.float32, 0.9, 0.999)
    s0 = a0 ** (1.0 / LRU_C)
    lru_lambda = jnp.log(s0) - jnp.log1p(-s0)
    return {
        "x_prompt": nrm(ks[0], (BATCH, SEQ, D_MODEL), 1.0),
        "x_sample": nrm(ks[1], (DEC_BATCH, DEC_SEQ, D_MODEL), 1.0),
        "cache_k": nrm(ks[2], (DEPTH, DEC_BATCH, n_buf, N_ATT_HEADS, HEAD_DIM), 1.0),
        "cache_v": nrm(ks[3], (DEPTH, DEC_BATCH, n_buf, N_ATT_HEADS, HEAD_DIM), 1.0),
        "state_conv": nrm(ks[4], (DEPTH, DEC_BATCH, CONV_WIDTH - 1, LRU_WIDTH), 1.0),
        "state_h": nrm(ks[5], (DEPTH, DEC_BATCH, LRU_WIDTH), 0.5),
        "norm1_g": 1.0 + nrm(ks[6], (DEPTH, D_MODEL), 0.02),
        "w_in": nrm(ks[7], (DEPTH, D_MODEL, IN_COLS), D_MODEL ** -0.5),
        "rel_bias": nrm(ks[8], (N_BUCKETS, N_ATT_HEADS), 0.5),
        "conv_w": nrm(ks[9], (DEPTH, CONV_WIDTH, LRU_WIDTH), CONV_WIDTH ** -0.5),
        "conv_b": nrm(ks[10], (DEPTH, LRU_WIDTH), 0.02),
        "gate_a_w": nrm(ks[11], (DEPTH, N_LRU_BLOCKS, LRU_BLOCK, LRU_BLOCK), LRU_BLOCK ** -0.5),
        "gate_a_b": nrm(ks[12], (DEPTH, N_LRU_BLOCKS, LRU_BLOCK), 0.02),
        "gate_x_w": nrm(ks[13], (DEPTH, N_LRU_BLOCKS, LRU_BLOCK, LRU_BLOCK), LRU_BLOCK ** -0.5),
        "gate_x_b": nrm(ks[15], (DEPTH, N_LRU_BLOCKS, LRU_BLOCK), 0.02),
        "lru_lambda": lru_lambda,
        "att_out_g": 1.0 + nrm(ks[16], (DEPTH, ATT_WIDTH), 0.02),
        "rnn_out_g": 1.0 + nrm(ks[17], (DEPTH, LRU_WIDTH), 0.02),
        "w_out": nrm(ks[18], (DEPTH, MIX_WIDTH, D_MODEL), MIX_WIDTH ** -0.5),
        "norm2_g": 1.0 + nrm(ks[19], (DEPTH, D_MODEL), 0.02),
        "w_mlp_in": nrm(ks[20], (DEPTH, D_MODEL, D_FF), D_MODEL ** -0.5),
        "w_mlp_out": nrm(ks[21], (DEPTH, D_FF, D_MODEL), D_FF ** -0.5),
        "final_g": 1.0 + nrm(ks[22], (D_MODEL,), 0.02),
    }


def reference(x_prompt, x_sample, cache_k, cache_v, state_conv, state_h, norm1_g, w_in, rel_bias,
              conv_w, conv_b, gate_a_w, gate_a_b, gate_x_w, gate_x_b, lru_lambda, att_out_g,
              rnn_out_g, w_out, norm2_g, w_mlp_in, w_mlp_out, final_g):
    xp, xs = x_prompt, x_sample
    kp_l, vp_l, cp_l, hp_l = [], [], [], []
    ks_l, vs_l, cs_l, hs_l = [], [], [], []
    for l in range(DEPTH):
        w = (norm1_g[l], w_in[l], rel_bias, conv_w[l], conv_b[l], gate_a_w[l], gate_a_b[l],
             gate_x_w[l], gate_x_b[l], lru_lambda[l], att_out_g[l], rnn_out_g[l], w_out[l],
             norm2_g[l], w_mlp_in[l], w_mlp_out[l])
        xp, kp, vp, cp, hp = hybrid_layer(xp, None, None, None, None, *w)
        xs, kn, vn, cn, hn = hybrid_layer(xs, state_conv[l], state_h[l], cache_k[l], cache_v[l], *w)
        kp_l.append(kp)
        vp_l.append(vp)
        cp_l.append(cp)
        hp_l.append(hp)
        ks_l.append(kn)
        vs_l.append(vn)
        cs_l.append(cn)
        hs_l.append(hn)
    y_prompt = rms_norm(xp, final_g)
    y_sample = rms_norm(xs, final_g)
    return (y_prompt, y_sample,
            jnp.stack(kp_l), jnp.stack(vp_l), jnp.stack(cp_l), jnp.stack(hp_l),
            jnp.stack(ks_l), jnp.stack(vs_l), jnp.stack(cs_l), jnp.stack(hs_l))
```

```python
import math
from contextlib import ExitStack
import numpy as np
import ml_dtypes
import concourse.bass as bass
import concourse.mybir as mybir
from concourse.bass_utils import run_bass_kernel_spmd

F32 = mybir.dt.float32
BF16 = mybir.dt.bfloat16
AF = mybir.ActivationFunctionType
ALU = mybir.AluOpType

NCORES = 8
D = 1024
NIN = 2560
DFF = 4096
SEQ = 2048
NPS = 2
NSS = 16
TS = 8
NTP = NPS * SEQ
NTOK = NTP + NSS * TS
TBL = 2304
NKEEP = 1280
NKB = NKEEP // 128
EPS = 1e-6
NV = 56


class Buf:
    __slots__ = ("name", "writers", "readers", "dsem", "dcount", "multi")

    def __init__(self, name, multi=False):
        self.name = name
        self.writers = []
        self.readers = []
        self.dsem = None
        self.dcount = 0
        self.multi = multi


class Eng:
    def __init__(self, name):
        self.name = name
        self.thunks = []
        self.count = 0
        self.seen = {}


class FW:
    ENG_NAMES = ("pe", "act", "dve", "pool", "sp")

    def __init__(self, nc, stack):
        self.nc = nc
        self.stack = stack
        self.engs = {n: Eng(n) for n in self.ENG_NAMES}
        self.sems = {}
        for n in self.ENG_NAMES:
            self.sems[("eng", n)] = stack.enter_context(nc.semaphore("s_" + n))
        self.dma_bufs = []

    def _dsem(self, buf):
        if buf.dsem is None:
            key = ("dma", len(self.dma_bufs))
            self.sems[key] = self.stack.enter_context(self.nc.semaphore("d%d" % len(self.dma_bufs)))
            buf.dsem = key
            self.dma_bufs.append(buf)
        return buf.dsem

    def _deps(self, reads, writes, pe_accum=False):
        deps = {}

        def add(tok):
            k, v = tok
            if deps.get(k, 0) < v:
                deps[k] = v
        for b in reads:
            for t in b.writers:
                add(t)
        for b in writes:
            if b.multi:
                continue
            for t in b.writers:
                if pe_accum and t[0] == ("eng", "pe"):
                    continue
                add(t)
            for t in b.readers:
                add(t)
        return deps

    def _emit_waits(self, e, deps):
        E = self.engs[e]
        for k, v in deps.items():
            if E.seen.get(k, 0) >= v:
                continue
            E.seen[k] = v
            h = self.sems[k]
            E.thunks.append(lambda eng, h=h, v=v: eng.wait_ge(h, v))

    def _update(self, reads, writes, tok):
        for b in reads:
            b.readers.append(tok)
            if len(b.readers) > 64:
                mx = {}
                for k, v in b.readers:
                    if mx.get(k, 0) < v:
                        mx[k] = v
                b.readers = list(mx.items())
        for b in writes:
            if b.multi:
                b.writers.append(tok)
                if len(b.writers) > 64:
                    mx = {}
                    for k, v in b.writers:
                        if mx.get(k, 0) < v:
                            mx[k] = v
                    b.writers = list(mx.items())
            else:
                b.writers = [tok]
                b.readers = []

    def op(self, e, fn, reads=(), writes=(), pe_accum=False):
        E = self.engs[e]
        self._emit_waits(e, self._deps(reads, writes, pe_accum))
        sem = self.sems[("eng", e)]
        E.count += 1
        tok = (("eng", e), E.count)
        E.thunks.append(lambda eng, fn=fn, sem=sem: fn(eng).then_inc(sem, 1))
        self._update(reads, writes, tok)
        return tok

    def dma(self, q, fns, primary, reads=(), writes=()):
        if not isinstance(fns, (list, tuple)):
            fns = [fns]
        E = self.engs[q]
        self._emit_waits(q, self._deps(reads, writes))
        key = self._dsem(primary)
        sem = self.sems[key]
        for fn in fns:
            primary.dcount += 16
            E.thunks.append(lambda eng, fn=fn, sem=sem: fn(eng).then_inc(sem, 16))
        tok = (key, primary.dcount)
        self._update(reads, writes, tok)
        return tok

    def barrier(self):
        deps = {}
        for b in self.dma_bufs:
            if b.dcount:
                deps[b.dsem] = b.dcount
        for n in self.ENG_NAMES:
            if self.engs[n].count:
                deps[("eng", n)] = self.engs[n].count
        for n in self.ENG_NAMES:
            d = {k: v for k, v in deps.items() if k != ("eng", n)}
            self._emit_waits(n, d)

    def run(self):
        nc = self.nc
        with nc.Block() as block:
            @block.tensor
            def _(eng):
                for t in self.engs["pe"].thunks:
                    t(eng)

            @block.scalar
            def _(eng):
                for t in self.engs["act"].thunks:
                    t(eng)

            @block.vector
            def _(eng):
                for t in self.engs["dve"].thunks:
                    t(eng)

            @block.gpsimd
            def _(eng):
                for t in self.engs["pool"].thunks:
                    t(eng)

            @block.sync
            def _(eng):
                for t in self.engs["sp"].thunks:
                    t(eng)
        for n in self.ENG_NAMES:
            self.engs[n].thunks = []


class _Stop(Exception):
    pass


DBG = dict(stop=None, nseq=NPS, ntile=4, proj=True, kv=True, rnn=True, attn=True, sample=True, sattn=True, phase2=True)

def build_program():
    nc = bass.Bass("TRN2", target_bir_lowering=False)

    def din(name, shape, dt=F32):
        return nc.dram_tensor(name, shape, dt, kind="ExternalInput").ap()

    def dout(name, shape, dt=F32):
        return nc.dram_tensor(name, shape, dt, kind="ExternalOutput").ap()

    xp = din("xp", [NTP, D])
    xs = din("xs", [NSS * TS, D])
    ckT = din("ckT", [NSS, 4, 128, NKEEP])
    cv = din("cv", [NSS, NKEEP, 512])
    sel0_d = din("sel0", [128, 128], BF16)
    sel1_d = din("sel1", [128, 128], BF16)
    sconvT = din("sconvT", [128, 4, NSS, 3])
    shT = din("shT", [128, 4, NSS])
    w_in = din("w_in", [D, NIN])
    w_out = din("w_out", [D, D])
    w_mi = din("w_mi", [D, DFF])
    w_mo = din("w_mo", [DFF, D])
    vecs = din("vecs", [128, NV])
    fg = din("fg", [D])
    gaw = din("gaw", [8, 64, 64])
    gxw = din("gxw", [8, 64, 64])
    relb = din("relb", [32, 8])
    onehot = din("onehot", [32, TBL])
    mult = din("mult", [8, TBL])
    identb_d = din("identb", [128, 128], BF16)
    identf_d = din("identf", [128, 128])
    jmat_d = din("jmat", [128, 128], BF16)

    yp = dout("yp", [NTP, D])
    ys = dout("ys", [NSS * TS, D])
    nkp = dout("nkp", [NTP, 512])
    nvp = dout("nvp", [NTP, 512])
    ncp = dout("ncp", [NPS, 3, 512])
    nhp = dout("nhp", [NPS, 512])
    nks = dout("nks", [NSS * TS, 512])
    nvs = dout("nvs", [NSS * TS, 512])
    ncs = dout("ncs", [NSS, 3, 512])
    nhs = dout("nhs", [NSS, 512])

    mixscr = nc.dram_tensor("mixscr", [8, 128, NTOK], BF16, kind="Internal").ap()
    tblscr = nc.dram_tensor("tblscr", [8, TBL], BF16, kind="Internal").ap()
    Bmix = Buf("mixscr", multi=True)
    Btbl = Buf("tblscr", multi=True)

    try:
      with ExitStack() as top:
        fw = FW(nc, top)

        def stop_here(tag):
            if DBG.get('stop') == tag:
                fw.barrier()
                fw.run()
                raise _Stop()

        def sbt(st, name, shape, dt):
            return st.enter_context(nc.sbuf_tensor("sb_" + name, shape, dt))

        pbank = [top.enter_context(nc.psum_tensor("pb%d" % i, [128, 512], F32)) for i in range(7)]
        ptp = top.enter_context(nc.psum_tensor("ptp", [128, 1024], BF16))
        Bpb = [Buf("pb%d" % i) for i in range(7)]
        Bptp = Buf("ptp")

        identb = sbt(top, "identb", [128, 128], BF16); Bidb = Buf("identb")
        identf = sbt(top, "identf", [128, 128], F32); Bidf = Buf("identf")
        vec = sbt(top, "vec", [128, NV], F32); Bvec = Buf("vec")
        vec2 = sbt(top, "vec2", [128, 24], F32); Bvec2 = Buf("vec2")
        cst = sbt(top, "cst", [128, 4], F32); Bcst = Buf("cst")
        fw.op("pool", lambda e: e.memset(cst[:, 0:1], EPS), writes=[Bcst])
        fw.op("pool", lambda e: e.memset(cst[:, 1:2], math.log(0.5)), writes=[Bcst])
        fw.op("pool", lambda e: e.memset(cst[:, 2:3], 1.0), writes=[Bcst])
        fw.dma("sp", lambda e: e.dma_start(out=identb[:], in_=identb_d), Bidb, writes=[Bidb])
        fw.dma("sp", lambda e: e.dma_start(out=identf[:], in_=identf_d), Bidf, writes=[Bidf])
        fw.dma("sp", lambda e: e.dma_start(out=vec[:], in_=vecs), Bvec, writes=[Bvec])
        G1, G2, ATG, RNG, CW, CB, BA, BX, LAM = 0, 8, 16, 20, 24, 40, 44, 48, 52
        HBA, HBX, CC, CH = 0, 4, 8, 12
        fw.op("dve", lambda e: e.tensor_scalar(out=vec2[:, 0:8], in0=vec[:, BA:BA + 8], scalar1=0.5, scalar2=None, op0=ALU.mult),
              reads=[Bvec], writes=[Bvec2])
        fw.op("act", lambda e: e.activation(out=vec2[:, 16:20], in_=vec[:, LAM:LAM + 4], func=AF.Exp, scale=-1.0), reads=[Bvec], writes=[Bvec2])
        fw.op("act", lambda e: e.activation(out=vec2[:, 16:20], in_=vec2[:, 16:20], func=AF.Ln, scale=1.0, bias=cst[:, 2:3]), reads=[Bvec2, Bcst], writes=[Bvec2])
        fw.op("dve", lambda e: e.tensor_scalar(out=vec2[:, CC:CC + 4], in0=vec2[:, 16:20], scalar1=-8.0, scalar2=None, op0=ALU.mult),
              reads=[Bvec2], writes=[Bvec2])
        fw.op("dve", lambda e: e.tensor_scalar(out=vec2[:, CH:CH + 4], in0=vec2[:, 16:20], scalar1=-4.0, scalar2=None, op0=ALU.mult),
              reads=[Bvec2], writes=[Bvec2])

        def rms_rstd(eng_in_ap, Bin, junk, Bjunk, ss, Bss, n):
            fw.op("act", lambda e: e.activation(out=junk, in_=eng_in_ap, func=AF.Square, accum_out=ss[:, 0:1]),
                  reads=[Bin], writes=[Bjunk, Bss])
            npart = ss.shape[0]
            fw.op("act", lambda e: e.activation(out=ss[:, 1:2], in_=cst[0:npart, 2:3], func=AF.Copy), reads=[Bss, Bcst], writes=[Bss])
            fw.op("act", lambda e: e.activation(out=ss[:, 1:2], in_=ss[:, 0:1], func=AF.Ln, scale=1.0 / n, bias=cst[0:npart, 0:1]), reads=[Bss, Bcst], writes=[Bss])
            fw.op("act", lambda e: e.activation(out=ss[:, 0:1], in_=ss[:, 1:2], func=AF.Exp, scale=-0.5), reads=[Bss], writes=[Bss])

        with ExitStack() as p1:
            w_in_bf = sbt(p1, "w_in_bf", [128, 8, NIN], BF16); Bwin = [Buf("w_in_bf%d" % g) for g in range(5)]
            wa_bd = sbt(p1, "wa_bd", [128, 4, 128], BF16); Bwa = Buf("wa_bd")
            wx_bd = sbt(p1, "wx_bd", [128, 4, 128], BF16); Bwx = Buf("wx_bd")
            ones_f = sbt(p1, "ones_f", [128, 128], F32); Bones = Buf("ones_f")
            onesb = sbt(p1, "onesb", [128, 1], BF16); Bonesb = Buf("onesb")
            xblk = [sbt(p1, "xblk%d" % i, [128, D], F32) for i in range(2)]; Bxblk = [Buf("xblk%d" % i) for i in range(2)]
            ss = sbt(p1, "ss", [128, 2], F32); Bss = Buf("ss")
            xn = [sbt(p1, "xn%d" % i, [128, D], BF16) for i in range(2)]; Bxn = [Buf("xn%d" % i) for i in range(2)]
            nT = sbt(p1, "nT", [128, 8, 512], BF16); BnT = Buf("nT")
            mixT = sbt(p1, "mixT", [128, 8, 512], BF16); BmixT = Buf("mixT")
            kst = sbt(p1, "kst", [128, 512], F32); Bkst = Buf("kst")
            vst = sbt(p1, "vst", [128, 512], F32); Bvst = Buf("vst")
            xr = sbt(p1, "xr", [128, 4, 515], F32); Bxr = [Buf("xr%d" % c) for c in range(4)]
            gg2 = [sbt(p1, "gg%d" % i, [128, 512], F32) for i in range(2)]; Bgg2 = [Buf("gg%d" % i) for i in range(2)]
            xc2 = [sbt(p1, "xc%d" % i, [128, 512], F32) for i in range(2)]; Bxc2 = [Buf("xc%d" % i) for i in range(2)]
            xcb2 = [sbt(p1, "xcb%d" % i, [128, 512], BF16) for i in range(2)]; Bxcb2 = [Buf("xcb%d" % i) for i in range(2)]
            t_r2 = [sbt(p1, "t_r%d" % i, [128, 512], F32) for i in range(2)]; Btr2 = [Buf("t_r%d" % i) for i in range(2)]
            t_g2 = [sbt(p1, "t_g%d" % i, [128, 512], F32) for i in range(2)]; Btg2 = [Buf("t_g%d" % i) for i in range(2)]
            a_t2 = [sbt(p1, "a_t%d" % i, [128, 512], F32) for i in range(2)]; Bat2 = [Buf("a_t%d" % i) for i in range(2)]
            tmp2 = [sbt(p1, "tmp%d" % i, [128, 512], F32) for i in range(2)]; Btmp2 = [Buf("tmp%d" % i) for i in range(2)]
            hs = sbt(p1, "hs", [128, 512], F32); Bhs = Buf("hs")
            rnn = sbt(p1, "rnn", [128, 4, 512], F32); Brnn = [Buf("rnn%d" % c) for c in range(4)]
            sqb = sbt(p1, "sqb", [128, 512], F32); Bsqb = Buf("sqb")
            rstdb = sbt(p1, "rstdb", [128, 512], F32); Brstdb = Buf("rstdb")
            hstate = sbt(p1, "hstate", [128, 4], F32); Bhst = Buf("hstate")
            fin = sbt(p1, "fin", [128, 4, 64], F32); Bfin = Buf("fin")
            att = sbt(p1, "att", [128, 4, 512], BF16); Batt = [Buf("att%d" % i) for i in range(4)]
            attn = sbt(p1, "attn", [128, 512], BF16); Battn = Buf("attn")
            rec = sbt(p1, "rec", [128, 4], F32); Brec = Buf("rec")

            for g in range(5):
                fw.dma("pool", lambda e, g=g: e.dma_start(
                    out=w_in_bf[:, :, g * 512:(g + 1) * 512],
                    in_=w_in.rearrange("(kc p) n -> p kc n", p=128)[:, :, g * 512:(g + 1) * 512]), Bwin[g], writes=[Bwin[g]])
            fw.op("pool", lambda e: e.memset(wa_bd[:], 0.0), writes=[Bwa])
            fw.op("pool", lambda e: e.memset(wx_bd[:], 0.0), writes=[Bwx])
            fw.op("pool", lambda e: e.memset(ones_f[:], 1.0), writes=[Bones])
            fw.op("pool", lambda e: e.memset(onesb[:], 1.0), writes=[Bonesb])
            for (src, dst, Bd) in ((gaw, wa_bd, Bwa), (gxw, wx_bd, Bwx)):
                for hp in range(2):
                    fw.dma("pool", lambda e, src=src, dst=dst, hp=hp: e.dma_start(
                        out=dst[hp * 64:(hp + 1) * 64, :, hp * 64:(hp + 1) * 64],
                        in_=src.rearrange("(c two) i o -> two i c o", two=2)[hp]), Bd, writes=[Bd])

            def norm_transpose(xsrc_ap, slot, gcol, dstT, BdstT, col0, ntok=128):
                xb, Bx_ = xblk[slot], Bxblk[slot]
                if xsrc_ap is not None:
                    fw.dma("sp", lambda e: e.dma_start(out=xb[0:ntok, :], in_=xsrc_ap), Bx_, writes=[Bx_])
                rms_rstd(xb[0:ntok, :], Bx_, xn[slot][0:ntok, :], Bxn[slot], ss[0:ntok, :], Bss, D)
                fw.op("act", lambda e: e.activation(out=xn[slot][0:ntok, :], in_=xb[0:ntok, :], func=AF.Copy, scale=ss[0:ntok, 0:1]),
                      reads=[Bx_, Bss], writes=[Bxn[slot]])
                for kc in range(8):
                    fw.op("pe", lambda e, kc=kc: e.transpose(out=ptp[:, kc * 128:kc * 128 + ntok], in_=xn[slot][0:ntok, kc * 128:(kc + 1) * 128],
                                                             identity=identb[0:ntok, 0:ntok]),
                          reads=[Bxn[slot], Bidb], writes=[Bptp], pe_accum=True)
                fw.op("dve", lambda e: e.tensor_tensor(
                    out=dstT[:, :, col0:col0 + ntok], in0=ptp[:].rearrange("p (k j) -> p k j", k=8)[:, :, 0:ntok],
                    in1=vec[:, gcol:gcol + 8].unsqueeze(2).to_broadcast([128, 8, ntok]), op=ALU.mult),
                    reads=[Bptp, Bvec], writes=[BdstT])

            mmi = [0]

            def mm_bank():
                i = mmi[0] % 2
                mmi[0] += 1
                return pbank[i], Bpb[i]

            def proj_fm(wcol0, N, evac):
                ps, Bps = mm_bank()
                for kc in range(8):
                    fw.op("pe", lambda e, kc=kc, ps=ps: e.matmul(ps[:, 0:N], lhsT=w_in_bf[:, kc, wcol0:wcol0 + 128], rhs=nT[:, kc, 0:N],
                                                                 start=(kc == 0), stop=(kc == 7)),
                          reads=[Bwin[wcol0 // 512], BnT], writes=[Bps], pe_accum=True)
                evac(ps, Bps)

            def rnn_chunk(c, N, seg, xr_view, conv_views, scan_fn, last_tile_fin):
                xc, Bxc = xc2[c % 2], Bxc2[c % 2]
                t_g, Btg = t_g2[c % 2], Btg2[c % 2]
                a_t, Bat = a_t2[c % 2], Bat2[c % 2]
                tmp, Btmp = tmp2[c % 2], Btmp2[c % 2]
                gg, Bgg = gg2[c % 2], Bgg2[c % 2]
                xcb, Bxcb = xcb2[c % 2], Bxcb2[c % 2]
                t_r, Btr = t_r2[c % 2], Btr2[c % 2]

                def ev_xr(ps, Bps):
                    fw.op("act", lambda e: e.activation(out=xr_view, in_=ps[:, 0:N] if seg is None else ps[:, 0:N].rearrange("p (s t) -> p s t", t=TS),
                                                        func=AF.Copy), reads=[Bps], writes=[Bxr[c]])
                proj_fm(1536 + c * 128, N, ev_xr)
                yield
                xcv = xc[:, 0:N] if seg is None else xc[:, 0:N].rearrange("p (s t) -> p s t", t=TS)
                cw = CW + c * 4
                fw.op("dve", lambda e: e.tensor_scalar(out=xcv, in0=conv_views[3], scalar1=vec[:, cw + 3:cw + 4], scalar2=vec[:, CB + c:CB + c + 1],
                                                       op0=ALU.mult, op1=ALU.add), reads=[Bxr[c], Bvec], writes=[Bxc])
                for j in (2, 1, 0):
                    fw.op("dve", lambda e, j=j: e.scalar_tensor_tensor(out=xcv, in0=conv_views[j], scalar=vec[:, cw + j:cw + j + 1], in1=xcv,
                                                                      op0=ALU.mult, op1=ALU.add), reads=[Bxr[c], Bvec, Bxc], writes=[Bxc])
                yield
                fw.op("act", lambda e: e.activation(out=xcb[:, 0:N], in_=xc[:, 0:N], func=AF.Copy), reads=[Bxc], writes=[Bxcb])
                psr, Bpsr = mm_bank()
                fw.op("pe", lambda e: e.matmul(psr[:, 0:N], lhsT=wa_bd[:, c, :], rhs=xcb[:, 0:N], start=True, stop=True), reads=[Bwa, Bxcb], writes=[Bpsr])
                psg, Bpsg = mm_bank()
                fw.op("pe", lambda e: e.matmul(psg[:, 0:N], lhsT=wx_bd[:, c, :], rhs=xcb[:, 0:N], start=True, stop=True), reads=[Bwx, Bxcb], writes=[Bpsg])
                psq, Bpsq = pbank[6], Bpb[6]
                for kc in range(8):
                    fw.op("pe", lambda e, kc=kc: e.matmul(psq[:, 0:N], lhsT=w_in_bf[:, kc, 2048 + c * 128:2048 + (c + 1) * 128], rhs=nT[:, kc, 0:N],
                                                          start=(kc == 0), stop=(kc == 7)), reads=[Bwin[4], BnT], writes=[Bpsq], pe_accum=True)
                fw.op("act", lambda e: e.activation(out=t_r[:, 0:N], in_=psr[:, 0:N], func=AF.Tanh, scale=0.5, bias=vec2[:, HBA + c:HBA + c + 1]),
                      reads=[Bpsr, Bvec2], writes=[Btr])
                fw.op("act", lambda e: e.activation(out=t_g[:, 0:N], in_=psg[:, 0:N], func=AF.Tanh, scale=0.5, bias=vec2[:, HBX + c:HBX + c + 1]),
                      reads=[Bpsg, Bvec2], writes=[Btg])
                fw.op("act", lambda e: e.activation(out=gg[:, 0:N], in_=psq[:, 0:N], func=AF.Gelu_apprx_tanh), reads=[Bpsq], writes=[Bgg])
                yield
                fw.op("act", lambda e: e.activation(out=a_t[:, 0:N], in_=t_r[:, 0:N], func=AF.Exp, scale=vec2[:, CH + c:CH + c + 1], bias=vec2[:, CH + c:CH + c + 1]),
                      reads=[Btr, Bvec2], writes=[Bat])
                fw.op("act", lambda e: e.activation(out=tmp[:, 0:N], in_=t_r[:, 0:N], func=AF.Exp, scale=vec2[:, CC + c:CC + c + 1], bias=vec2[:, CC + c:CC + c + 1]),
                      reads=[Btr, Bvec2], writes=[Btmp])
                fw.op("act", lambda e: e.activation(out=tmp[:, 0:N], in_=tmp[:, 0:N], func=AF.Ln, scale=-1.0, bias=cst[:, 2:3]), reads=[Btmp, Bcst], writes=[Btmp])
                fw.op("act", lambda e: e.activation(out=tmp[:, 0:N], in_=tmp[:, 0:N], func=AF.Exp, scale=0.5, bias=cst[:, 1:2]), reads=[Btmp, Bcst], writes=[Btmp])
                yield
                fw.op("dve", lambda e: e.scalar_tensor_tensor(out=t_g[:, 0:N], in0=t_g[:, 0:N], scalar=1.0, in1=xc[:, 0:N], op0=ALU.add, op1=ALU.mult),
                      reads=[Btg, Bxc], writes=[Btg])
                fw.op("dve", lambda e: e.tensor_tensor(out=t_g[:, 0:N], in0=t_g[:, 0:N], in1=tmp[:, 0:N], op=ALU.mult), reads=[Btg, Btmp], writes=[Btg])
                yield
                scan_fn(c)
                fw.op("dve", lambda e: e.tensor_tensor(out=rnn[:, c, 0:N], in0=hs[:, 0:N], in1=gg[:, 0:N], op=ALU.mult), reads=[Bhs, Bgg], writes=[Brnn[c]])
                last_tile_fin(c)
                yield

            def rnn_norm(N, col0):
                grp_norm(rnn, Brnn, RNG, 4, N, col0)

            def grp_norm(src, Bsrc, gcol, dst_c0, N, col0):
                rnn, Brnn, RNG = src, Bsrc, gcol
                ps, Bps = pbank[6], Bpb[6]
                for c in range(4):
                    fw.op("act", lambda e, c=c: e.activation(out=sqb[:, 0:N], in_=rnn[:, c, 0:N], func=AF.Square), reads=[Brnn[c]], writes=[Bsqb])
                    fw.op("pe", lambda e, c=c: e.matmul(ps[:, 0:N], lhsT=ones_f[:], rhs=sqb[:, 0:N], start=(c == 0), stop=(c == 3)),
                          reads=[Bones, Bsqb], writes=[Bps], pe_accum=True)
                fw.op("act", lambda e: e.activation(out=rstdb[:, 0:N], in_=ps[:, 0:N], func=AF.Ln, scale=1.0 / 512, bias=cst[:, 0:1]), reads=[Bps, Bcst], writes=[Brstdb])
                fw.op("act", lambda e: e.activation(out=rstdb[:, 0:N], in_=rstdb[:, 0:N], func=AF.Exp, scale=-0.5), reads=[Brstdb], writes=[Brstdb])
                for c in range(4):
                    fw.op("dve", lambda e, c=c: e.scalar_tensor_tensor(out=mixT[:, dst_c0 + c, col0:col0 + N], in0=rnn[:, c, 0:N], scalar=vec[:, RNG + c:RNG + c + 1],
                                                                      in1=rstdb[:, 0:N], op0=ALU.mult, op1=ALU.mult),
                          reads=[Brnn[c], Bvec, Brstdb], writes=[BmixT])

            def att_norm_block(att_ap, Batt_, ntok, col0):
                rms_rstd(att_ap, Batt_, attn[0:ntok, :], Battn, ss[0:ntok, :], Bss, 512)
                fw.op("act", lambda e: e.activation(out=attn[0:ntok, :], in_=att_ap, func=AF.Copy, scale=ss[0:ntok, 0:1]), reads=[Batt_, Bss], writes=[Battn])
                for cc in range(4):
                    fw.op("pe", lambda e, cc=cc: e.transpose(out=ptp[:, cc * 128:cc * 128 + ntok], in_=attn[0:ntok, cc * 128:(cc + 1) * 128],
                                                             identity=identb[0:ntok, 0:ntok]), reads=[Battn, Bidb], writes=[Bptp], pe_accum=True)
                fw.op("dve", lambda e: e.tensor_tensor(
                    out=mixT[:, 0:4, col0:col0 + ntok], in0=ptp[:, 0:512].rearrange("p (k j) -> p k j", k=4)[:, :, 0:ntok],
                    in1=vec[:, ATG:ATG + 4].unsqueeze(2).to_broadcast([128, 4, ntok]), op=ALU.mult), reads=[Bptp, Bvec], writes=[BmixT])

            def fin_out(ncols, rows_conv, rows_h, conv_dst_fn, h_dst):
                ps, Bps = pbank[6], Bpb[6]
                for c in range(4):
                    fw.op("pe", lambda e, c=c: e.transpose(out=ps[0:ncols, c * 128:(c + 1) * 128], in_=fin[:, c, 0:ncols], identity=identf[:]),
                          reads=[Bfin, Bidf], writes=[Bps], pe_accum=True)
                fw.op("act", lambda e: e.activation(out=kst[0:ncols, :], in_=ps[0:ncols, :], func=AF.Copy), reads=[Bps], writes=[Bkst])
                fns = []
                for (r0, n, dst) in rows_conv:
                    fns.append(lambda e, r0=r0, n=n, dst=dst: e.dma_start(out=dst, in_=kst[r0:r0 + n, :]))
                fns.append(lambda e: e.dma_start(out=h_dst, in_=kst[rows_h[0]:rows_h[0] + rows_h[1], :]))
                fw.dma("sp", fns, Bkst, reads=[Bkst])

            with ExitStack() as pp:
                QT = sbt(pp, "QT", [128, 2, 4, 512], BF16); BQT = Buf("QT")
                fw.op("pool", lambda e: e.memset(QT[:], 0.0), writes=[BQT])
                KT = sbt(pp, "KT", [128, 4, SEQ], BF16); BKT = Buf("KT")
                Vaug = sbt(pp, "Vaug", [128, 16, 8, 66], BF16); BV = Buf("Vaug")
                Mbig = sbt(pp, "Mbig", [128, 8, SEQ], BF16); BM = Buf("Mbig")
                Eb = [sbt(pp, "Eb%d" % i, [128, 512], BF16) for i in range(2)]; BEb = [Buf("Eb%d" % i) for i in range(2)]
                PT = [sbt(pp, "PT%d" % i, [128, 512], BF16) for i in range(3)]; BPT = [Buf("PT%d" % i) for i in range(3)]
                jmat = sbt(pp, "jmat", [128, 128], BF16); Bjm = Buf("jmat")

                def gen_mask():
                    relb_sb, Brelb = tmp2[0][0:32, 0:8], Btmp2[0]
                    oh_sb, Boh = t_r2[0][0:32, :], Btr2[0]
                    mult_sb, Bmult = t_g2[0][0:8, :], Btg2[0]
                    tabf, Btabf = a_t2[0][0:8, :], Bat2[0]
                    tabb, Btabb = mixT[0:8].rearrange("p k t -> p (k t)")[:, 0:TBL], BmixT
                    fw.dma("sp", lambda e: e.dma_start(out=relb_sb, in_=relb), Brelb, writes=[Brelb])
                    fw.dma("sp", lambda e: e.dma_start(out=jmat[:], in_=jmat_d), Bjm, writes=[Bjm])
                    fw.op("pool", lambda e: e.memset(Vaug[:, :, :, 64:65], 1.0), writes=[BV])
                    for i0 in range(0, TBL, 512):
                        n = min(512, TBL - i0)
                        fw.dma("sp", lambda e, i0=i0, n=n: e.dma_start(out=oh_sb[:, 0:n], in_=onehot[:, i0:i0 + n]), Boh, writes=[Boh])
                        fw.dma("sp", lambda e, i0=i0, n=n: e.dma_start(out=mult_sb[:, 0:n], in_=mult[:, i0:i0 + n]), Bmult, writes=[Bmult])
                        ps, Bps = mm_bank()
                        fw.op("pe", lambda e, ps=ps, n=n: e.matmul(ps[0:8, 0:n], lhsT=relb_sb, rhs=oh_sb[:, 0:n], start=True, stop=True),
                              reads=[Brelb, Boh], writes=[Bps])
                        fw.op("act", lambda e, ps=ps, n=n: e.activation(out=tabf[:, 0:n], in_=ps[0:8, 0:n], func=AF.Exp), reads=[Bps], writes=[Btabf])
                        fw.op("dve", lambda e, i0=i0, n=n: e.tensor_tensor(out=tabb[:, i0:i0 + n], in0=tabf[:, 0:n], in1=mult_sb[:, 0:n], op=ALU.mult),
                              reads=[Btabf, Bmult], writes=[Btabb])
                        yield
                    fw.dma("sp", lambda e: e.dma_start(out=tblscr, in_=tabb), Btabb, reads=[Btabb], writes=[Btbl])
                    k = 0
                    for h in range(8):
                        for hf in range(4):
                            mrev, Bmrev = PT[k % 2], BPT[k % 2]
                            k += 1
                            fw.dma("sp", lambda e, h=h, hf=hf, mrev=mrev: e.dma_start(out=mrev[:], in_=bass.AP(tblscr.tensor, h * TBL + 1 + hf * 512, [[1, 128], [1, 512]])),
                                   Bmrev, reads=[Btbl], writes=[Bmrev])
                            ps, Bps = mm_bank()
                            fw.op("pe", lambda e, ps=ps, mrev=mrev: e.matmul(ps[:], lhsT=jmat[:], rhs=mrev[:], start=True, stop=True),
                                  reads=[Bjm, Bmrev], writes=[Bps])
                            fw.op("act", lambda e, ps=ps, h=h, hf=hf: e.activation(out=Mbig[:, h, hf * 512:(hf + 1) * 512], in_=ps[:], func=AF.Copy),
                                  reads=[Bps], writes=[BM])
                            yield
                mask_gen = gen_mask()

                pending_tail = []
                for b in range(DBG['nseq']):
                    fw.op("pool", lambda e: e.memset(xr[:, :, 0:3], 0.0), writes=Bxr)
                    fw.op("pool", lambda e: e.memset(hstate[:], 0.0), writes=[Bhst])
                    for T in range(DBG['ntile']):
                        g0 = b * SEQ + T * 512
                        def gen_norm(g0):
                            for blk in range(4):
                                pre = (blk < 2) and not (g0 == 0)
                                norm_transpose(None if pre else xp[g0 + blk * 128:g0 + (blk + 1) * 128, :], blk % 2, G1, nT, BnT, blk * 128)
                                yield
                            gn = g0 + 512
                            if gn < NTP:
                                for blk in range(2):
                                    fw.dma("sp", lambda e, gn=gn, blk=blk: e.dma_start(out=xblk[blk][:], in_=xp[gn + blk * 128:gn + (blk + 1) * 128, :]),
                                           Bxblk[blk], writes=[Bxblk[blk]])

                        def gen_head(b=b, T=T, g0=g0):
                            if g0 == 0:
                                yield from gen_norm(g0)
                            for c in range(4):
                                def ev_q(ps, Bps, c=c):
                                    for hp in range(2):
                                        fw.op("act", lambda e, hp=hp: e.activation(out=QT[hp * 64:(hp + 1) * 64, hp, c, :], in_=ps[hp * 64:(hp + 1) * 64, :],
                                                                                   func=AF.Copy, scale=0.125), reads=[Bps], writes=[BQT])
                                proj_fm(c * 128, 512, ev_q)
                                yield
                            for c in range(4):
                                def ev_k(ps, Bps, c=c, T=T):
                                    fw.op("dve", lambda e: e.tensor_copy(out=KT[:, c, T * 512:(T + 1) * 512], in_=ps[:]), reads=[Bps], writes=[BKT])
                                proj_fm(512 + c * 128, 512, ev_k)
                                yield

                        gh = gen_head()
                        if b == 0 and T == 0:
                            alive_h, alive_m = True, True
                            while alive_h or alive_m:
                                if alive_h:
                                    try:
                                        next(gh)
                                    except StopIteration:
                                        alive_h = False
                                if alive_m:
                                    try:
                                        next(mask_gen)
                                    except StopIteration:
                                        alive_m = False
                                if not alive_h:
                                    break
                        else:
                            for _ in gh:
                                pass
                        for fn_ in pending_tail:
                            fn_()
                        del pending_tail[:]
                        stop_here('qk')
                        def gen_kv(T=T, g0=g0):
                            for blk in range(4):
                                r0 = g0 + blk * 128
                                ps, Bps = mm_bank()
                                for kc in range(8):
                                    fw.op("pe", lambda e, kc=kc, ps=ps, blk=blk: e.matmul(ps[:], lhsT=nT[:, kc, blk * 128:(blk + 1) * 128], rhs=w_in_bf[:, kc, 512:1024],
                                                                                           start=(kc == 0), stop=(kc == 7)), reads=[Bwin[1], BnT], writes=[Bps], pe_accum=True)
                                fw.op("dve", lambda e, ps=ps: e.tensor_copy(out=kst[:], in_=ps[:]), reads=[Bps], writes=[Bkst])
                                fw.dma("sp", lambda e, r0=r0: e.dma_start(out=nkp[r0:r0 + 128, :], in_=kst[:]), Bkst, reads=[Bkst])
                                yield
                                ps, Bps = mm_bank()
                                for kc in range(8):
                                    fw.op("pe", lambda e, kc=kc, ps=ps, blk=blk: e.matmul(ps[:], lhsT=nT[:, kc, blk * 128:(blk + 1) * 128], rhs=w_in_bf[:, kc, 1024:1536],
                                                                                           start=(kc == 0), stop=(kc == 7)), reads=[Bwin[2], BnT], writes=[Bps], pe_accum=True)
                                fw.op("act", lambda e, ps=ps: e.activation(out=vst[:], in_=ps[:], func=AF.Copy), reads=[Bps], writes=[Bvst])
                                if not DBG.get('novaug'):
                                    fw.op("dve", lambda e, blk=blk, T=T: e.tensor_copy(out=Vaug[:, T * 4 + blk, :, 0:64], in_=vst[:].rearrange("p (h d) -> p h d", h=8)),
                                          reads=[Bvst], writes=[BV])
                                fw.dma("sp", lambda e, r0=r0: e.dma_start(out=nvp[r0:r0 + 128, :], in_=vst[:]), Bvst, reads=[Bvst])
                                yield
                        stop_here('kv')
                        last = (T == 3)

                        def scan_p(c):
                            fw.op("dve", lambda e: e.tensor_tensor_scan(out=hs[:], data0=a_t2[c % 2][:], data1=t_g2[c % 2][:], initial=hstate[:, c:c + 1], op0=ALU.mult, op1=ALU.add),
                                  reads=[Bat2[c % 2], Btg2[c % 2], Bhst], writes=[Bhs])
                            fw.op("dve", lambda e: e.tensor_copy(out=hstate[:, c:c + 1], in_=hs[:, 511:512]), reads=[Bhs], writes=[Bhst])

                        def fin_p(c):
                            if last:
                                fw.op("dve", lambda e: e.tensor_copy(out=fin[:, c, 0:3], in_=xr[:, c, 512:515]), reads=[Bxr[c]], writes=[Bfin])
                                fw.op("dve", lambda e: e.tensor_copy(out=fin[:, c, 3:4], in_=hs[:, 511:512]), reads=[Bhs], writes=[Bfin])
                            else:
                                fw.op("dve", lambda e: e.tensor_copy(out=xr[:, c, 0:3], in_=xr[:, c, 512:515]), reads=[Bxr[c]], writes=[Bxr[c]])
                        def gen_rnn(b=b, last=last):
                            for c0 in (0, 2):
                                gens = [rnn_chunk(c, 512, None, xr[:, c, 3:515], [xr[:, c, j:j + 512] for j in range(4)], scan_p, fin_p)
                                        for c in (c0, c0 + 1)]
                                alive = [True, True]
                                while alive[0] or alive[1]:
                                    for gi in range(2):
                                        if alive[gi]:
                                            try:
                                                next(gens[gi])
                                            except StopIteration:
                                                alive[gi] = False
                                    yield
                            rnn_norm(512, 0)
                            yield
                            if last:
                                fin_out(4, [(0, 3, ncp[b])], (3, 1), None, nhp[b:b + 1, :])
                                yield

                        def gen_attn(T=T):
                            its = [(h, kb) for h in range(8) for kb in range(4 * T + 4)]

                            def bufs(i):
                                return (pbank[2 + i % 2], Bpb[2 + i % 2], Eb[i % 2], BEb[i % 2], PT[i % 3], BPT[i % 3])

                            def emit_S(i):
                                h, kb = its[i]
                                c, hp = h // 2, h % 2
                                c0 = max(0, 128 * kb - T * 512)
                                s_ps, Bs_ps = bufs(i)[0:2]
                                fw.op("pe", lambda e, s_ps=s_ps, c=c, hp=hp, kb=kb, c0=c0: e.matmul(
                                    s_ps[:, c0:512], lhsT=KT[:, c, kb * 128:(kb + 1) * 128], rhs=QT[:, hp, c, c0:512],
                                    start=True, stop=True), reads=[BKT, BQT], writes=[Bs_ps])

                            def emit_mid(i):
                                h, kb = its[i]
                                c0 = max(0, 128 * kb - T * 512)
                                s_ps, Bs_ps, E_, BE_, P_, BP_ = bufs(i)
                                fw.op("act", lambda e, s_ps=s_ps, E_=E_, c0=c0: e.activation(out=E_[:, c0:512], in_=s_ps[:, c0:512], func=AF.Exp),
                                      reads=[Bs_ps], writes=[BE_])
                                j0 = T * 512 + c0 - 128 * kb
                                fw.op("dve", lambda e, E_=E_, P_=P_, c0=c0, j0=j0, h=h: e.tensor_tensor(
                                    out=P_[:, c0:512], in0=E_[:, c0:512], in1=Mbig[:, h, j0:j0 + 512 - c0], op=ALU.mult), reads=[BE_, BM], writes=[BP_])

                            def emit_pv(i):
                                h, kb = its[i]
                                c0 = max(0, 128 * kb - T * 512)
                                s_ps, Bs_ps, E_, BE_, P_, BP_ = bufs(i)
                                acc, Bacc = pbank[4 + h % 2], Bpb[4 + h % 2]
                                first = (kb == 0)
                                for ii in range(c0 // 128, 4):
                                    fw.op("pe", lambda e, acc=acc, P_=P_, ii=ii, kb=kb, h=h, first=first: e.matmul(
                                        acc[:, ii * 65:(ii + 1) * 65], lhsT=P_[:, ii * 128:(ii + 1) * 128], rhs=Vaug[:, kb, h, 0:65],
                                        start=first, stop=False, skip_group_check=True), reads=[BP_, BV], writes=[Bacc], pe_accum=True)
                                    first = False
                                if kb == 4 * T + 3:
                                    accv = acc[:, 0:260].rearrange("p (i d) -> p i d", d=65)
                                    fw.op("dve", lambda e, accv=accv: e.reciprocal(out=rec[:].unsqueeze(2), in_=accv[:, :, 64:65]), reads=[Bacc], writes=[Brec])
                                    fw.op("dve", lambda e, accv=accv, h=h: e.tensor_tensor(
                                        out=att[:, :, h * 64:(h + 1) * 64], in0=accv[:, :, 0:64], in1=rec[:].unsqueeze(2).to_broadcast([128, 4, 64]), op=ALU.mult),
                                        reads=[Bacc, Brec], writes=Batt)

                            n_it = len(its)
                            emit_S(0)
                            for i in range(n_it + 1):
                                if i + 1 < n_it:
                                    emit_S(i + 1)
                                if i < n_it:
                                    emit_mid(i)
                                if i >= 1:
                                    emit_pv(i - 1)
                                yield

                        if DBG.get('interleave', 1):
                            def gen_side(g0=g0):
                                yield from gen_rnn()
                                if g0 + 512 < NTP:
                                    yield from gen_norm(g0 + 512)
                            gr = gen_side()
                            ga = gen_attn()
                            gk = gen_kv()

                            def step(g):
                                try:
                                    next(g)
                                    return True
                                except StopIteration:
                                    return False
                            if T == 0:
                                while step(gk):
                                    step(gr)
                                    if g0 == 0:
                                        step(mask_gen)
                                        step(mask_gen)
                                        step(mask_gen)
                                if g0 == 0:
                                    while step(mask_gen):
                                        pass
                            else:
                                per = -(-8 // (4 * T))
                                for _ in range(4 * T):
                                    step(ga)
                                    for _ in range(per):
                                        step(gk)
                                    step(gr)
                                while step(gk):
                                    pass
                            n_att = 8 * (4 * T + 4)
                            kstep = 1
                            astep = max(1, n_att // 44)
                            alive_a, alive_r = True, True
                            while alive_a or alive_r:
                                for _ in range(astep):
                                    if alive_a:
                                        try:
                                            next(ga)
                                        except StopIteration:
                                            alive_a = False
                                for _ in range(kstep if alive_a else 4):
                                    if alive_r:
                                        try:
                                            next(gr)
                                        except StopIteration:
                                            alive_r = False
                        else:
                            for _ in gen_kv():
                                pass
                            for _ in gen_rnn():
                                pass
                            for _ in gen_attn():
                                pass
                            if g0 + 512 < NTP:
                                for _ in gen_norm(g0 + 512):
                                    pass
                        stop_here('attn')

                        def tile_tail(g0=g0):
                            for i in range(4):
                                att_norm_block(att[:, i, :], Batt[i], 128, i * 128)
                            fw.dma("sp", lambda e, g0=g0: e.dma_start(out=mixscr[:, :, g0:g0 + 512].rearrange("k p t -> p k t"), in_=mixT[:]),
                                   BmixT, reads=[BmixT], writes=[Bmix])
                        pending_tail.append(tile_tail)
                for fn_ in pending_tail:
                    fn_()
                del pending_tail[:]
                fw.barrier()
                fw.run()

            stop_here('prompt')
            with ExitStack() as sp_:
                Qbd = sbt(sp_, "Qbd", [128, 4, NSS, 16], BF16); BQbd = Buf("Qbd")
                onesbb = sbt(sp_, "onesbb", [128, 128], BF16); Bonesbb = Buf("onesbb")
                attT = sbt(sp_, "attT", [128, 4, 128], F32); BattT = [Buf("attT%d" % c) for c in range(4)]
                rd = sbt(sp_, "rd", [128, 64], F32); Brd = Buf("rd")
                fw.op("pool", lambda e: e.memset(Qbd[:], 0.0), writes=[BQbd])
                fw.op("pool", lambda e: e.memset(onesbb[:], 1.0), writes=[Bonesbb])
                KTn = sbt(sp_, "KTn", [128, 4, 128], BF16); BKTn = Buf("KTn")
                KTs = [sbt(sp_, "KTs%d" % i, [128, 4, NKEEP], BF16) for i in range(2)]; BKTs = [Buf("KTs%d" % i) for i in range(2)]
                Vs = [sbt(sp_, "Vs%d" % i, [128, NKB, 512], BF16) for i in range(2)]; BVs = [Buf("Vs%d" % i) for i in range(2)]
                Vn = sbt(sp_, "Vn", [8, 512], BF16); BVn = Buf("Vn")
                Ms = sbt(sp_, "Ms", [128, 17, 64], BF16); BMs = Buf("Ms")
                Mc = sbt(sp_, "Mc", [128, NKB + 1, 64], BF16); BMc = Buf("Mc")
                sel = [sbt(sp_, "sel%d" % i, [128, 128], BF16) for i in range(2)]; Bsel = [Buf("sel%d" % i) for i in range(2)]
                fw.dma("sp", lambda e: e.dma_start(out=sel[0][:], in_=sel0_d), Bsel[0], writes=[Bsel[0]])
                fw.dma("sp", lambda e: e.dma_start(out=sel[1][:], in_=sel1_d), Bsel[1], writes=[Bsel[1]])
                mrevs = sbt(sp_, "mrevs", [128, 17, 64], BF16); Bmrevs = Buf("mrevs")
                jmat2 = sbt(sp_, "jmat2", [128, 128], BF16); Bjm2 = Buf("jmat2")
                Es = [sbt(sp_, "Es%d" % i, [128, 64], F32) for i in range(2)]; BEs = [Buf("Es%d" % i) for i in range(2)]
                Ps = [sbt(sp_, "Ps%d" % i, [128, 64], BF16) for i in range(2)]; BPs = [Buf("Ps%d" % i) for i in range(2)]
                atts = sbt(sp_, "atts", [8, 512], F32); Batts = Buf("atts")
                recs = sbt(sp_, "recs", [8, 8], F32); Brecs = Buf("recs")

                fw.dma("sp", lambda e: e.dma_start(out=jmat2[:], in_=jmat_d), Bjm2, writes=[Bjm2])
                fns = []
                for kb in range(17):
                    fns.append(lambda e, kb=kb: e.dma_start(out=mrevs[:, kb, :].rearrange("p (h t) -> p h t", t=TS),
                                                            in_=bass.AP(tblscr.tensor, 2049 - 128 * kb, [[1, 128], [TBL, 8], [1, TS]])))
                fw.dma("sp", fns, Bmrevs, reads=[Btbl], writes=[Bmrevs])
                mflat = mrevs[:].rearrange("p k x -> p (k x)")
                Mflat = Ms[:].rearrange("p k x -> p (k x)")
                for i0 in range(0, 17 * 64, 512):
                    n = min(512, 17 * 64 - i0)
                    ps, Bps = mm_bank()
                    fw.op("pe", lambda e, ps=ps, i0=i0, n=n: e.matmul(ps[:, 0:n], lhsT=jmat2[:], rhs=mflat[:, i0:i0 + n], start=True, stop=True),
                          reads=[Bjm2, Bmrevs], writes=[Bps])
                    fw.op("act", lambda e, ps=ps, i0=i0, n=n: e.activation(out=Mflat[:, i0:i0 + n], in_=ps[:, 0:n], func=AF.Copy), reads=[Bps], writes=[BMs])

                stop_here('smask')
                for m in range(6):
                    ps, Bps = mm_bank()
                    for t2 in range(2):
                        fw.op("pe", lambda e, ps=ps, m=m, t2=t2: e.matmul(ps[:, 0:64], lhsT=sel[t2][:], rhs=Ms[:, 2 * m + t2, :], start=(t2 == 0), stop=(t2 == 1)),
                              reads=[Bsel[t2], BMs], writes=[Bps], pe_accum=True)
                    fw.op("act", lambda e, ps=ps, m=m: e.activation(out=Mc[:, m, :], in_=ps[:, 0:64], func=AF.Copy), reads=[Bps], writes=[BMc])
                fw.op("dve", lambda e: e.tensor_copy(out=Mc[:, 6:11, :], in_=Ms[:, 12:17, :]), reads=[BMs], writes=[BMc])
                def load_cache(s):
                    sl = s % 2
                    fw.dma("pool", lambda e: e.dma_start(out=KTs[sl][:], in_=ckT[s].rearrange("c p t -> p c t")), BKTs[sl], writes=[BKTs[sl]])
                    fw.dma("pool", lambda e: e.dma_start(out=Vs[sl][:], in_=cv[s].rearrange("(kb p) f -> p kb f", p=128)), BVs[sl], writes=[BVs[sl]])
                load_cache(0)
                load_cache(1)

                g0 = NTP
                norm_transpose(xs[:, :], 0, G1, nT, BnT, 0)
                for c in range(4):
                    def ev_q(ps, Bps, c=c):
                        for hp in range(2):
                            fw.op("act", lambda e, hp=hp: e.activation(out=Qbd[hp * 64:(hp + 1) * 64, c, :, hp * 8:(hp + 1) * 8],
                                                                       in_=ps[hp * 64:(hp + 1) * 64, 0:128].rearrange("p (s t) -> p s t", t=TS),
                                                                       func=AF.Copy, scale=0.125), reads=[Bps], writes=[BQbd])
                    proj_fm(c * 128, 128, ev_q)
                for c in range(4):
                    def ev_k(ps, Bps, c=c):
                        fw.op("dve", lambda e: e.tensor_copy(out=KTn[:, c, :], in_=ps[:, 0:128]), reads=[Bps], writes=[BKTn])
                    proj_fm(512 + c * 128, 128, ev_k)
                for (w0, dst, stg, Bstg) in ((512, nks, kst, Bkst), (1024, nvs, vst, Bvst)):
                    ps, Bps = mm_bank()
                    for kc in range(8):
                        fw.op("pe", lambda e, kc=kc, ps=ps, w0=w0: e.matmul(ps[:], lhsT=nT[:, kc, 0:128], rhs=w_in_bf[:, kc, w0:w0 + 512],
                                                                           start=(kc == 0), stop=(kc == 7)), reads=[Bwin[w0 // 512], BnT], writes=[Bps], pe_accum=True)
                    fw.op("act", lambda e, ps=ps, stg=stg: e.activation(out=stg[:], in_=ps[:], func=AF.Copy), reads=[Bps], writes=[Bstg])
                    fw.dma("sp", lambda e, dst=dst, stg=stg: e.dma_start(out=dst, in_=stg[:]), Bstg, reads=[Bstg])
                stop_here('sproj')
                scs = sbt(sp_, "scs", [128, 4, NSS, 3], F32); Bscs = Buf("scs")
                fw.dma("sp", lambda e: e.dma_start(out=scs[:], in_=sconvT), Bscs, writes=[Bscs])
                for c in range(4):
                    fw.op("pool", lambda e, c=c: e.tensor_copy(out=xr[:, c, 0:176].rearrange("p (s j) -> p s j", j=11)[:, :, 0:3], in_=scs[:, c, :, :]),
                          reads=[Bscs], writes=[Bxr[c]])
                shs = sbt(sp_, "shs", [128, 4, NSS], F32); Bshs = Buf("shs")
                fw.dma("sp", lambda e: e.dma_start(out=shs[:], in_=shT), Bshs, writes=[Bshs])

                def scan_s(c):
                    for s in range(NSS):
                        fw.op("dve", lambda e, s=s: e.tensor_tensor_scan(out=hs[:, s * 8:(s + 1) * 8], data0=a_t2[c % 2][:, s * 8:(s + 1) * 8], data1=t_g2[c % 2][:, s * 8:(s + 1) * 8],
                                                                        initial=shs[:, c, s:s + 1], op0=ALU.mult, op1=ALU.add),
                              reads=[Bat2[c % 2], Btg2[c % 2], Bshs], writes=[Bhs])

                def fin_s(c):
                    xv = xr[:, c, 0:176].rearrange("p (s j) -> p s j", j=11)
                    fw.op("dve", lambda e: e.tensor_copy(out=fin[:, c, 0:48].rearrange("p (j s) -> p j s", s=NSS), in_=xv[:, :, 8:11].rearrange("p s j -> p j s")),
                          reads=[Bxr[c]], writes=[Bfin])
                    fw.op("dve", lambda e: e.tensor_copy(out=fin[:, c, 48:64], in_=hs[:, 0:128].rearrange("p (s t) -> p s t", t=TS)[:, :, 7]),
                          reads=[Bhs], writes=[Bfin])
                def gen_srnn():
                    for c in range(4):
                        xv = xr[:, c, 0:176].rearrange("p (s j) -> p s j", j=11)
                        yield from rnn_chunk(c, 128, True, xv[:, :, 3:11], [xv[:, :, j:j + 8] for j in range(4)], scan_s, fin_s)
                    rnn_norm(128, 0)
                    yield
                    fin_out(64, [(j * 16, 16, ncs[:, j, :]) for j in range(3)], (48, 16), None, nhs)
                    yield
                srnn = gen_srnn()
                srnn_alive = [True]

                def srnn_step():
                    if srnn_alive[0]:
                        try:
                            next(srnn)
                        except StopIteration:
                            srnn_alive[0] = False
                for s in range(NSS):
                    sl = s % 2
                    ps, Bps = pbank[0], Bpb[0]
                    for kc in range(8):
                        fw.op("pe", lambda e, kc=kc, ps=ps, s=s: e.matmul(ps[0:8, :], lhsT=nT[:, kc, s * 8:(s + 1) * 8], rhs=w_in_bf[:, kc, 1024:1536],
                                                                         start=(kc == 0), stop=(kc == 7)), reads=[Bwin[2], BnT], writes=[Bps], pe_accum=True)
                    fw.op("act", lambda e, ps=ps: e.activation(out=Vn[:], in_=ps[0:8, :], func=AF.Copy), reads=[Bps], writes=[BVn])
                    accb, Baccb = pbank[4 + s % 2], Bpb[4 + s % 2]

                    def s_emit_S(kb, s=s, sl=sl):
                        npart = 128 if kb < NKB else 8
                        s_ps, Bs_ps = pbank[2 + kb % 2], Bpb[2 + kb % 2]
                        for c in range(4):
                            if kb < NKB:
                                lhs = KTs[sl][:, c, kb * 128:(kb + 1) * 128]
                                rdl = [BKTs[sl], BQbd]
                            else:
                                lhs = KTn[:, c, s * 8:(s + 1) * 8]
                                rdl = [BKTn, BQbd]
                            fw.op("pe", lambda e, s_ps=s_ps, lhs=lhs, c=c, s=s, npart=npart: e.matmul(
                                s_ps[0:npart, c * 16:(c + 1) * 16], lhsT=lhs, rhs=Qbd[:, c, s, :],
                                start=True, stop=True, skip_group_check=True), reads=rdl, writes=[Bs_ps], pe_accum=True)

                    def s_emit_rest(kb, s=s, sl=sl, accb=accb, Baccb=Baccb):
                        npart = 128 if kb < NKB else 8
                        s_ps, Bs_ps = pbank[2 + kb % 2], Bpb[2 + kb % 2]
                        E_, BE_ = Es[kb % 2], BEs[kb % 2]
                        P_, BP_ = Ps[kb % 2], BPs[kb % 2]
                        fw.op("act", lambda e, s_ps=s_ps, E_=E_, npart=npart: e.activation(out=E_[0:npart, :], in_=s_ps[0:npart, 0:64], func=AF.Exp),
                              reads=[Bs_ps], writes=[BE_])
                        fw.op("dve", lambda e, E_=E_, P_=P_, kb=kb, npart=npart: e.tensor_tensor(out=P_[0:npart, :], in0=E_[0:npart, :], in1=Mc[0:npart, kb, :], op=ALU.mult),
                              reads=[BE_, BMc], writes=[BP_])
                        for cp in range(4):
                            if kb < NKB:
                                lhs = Vs[sl][:, kb, cp * 128:(cp + 1) * 128]
                                rdv = BVs[sl]
                            else:
                                lhs = Vn[0:8, cp * 128:(cp + 1) * 128]
                                rdv = BVn
                            fw.op("pe", lambda e, P_=P_, cp=cp, lhs=lhs, npart=npart, f=(kb == 0 and cp == 0): e.matmul(
                                accb[:, cp * 64:(cp + 1) * 64], lhsT=lhs, rhs=P_[0:npart, :], start=f, stop=False, skip_group_check=True),
                                reads=[BP_, rdv], writes=[Baccb], pe_accum=True)
                        fw.op("pe", lambda e, P_=P_, npart=npart: e.matmul(
                            accb[:, 256:320], lhsT=onesbb[0:npart, :], rhs=P_[0:npart, :], start=False, stop=False, skip_group_check=True),
                            reads=[BP_, Bonesbb], writes=[Baccb], pe_accum=True)

                    s_emit_S(0)
                    for kb in range(NKB + 1):
                        if kb + 1 < NKB + 1:
                            s_emit_S(kb + 1)
                        s_emit_rest(kb)
                        if kb % 2 == 1:
                            srnn_step()
                    if s + 2 < NSS:
                        load_cache(s + 2)
                    fw.op("dve", lambda e, accb=accb: e.reciprocal(out=rd[:], in_=accb[:, 256:320]), reads=[Baccb], writes=[Brd])
                    for hp in range(2):
                        fw.op("dve", lambda e, accb=accb, hp=hp, s=s: e.tensor_tensor(
                            out=attT[hp * 64:(hp + 1) * 64, :, s * 8:(s + 1) * 8],
                            in0=accb[hp * 64:(hp + 1) * 64, 0:320].rearrange("p (c x) -> p c x", x=80)[:, :, hp * 8:hp * 8 + 8],
                            in1=rd[hp * 64:(hp + 1) * 64, :].rearrange("p (c x) -> p c x", x=16)[:, :, hp * 8:hp * 8 + 8], op=ALU.mult),
                            reads=[Baccb, Brd], writes=BattT)
                while srnn_alive[0]:
                    srnn_step()
                grp_norm(attT, BattT, ATG, 0, 128, 0)
                fw.dma("sp", lambda e: e.dma_start(out=mixscr[:, :, NTP:NTP + 128].rearrange("k p t -> p k t"), in_=mixT[:, :, 0:128]),
                       BmixT, reads=[BmixT], writes=[Bmix])
                fw.barrier()
                fw.run()

        stop_here('sample')
        with ExitStack() as p2:
            TB = 3
            NTK = TB * 128
            w_out_bf = sbt(p2, "w_out_bf", [128, 8, D], BF16); Bwo = Buf("w_out_bf")
            w_mi_bf = sbt(p2, "w_mi_bf", [128, 8, DFF], BF16); Bwmi = [Buf("w_mi_bf%d" % q) for q in range(4)]
            w_mo_bf = sbt(p2, "w_mo_bf", [128, 32, D], BF16); Bwmo = [Buf("w_mo_bf%d" % q) for q in range(4)]
            fgB = sbt(p2, "fgB", [128, D], F32); BfgB = Buf("fgB")
            mix2 = sbt(p2, "mix2", [128, 8, NTK], BF16); Bmix2 = Buf("mix2")
            xmid = sbt(p2, "xmid", [128, TB, D], F32); Bxmid = [Buf("xmid%d" % i) for i in range(TB)]
            ss2 = sbt(p2, "ss2", [128, 2], F32); Bss2 = Buf("ss2")
            xn2 = sbt(p2, "xn2", [128, D], BF16); Bxn2 = Buf("xn2")
            n2T = sbt(p2, "n2T", [128, 8, NTK], BF16); Bn2T = Buf("n2T")
            hT = sbt(p2, "hT", [128, 32, NTK], BF16); BhT = Buf("hT")
            rl = sbt(p2, "rl", [128, NTK], F32); Brl = Buf("rl")
            yst = sbt(p2, "yst", [128, D], F32); Byst = Buf("yst")

            fw.dma("pool", lambda e: e.dma_start(out=w_out_bf[:], in_=w_out.rearrange("(kc p) n -> p kc n", p=128)), Bwo, writes=[Bwo])
            for q in range(4):
                fw.dma("pool", lambda e, q=q: e.dma_start(out=w_mi_bf[:, :, q * 1024:(q + 1) * 1024],
                                                          in_=w_mi.rearrange("(kc p) n -> p kc n", p=128)[:, :, q * 1024:(q + 1) * 1024]), Bwmi[q], writes=[Bwmi[q]])
            for q in range(4):
                fw.dma("pool", lambda e, q=q: e.dma_start(out=w_mo_bf[:, q * 8:(q + 1) * 8, :],
                                                          in_=w_mo.rearrange("(kc p) n -> p kc n", p=128)[:, q * 8:(q + 1) * 8, :]), Bwmo[q], writes=[Bwmo[q]])
            fw.dma("sp", lambda e: e.dma_start(out=fgB[:], in_=fg.partition_broadcast(128)), BfgB, writes=[BfgB])

            NBLK = NTOK // 128

            def xsrc_of(g):
                return xp[g * 128:(g + 1) * 128, :] if g < NTP // 128 else xs[:, :]

            def ydst_of(g):
                return yp[g * 128:(g + 1) * 128, :] if g < NTP // 128 else ys[:, :]

            def load_tile_inputs(i):
                fw.dma("sp", lambda e, i=i: e.dma_start(out=mix2[:], in_=mixscr[:, :, i * NTK:(i + 1) * NTK].rearrange("k p t -> p k t")),
                       Bmix2, reads=[Bmix], writes=[Bmix2])

            def load_x(i, j):
                g = i * TB + j
                fw.dma("sp", lambda e, g=g, j=j: e.dma_start(out=xmid[:, j, :], in_=xsrc_of(g)), Bxmid[j], writes=[Bxmid[j]])

            ntiles = NBLK // TB
            load_tile_inputs(0)
            for j in range(TB):
                load_x(0, j)
            for i in range(ntiles):
                for j in range(TB):
                    for hf in range(2):
                        ps, Bps = pbank[hf], Bpb[hf]
                        for kc in range(8):
                            fw.op("pe", lambda e, ps=ps, kc=kc, j=j, hf=hf: e.matmul(
                                ps[:], lhsT=mix2[:, kc, j * 128:(j + 1) * 128], rhs=w_out_bf[:, kc, hf * 512:(hf + 1) * 512],
                                start=(kc == 0), stop=(kc == 7)), reads=[Bmix2, Bwo], writes=[Bps], pe_accum=True)
                        fw.op("dve", lambda e, ps=ps, j=j, hf=hf: e.tensor_tensor(out=xmid[:, j, hf * 512:(hf + 1) * 512], in0=ps[:],
                                                                                  in1=xmid[:, j, hf * 512:(hf + 1) * 512], op=ALU.add),
                              reads=[Bps, Bxmid[j]], writes=[Bxmid[j]])
                if i + 1 < ntiles:
                    load_tile_inputs(i + 1)
                for j in range(TB):
                    rms_rstd(xmid[:, j, :], Bxmid[j], xn2[:], Bxn2, ss2[:], Bss2, D)
                    fw.op("dve", lambda e, j=j: e.tensor_scalar(out=xn2[:], in0=xmid[:, j, :], scalar1=ss2[:, 0:1], scalar2=None, op0=ALU.mult),
                          reads=[Bxmid[j], Bss2], writes=[Bxn2])
                    for kc in range(8):
                        fw.op("pe", lambda e, kc=kc: e.transpose(out=ptp[:, kc * 128:(kc + 1) * 128], in_=xn2[:, kc * 128:(kc + 1) * 128], identity=identb[:]),
                              reads=[Bxn2, Bidb], writes=[Bptp], pe_accum=True)
                    fw.op("dve", lambda e, j=j: e.tensor_tensor(
                        out=n2T[:, :, j * 128:(j + 1) * 128], in0=ptp[:].rearrange("p (k j) -> p k j", k=8),
                        in1=vec[:, G2:G2 + 8].unsqueeze(2).to_broadcast([128, 8, 128]), op=ALU.mult), reads=[Bptp, Bvec], writes=[Bn2T])
                for fc in range(32):
                    ps, Bps = pbank[2 + fc % 2], Bpb[2 + fc % 2]
                    for kc in range(8):
                        fw.op("pe", lambda e, ps=ps, kc=kc, fc=fc: e.matmul(ps[:, 0:NTK], lhsT=w_mi_bf[:, kc, fc * 128:(fc + 1) * 128], rhs=n2T[:, kc, :],
                                                                            start=(kc == 0), stop=(kc == 7)), reads=[Bwmi[fc // 8], Bn2T], writes=[Bps], pe_accum=True)
                    fw.op("act", lambda e, ps=ps: e.activation(out=rl[:], in_=ps[:, 0:NTK], func=AF.Relu), reads=[Bps], writes=[Brl])
                    fw.op("dve", lambda e, fc=fc: e.tensor_tensor(out=hT[:, fc, :], in0=rl[:], in1=rl[:], op=ALU.mult), reads=[Brl], writes=[BhT])
                for j in range(TB):
                    g = i * TB + j
                    for hf in range(2):
                        ps, Bps = pbank[4 + hf], Bpb[4 + hf]
                        for fc in range(32):
                            fw.op("pe", lambda e, ps=ps, fc=fc, j=j, hf=hf: e.matmul(
                                ps[:], lhsT=hT[:, fc, j * 128:(j + 1) * 128], rhs=w_mo_bf[:, fc, hf * 512:(hf + 1) * 512],
                                start=(fc == 0), stop=(fc == 31)), reads=[BhT, Bwmo[fc // 8]], writes=[Bps], pe_accum=True)
                        fw.op("dve", lambda e, ps=ps, j=j, hf=hf: e.tensor_tensor(out=xmid[:, j, hf * 512:(hf + 1) * 512], in0=ps[:],
                                                                                  in1=xmid[:, j, hf * 512:(hf + 1) * 512], op=ALU.add),
                              reads=[Bps, Bxmid[j]], writes=[Bxmid[j]])
                    rms_rstd(xmid[:, j, :], Bxmid[j], xn2[:], Bxn2, ss2[:], Bss2, D)
                    fw.op("dve", lambda e, j=j: e.scalar_tensor_tensor(out=yst[:], in0=xmid[:, j, :], scalar=ss2[:, 0:1], in1=fgB[:],
                                                                      op0=ALU.mult, op1=ALU.mult), reads=[Bxmid[j], Bss2, BfgB], writes=[Byst])
                    fw.dma("sp", lambda e, g=g: e.dma_start(out=ydst_of(g), in_=yst[:]), Byst, reads=[Byst])
                    if i + 1 < ntiles:
                        load_x(i + 1, j)
            fw.barrier()
            fw.run()
    except _Stop:
        pass
    return nc


def _t5_bucket_np(dist):
    dist = np.asarray(dist, np.int32)
    d_f = np.maximum(dist, 16).astype(np.float32)
    large = 16 + (np.log(d_f / np.float32(16)) / np.float32(math.log(2048 / 16)) * np.float32(16)).astype(np.int32)
    large = np.minimum(large, 31)
    return np.where(dist < 16, dist, large)


def _consts():
    delta = np.arange(TBL) - 128
    valid = delta >= 0
    bucket = _t5_bucket_np(np.maximum(delta, 0))
    onehot = np.zeros((32, TBL), np.float32)
    onehot[bucket, np.arange(TBL)] = 1.0
    onehot[:, ~valid] = 0.0
    m = ((delta >= 0) & (delta <= 128)).astype(np.float32) \
        + ((delta >= 0) & (delta <= 512) & (delta % 4 == 0)).astype(np.float32) \
        + ((delta >= 0) & (delta <= 2048) & (delta % 16 == 0)).astype(np.float32)
    mult = np.broadcast_to(m[None, :], (8, TBL)).astype(np.float32).copy()
    return onehot, mult


_NC_CACHE = {}


def kernel(x_prompt, x_sample, cache_k, cache_v, state_conv, state_h, norm1_g, w_in, rel_bias,
           conv_w, conv_b, gate_a_w, gate_a_b, gate_x_w, gate_x_b, lru_lambda, att_out_g,
           rnn_out_g, w_out, norm2_g, w_mlp_in, w_mlp_out, final_g):
    f32 = np.float32
    x_prompt = np.asarray(x_prompt, f32)
    x_sample = np.asarray(x_sample, f32)
    cache_k = np.asarray(cache_k, f32)
    cache_v = np.asarray(cache_v, f32)
    state_conv = np.asarray(state_conv, f32)
    state_h = np.asarray(state_h, f32)

    def fm(v, n):
        return np.asarray(v, f32).reshape(n, 128).T

    vecs = np.zeros((128, NV), f32)
    vecs[:, 0:8] = fm(norm1_g[0], 8)
    vecs[:, 8:16] = fm(norm2_g[0], 8)
    vecs[:, 16:20] = fm(att_out_g[0], 4)
    vecs[:, 20:24] = fm(rnn_out_g[0], 4)
    cw = np.asarray(conv_w[0], f32)
    for c in range(4):
        for j in range(4):
            vecs[:, 24 + c * 4 + j] = cw[j, c * 128:(c + 1) * 128]
    vecs[:, 40:44] = fm(conv_b[0], 4)
    vecs[:, 44:48] = fm(np.asarray(gate_a_b[0], f32).reshape(512), 4)
    vecs[:, 48:52] = fm(np.asarray(gate_x_b[0], f32).reshape(512), 4)
    vecs[:, 52:56] = fm(lru_lambda[0], 4)
    onehot, mult = _consts()
    shared = dict(
        w_in=np.ascontiguousarray(np.asarray(w_in[0], f32)), w_out=np.ascontiguousarray(np.asarray(w_out[0], f32)),
        w_mi=np.ascontiguousarray(np.asarray(w_mlp_in[0], f32)), w_mo=np.ascontiguousarray(np.asarray(w_mlp_out[0], f32)),
        vecs=vecs, fg=np.asarray(final_g, f32), gaw=np.ascontiguousarray(np.asarray(gate_a_w[0], f32)),
        gxw=np.ascontiguousarray(np.asarray(gate_x_w[0], f32)), relb=np.ascontiguousarray(np.asarray(rel_bias, f32)),
        onehot=onehot, mult=mult, identb=np.eye(128).astype(ml_dtypes.bfloat16), identf=np.eye(128, dtype=f32),
        jmat=np.ascontiguousarray(np.eye(128)[::-1]).astype(ml_dtypes.bfloat16),
    )
    rows_sel = np.array([r for r in range(1536) if r % 16 < 8] + list(range(1536, SEQ)))
    assert len(rows_sel) == NKEEP
    sel0 = np.zeros((128, 128), f32)
    sel1 = np.zeros((128, 128), f32)
    for ik in range(128):
        if ik % 16 < 8:
            sel0[ik, (ik // 16) * 8 + ik % 16] = 1.0
            sel1[ik, (8 + ik // 16) * 8 + ik % 16] = 1.0
    shared["sel0"] = sel0.astype(ml_dtypes.bfloat16)
    shared["sel1"] = sel1.astype(ml_dtypes.bfloat16)
    in_maps = []
    for c in range(NCORES):
        ck = cache_k[0, c * NSS:(c + 1) * NSS][:, rows_sel]
        ckT = np.ascontiguousarray(ck.transpose(0, 2, 3, 1)).reshape(NSS, 4, 128, NKEEP)
        sc = state_conv[0, c * NSS:(c + 1) * NSS]
        sconvT = np.ascontiguousarray(sc.reshape(NSS, 3, 4, 128).transpose(3, 2, 0, 1))
        sh = state_h[0, c * NSS:(c + 1) * NSS]
        shT = np.ascontiguousarray(sh.reshape(NSS, 4, 128).transpose(2, 1, 0))
        m = dict(shared)
        m.update(
            xp=np.ascontiguousarray(x_prompt[c * NPS:(c + 1) * NPS].reshape(NTP, D)),
            xs=np.ascontiguousarray(x_sample[c * NSS:(c + 1) * NSS].reshape(NSS * TS, D)),
            ckT=ckT, cv=np.ascontiguousarray(cache_v[0, c * NSS:(c + 1) * NSS][:, rows_sel].reshape(NSS, NKEEP, 512)),
            sconvT=sconvT, shT=shT,
        )
        in_maps.append(m)
    if "nc" not in _NC_CACHE:
        _NC_CACHE["nc"] = build_program()
    nc = _NC_CACHE["nc"]
    res = run_bass_kernel_spmd(nc, in_maps, core_ids=list(range(NCORES)))
    R = res.results

    def cat(name, shape):
        return np.concatenate([np.asarray(r[name], f32).reshape(shape) for r in R], axis=0)

    y_prompt = cat("yp", (NPS, SEQ, D))
    y_sample = cat("ys", (NSS, TS, D))
    nk_p = cat("nkp", (NPS, SEQ, 8, 64))[None]
    nv_p = cat("nvp", (NPS, SEQ, 8, 64))[None]
    nc_p = cat("ncp", (NPS, 3, 512))[None]
    nh_p = cat("nhp", (NPS, 512))[None]
    nk_s = cat("nks", (NSS, TS, 8, 64))[None]
    nv_s = cat("nvs", (NSS, TS, 8, 64))[None]
    nc_s = cat("ncs", (NSS, 3, 512))[None]
    nh_s = cat("nhs", (NSS, 512))[None]
    return (y_prompt, y_sample, nk_p, nv_p, nc_p, nh_p, nk_s, nv_s, nc_s, nh_s)


if __name__ == "__main__":
    import time
    t0 = time.time()
    nc = build_program()
    print("built in", time.time() - t0, "n_instructions", nc.n_instructions())
```

```python
import math
from contextlib import ExitStack
import numpy as np
import ml_dtypes
import concourse.bass as bass
import concourse.mybir as mybir
from concourse.bass_utils import run_bass_kernel_spmd

F32 = mybir.dt.float32
BF16 = mybir.dt.bfloat16
AF = mybir.ActivationFunctionType
ALU = mybir.AluOpType

NCORES = 8
D = 1024
NIN = 2560
DFF = 4096
SEQ = 2048
NPS = 2
NSS = 16
TS = 8
NTP = NPS * SEQ
NTOK = NTP + NSS * TS
TBL = 2304
NKEEP = 1280
NKB = NKEEP // 128
EPS = 1e-6
NV = 56


class Buf:
    __slots__ = ("name", "writers", "readers", "dsem", "dcount", "multi")

    def __init__(self, name, multi=False):
        self.name = name
        self.writers = []
        self.readers = []
        self.dsem = None
        self.dcount = 0
        self.multi = multi


class Eng:
    def __init__(self, name):
        self.name = name
        self.thunks = []
        self.count = 0
        self.seen = {}


class FW:
    ENG_NAMES = ("pe", "act", "dve", "pool", "sp")

    def __init__(self, nc, stack):
        self.nc = nc
        self.stack = stack
        self.engs = {n: Eng(n) for n in self.ENG_NAMES}
        self.sems = {}
        for n in self.ENG_NAMES:
            self.sems[("eng", n)] = stack.enter_context(nc.semaphore("s_" + n))
        self.dma_bufs = []

    def _dsem(self, buf):
        if buf.dsem is None:
            key = ("dma", len(self.dma_bufs))
            self.sems[key] = self.stack.enter_context(self.nc.semaphore("d%d" % len(self.dma_bufs)))
            buf.dsem = key
            self.dma_bufs.append(buf)
        return buf.dsem

    def _deps(self, reads, writes, pe_accum=False):
        deps = {}

        def add(tok):
            k, v = tok
            if deps.get(k, 0) < v:
                deps[k] = v
        for b in reads:
            for t in b.writers:
                add(t)
        for b in writes:
            if b.multi:
                continue
            for t in b.writers:
                if pe_accum and t[0] == ("eng", "pe"):
                    continue
                add(t)
            for t in b.readers:
                add(t)
        return deps

    def _emit_waits(self, e, deps):
        E = self.engs[e]
        for k, v in deps.items():
            if E.seen.get(k, 0) >= v:
                continue
            E.seen[k] = v
            h = self.sems[k]
            E.thunks.append(lambda eng, h=h, v=v: eng.wait_ge(h, v))

    def _update(self, reads, writes, tok):
        for b in reads:
            b.readers.append(tok)
            if len(b.readers) > 64:
                mx = {}
                for k, v in b.readers:
                    if mx.get(k, 0) < v:
                        mx[k] = v
                b.readers = list(mx.items())
        for b in writes:
            if b.multi:
                b.writers.append(tok)
                if len(b.writers) > 64:
                    mx = {}
                    for k, v in b.writers:
                        if mx.get(k, 0) < v:
                            mx[k] = v
                    b.writers = list(mx.items())
            else:
                b.writers = [tok]
                b.readers = []

    def op(self, e, fn, reads=(), writes=(), pe_accum=False):
        E = self.engs[e]
        self._emit_waits(e, self._deps(reads, writes, pe_accum))
        sem = self.sems[("eng", e)]
        E.count += 1
        tok = (("eng", e), E.count)
        E.thunks.append(lambda eng, fn=fn, sem=sem: fn(eng).then_inc(sem, 1))
        self._update(reads, writes, tok)
        return tok

    def dma(self, q, fns, primary, reads=(), writes=()):
        if not isinstance(fns, (list, tuple)):
            fns = [fns]
        E = self.engs[q]
        self._emit_waits(q, self._deps(reads, writes))
        key = self._dsem(primary)
        sem = self.sems[key]
        for fn in fns:
            primary.dcount += 16
            E.thunks.append(lambda eng, fn=fn, sem=sem: fn(eng).then_inc(sem, 16))
        tok = (key, primary.dcount)
        self._update(reads, writes, tok)
        return tok

    def barrier(self):
        deps = {}
        for b in self.dma_bufs:
            if b.dcount:
                deps[b.dsem] = b.dcount
        for n in self.ENG_NAMES:
            if self.engs[n].count:
                deps[("eng", n)] = self.engs[n].count
        for n in self.ENG_NAMES:
            d = {k: v for k, v in deps.items() if k != ("eng", n)}
            self._emit_waits(n, d)

    def run(self):
        nc = self.nc
        with nc.Block() as block:
            @block.tensor
            def _(eng):
                for t in self.engs["pe"].thunks:
                    t(eng)

            @block.scalar
            def _(eng):
                for t in self.engs["act"].thunks:
                    t(eng)

            @block.vector
            def _(eng):
                for t in self.engs["dve"].thunks:
                    t(eng)

            @block.gpsimd
            def _(eng):
                for t in self.engs["pool"].thunks:
                    t(eng)

            @block.sync
            def _(eng):
                for t in self.engs["sp"].thunks:
                    t(eng)
        for n in self.ENG_NAMES:
            self.engs[n].thunks = []


class _Stop(Exception):
    pass


DBG = dict(stop=None, nseq=NPS, ntile=4, proj=True, kv=True, rnn=True, attn=True, sample=True, sattn=True, phase2=True)

def build_program():
    nc = bass.Bass("TRN2", target_bir_lowering=False)

    def din(name, shape, dt=F32):
        return nc.dram_tensor(name, shape, dt, kind="ExternalInput").ap()

    def dout(name, shape, dt=F32):
        return nc.dram_tensor(name, shape, dt, kind="ExternalOutput").ap()

    xp = din("xp", [NTP, D])
    xs = din("xs", [NSS * TS, D])
    ckT = din("ckT", [NSS, 4, 128, NKEEP])
    cv = din("cv", [NSS, NKEEP, 512])
    sel0_d = din("sel0", [128, 128], BF16)
    sel1_d = din("sel1", [128, 128], BF16)
    sconvT = din("sconvT", [128, 4, NSS, 3])
    shT = din("shT", [128, 4, NSS])
    w_in = din("w_in", [D, NIN])
    w_out = din("w_out", [D, D])
    w_mi = din("w_mi", [D, DFF])
    w_mo = din("w_mo", [DFF, D])
    vecs = din("vecs", [128, NV])
    fg = din("fg", [D])
    gaw = din("gaw", [8, 64, 64])
    gxw = din("gxw", [8, 64, 64])
    relb = din("relb", [32, 8])
    onehot = din("onehot", [32, TBL])
    mult = din("mult", [8, TBL])
    identb_d = din("identb", [128, 128], BF16)
    identf_d = din("identf", [128, 128])
    jmat_d = din("jmat", [128, 128], BF16)

    yp = dout("yp", [NTP, D])
    ys = dout("ys", [NSS * TS, D])
    nkp = dout("nkp", [NTP, 512])
    nvp = dout("nvp", [NTP, 512])
    ncp = dout("ncp", [NPS, 3, 512])
    nhp = dout("nhp", [NPS, 512])
    nks = dout("nks", [NSS * TS, 512])
    nvs = dout("nvs", [NSS * TS, 512])
    ncs = dout("ncs", [NSS, 3, 512])
    nhs = dout("nhs", [NSS, 512])

    mixscr = nc.dram_tensor("mixscr", [8, 128, NTOK], BF16, kind="Internal").ap()
    tblscr = nc.dram_tensor("tblscr", [8, TBL], BF16, kind="Internal").ap()
    Bmix = Buf("mixscr", multi=True)
    Btbl = Buf("tblscr", multi=True)

    try:
      with ExitStack() as top:
        fw = FW(nc, top)

        def stop_here(tag):
            if DBG.get('stop') == tag:
                fw.barrier()
                fw.run()
                raise _Stop()

        def sbt(st, name, shape, dt):
            return st.enter_context(nc.sbuf_tensor("sb_" + name, shape, dt))

        pbank = [top.enter_context(nc.psum_tensor("pb%d" % i, [128, 512], F32)) for i in range(7)]
        ptp = top.enter_context(nc.psum_tensor("ptp", [128, 1024], BF16))
        Bpb = [Buf("pb%d" % i) for i in range(7)]
        Bptp = Buf("ptp")

        identb = sbt(top, "identb", [128, 128], BF16); Bidb = Buf("identb")
        identf = sbt(top, "identf", [128, 128], F32); Bidf = Buf("identf")
        vec = sbt(top, "vec", [128, NV], F32); Bvec = Buf("vec")
        vec2 = sbt(top, "vec2", [128, 24], F32); Bvec2 = Buf("vec2")
        cst = sbt(top, "cst", [128, 4], F32); Bcst = Buf("cst")
        fw.op("pool", lambda e: e.memset(cst[:, 0:1], EPS), writes=[Bcst])
        fw.op("pool", lambda e: e.memset(cst[:, 1:2], math.log(0.5)), writes=[Bcst])
        fw.op("pool", lambda e: e.memset(cst[:, 2:3], 1.0), writes=[Bcst])
        fw.dma("sp", lambda e: e.dma_start(out=identb[:], in_=identb_d), Bidb, writes=[Bidb])
        fw.dma("sp", lambda e: e.dma_start(out=identf[:], in_=identf_d), Bidf, writes=[Bidf])
        fw.dma("sp", lambda e: e.dma_start(out=vec[:], in_=vecs), Bvec, writes=[Bvec])
        G1, G2, ATG, RNG, CW, CB, BA, BX, LAM = 0, 8, 16, 20, 24, 40, 44, 48, 52
        HBA, HBX, CC, CH = 0, 4, 8, 12
        fw.op("dve", lambda e: e.tensor_scalar(out=vec2[:, 0:8], in0=vec[:, BA:BA + 8], scalar1=0.5, scalar2=None, op0=ALU.mult),
              reads=[Bvec], writes=[Bvec2])
        fw.op("act", lambda e: e.activation(out=vec2[:, 16:20], in_=vec[:, LAM:LAM + 4], func=AF.Exp, scale=-1.0), reads=[Bvec], writes=[Bvec2])
        fw.op("act", lambda e: e.activation(out=vec2[:, 16:20], in_=vec2[:, 16:20], func=AF.Ln, scale=1.0, bias=cst[:, 2:3]), reads=[Bvec2, Bcst], writes=[Bvec2])
        fw.op("dve", lambda e: e.tensor_scalar(out=vec2[:, CC:CC + 4], in0=vec2[:, 16:20], scalar1=-8.0, scalar2=None, op0=ALU.mult),
              reads=[Bvec2], writes=[Bvec2])
        fw.op("dve", lambda e: e.tensor_scalar(out=vec2[:, CH:CH + 4], in0=vec2[:, 16:20], scalar1=-4.0, scalar2=None, op0=ALU.mult),
              reads=[Bvec2], writes=[Bvec2])

        def rms_rstd(eng_in_ap, Bin, junk, Bjunk, ss, Bss, n):
            fw.op("act", lambda e: e.activation(out=junk, in_=eng_in_ap, func=AF.Square, accum_out=ss[:, 0:1]),
                  reads=[Bin], writes=[Bjunk, Bss])
            npart = ss.shape[0]
            fw.op("act", lambda e: e.activation(out=ss[:, 1:2], in_=cst[0:npart, 2:3], func=AF.Copy), reads=[Bss, Bcst], writes=[Bss])
            fw.op("act", lambda e: e.activation(out=ss[:, 1:2], in_=ss[:, 0:1], func=AF.Ln, scale=1.0 / n, bias=cst[0:npart, 0:1]), reads=[Bss, Bcst], writes=[Bss])
            fw.op("act", lambda e: e.activation(out=ss[:, 0:1], in_=ss[:, 1:2], func=AF.Exp, scale=-0.5), reads=[Bss], writes=[Bss])

        with ExitStack() as p1:
            w_in_bf = sbt(p1, "w_in_bf", [128, 8, NIN], BF16); Bwin = [Buf("w_in_bf%d" % g) for g in range(5)]
            wa_bd = sbt(p1, "wa_bd", [128, 4, 128], BF16); Bwa = Buf("wa_bd")
            wx_bd = sbt(p1, "wx_bd", [128, 4, 128], BF16); Bwx = Buf("wx_bd")
            ones_f = sbt(p1, "ones_f", [128, 128], F32); Bones = Buf("ones_f")
            onesb = sbt(p1, "onesb", [128, 1], BF16); Bonesb = Buf("onesb")
            xblk = [sbt(p1, "xblk%d" % i, [128, D], F32) for i in range(2)]; Bxblk = [Buf("xblk%d" % i) for i in range(2)]
            ss = sbt(p1, "ss", [128, 2], F32); Bss = Buf("ss")
            xn = [sbt(p1, "xn%d" % i, [128, D], BF16) for i in range(2)]; Bxn = [Buf("xn%d" % i) for i in range(2)]
            nT = sbt(p1, "nT", [128, 8, 512], BF16); BnT = Buf("nT")
            mixT = sbt(p1, "mixT", [128, 8, 512], BF16); BmixT = Buf("mixT")
            kst = sbt(p1, "kst", [128, 512], F32); Bkst = Buf("kst")
            vst = sbt(p1, "vst", [128, 512], F32); Bvst = Buf("vst")
            xr = sbt(p1, "xr", [128, 4, 515], F32); Bxr = [Buf("xr%d" % c) for c in range(4)]
            gg2 = [sbt(p1, "gg%d" % i, [128, 512], F32) for i in range(2)]; Bgg2 = [Buf("gg%d" % i) for i in range(2)]
            xc2 = [sbt(p1, "xc%d" % i, [128, 512], F32) for i in range(2)]; Bxc2 = [Buf("xc%d" % i) for i in range(2)]
            xcb2 = [sbt(p1, "xcb%d" % i, [128, 512], BF16) for i in range(2)]; Bxcb2 = [Buf("xcb%d" % i) for i in range(2)]
            t_r2 = [sbt(p1, "t_r%d" % i, [128, 512], F32) for i in range(2)]; Btr2 = [Buf("t_r%d" % i) for i in range(2)]
            t_g2 = [sbt(p1, "t_g%d" % i, [128, 512], F32) for i in range(2)]; Btg2 = [Buf("t_g%d" % i) for i in range(2)]
            a_t2 = [sbt(p1, "a_t%d" % i, [128, 512], F32) for i in range(2)]; Bat2 = [Buf("a_t%d" % i) for i in range(2)]
            tmp2 = [sbt(p1, "tmp%d" % i, [128, 512], F32) for i in range(2)]; Btmp2 = [Buf("tmp%d" % i) for i in range(2)]
            hs = sbt(p1, "hs", [128, 512], F32); Bhs = Buf("hs")
            rnn = sbt(p1, "rnn", [128, 4, 512], F32); Brnn = [Buf("rnn%d" % c) for c in range(4)]
            sqb = sbt(p1, "sqb", [128, 512], F32); Bsqb = Buf("sqb")
            rstdb = sbt(p1, "rstdb", [128, 512], F32); Brstdb = Buf("rstdb")
            hstate = sbt(p1, "hstate", [128, 4], F32); Bhst = Buf("hstate")
            fin = sbt(p1, "fin", [128, 4, 64], F32); Bfin = Buf("fin")
            att = sbt(p1, "att", [128, 4, 512], BF16); Batt = [Buf("att%d" % i) for i in range(4)]
            attn = sbt(p1, "attn", [128, 512], BF16); Battn = Buf("attn")
            rec = sbt(p1, "rec", [128, 4], F32); Brec = Buf("rec")

            for g in range(5):
                fw.dma("pool", lambda e, g=g: e.dma_start(
                    out=w_in_bf[:, :, g * 512:(g + 1) * 512],
                    in_=w_in.rearrange("(kc p) n -> p kc n", p=128)[:, :, g * 512:(g + 1) * 512]), Bwin[g], writes=[Bwin[g]])
            fw.op("pool", lambda e: e.memset(wa_bd[:], 0.0), writes=[Bwa])
            fw.op("pool", lambda e: e.memset(wx_bd[:], 0.0), writes=[Bwx])
            fw.op("pool", lambda e: e.memset(ones_f[:], 1.0), writes=[Bones])
            fw.op("pool", lambda e: e.memset(onesb[:], 1.0), writes=[Bonesb])
            for (src, dst, Bd) in ((gaw, wa_bd, Bwa), (gxw, wx_bd, Bwx)):
                for hp in range(2):
                    fw.dma("pool", lambda e, src=src, dst=dst, hp=hp: e.dma_start(
                        out=dst[hp * 64:(hp + 1) * 64, :, hp * 64:(hp + 1) * 64],
                        in_=src.rearrange("(c two) i o -> two i c o", two=2)[hp]), Bd, writes=[Bd])

            def norm_transpose(xsrc_ap, slot, gcol, dstT, BdstT, col0, ntok=128):
                xb, Bx_ = xblk[slot], Bxblk[slot]
                if xsrc_ap is not None:
                    fw.dma("sp", lambda e: e.dma_start(out=xb[0:ntok, :], in_=xsrc_ap), Bx_, writes=[Bx_])
                rms_rstd(xb[0:ntok, :], Bx_, xn[slot][0:ntok, :], Bxn[slot], ss[0:ntok, :], Bss, D)
                fw.op("act", lambda e: e.activation(out=xn[slot][0:ntok, :], in_=xb[0:ntok, :], func=AF.Copy, scale=ss[0:ntok, 0:1]),
                      reads=[Bx_, Bss], writes=[Bxn[slot]])
                for kc in range(8):
                    fw.op("pe", lambda e, kc=kc: e.transpose(out=ptp[:, kc * 128:kc * 128 + ntok], in_=xn[slot][0:ntok, kc * 128:(kc + 1) * 128],
                                                             identity=identb[0:ntok, 0:ntok]),
                          reads=[Bxn[slot], Bidb], writes=[Bptp], pe_accum=True)
                fw.op("dve", lambda e: e.tensor_tensor(
                    out=dstT[:, :, col0:col0 + ntok], in0=ptp[:].rearrange("p (k j) -> p k j", k=8)[:, :, 0:ntok],
                    in1=vec[:, gcol:gcol + 8].unsqueeze(2).to_broadcast([128, 8, ntok]), op=ALU.mult),
                    reads=[Bptp, Bvec], writes=[BdstT])

            mmi = [0]

            def mm_bank():
                i = mmi[0] % 2
                mmi[0] += 1
                return pbank[i], Bpb[i]

            def proj_fm(wcol0, N, evac):
                ps, Bps = mm_bank()
                for kc in range(8):
                    fw.op("pe", lambda e, kc=kc, ps=ps: e.matmul(ps[:, 0:N], lhsT=w_in_bf[:, kc, wcol0:wcol0 + 128], rhs=nT[:, kc, 0:N],
                                                                 start=(kc == 0), stop=(kc == 7)),
                          reads=[Bwin[wcol0 // 512], BnT], writes=[Bps], pe_accum=True)
                evac(ps, Bps)

            def rnn_chunk(c, N, seg, xr_view, conv_views, scan_fn, last_tile_fin):
                xc, Bxc = xc2[c % 2], Bxc2[c % 2]
                t_g, Btg = t_g2[c % 2], Btg2[c % 2]
                a_t, Bat = a_t2[c % 2], Bat2[c % 2]
                tmp, Btmp = tmp2[c % 2], Btmp2[c % 2]
                gg, Bgg = gg2[c % 2], Bgg2[c % 2]
                xcb, Bxcb = xcb2[c % 2], Bxcb2[c % 2]
                t_r, Btr = t_r2[c % 2], Btr2[c % 2]

                def ev_xr(ps, Bps):
                    fw.op("act", lambda e: e.activation(out=xr_view, in_=ps[:, 0:N] if seg is None else ps[:, 0:N].rearrange("p (s t) -> p s t", t=TS),
                                                        func=AF.Copy), reads=[Bps], writes=[Bxr[c]])
                proj_fm(1536 + c * 128, N, ev_xr)
                yield
                xcv = xc[:, 0:N] if seg is None else xc[:, 0:N].rearrange("p (s t) -> p s t", t=TS)
                cw = CW + c * 4
                fw.op("dve", lambda e: e.tensor_scalar(out=xcv, in0=conv_views[3], scalar1=vec[:, cw + 3:cw + 4], scalar2=vec[:, CB + c:CB + c + 1],
                                                       op0=ALU.mult, op1=ALU.add), reads=[Bxr[c], Bvec], writes=[Bxc])
                for j in (2, 1, 0):
                    fw.op("dve", lambda e, j=j: e.scalar_tensor_tensor(out=xcv, in0=conv_views[j], scalar=vec[:, cw + j:cw + j + 1], in1=xcv,
                                                                      op0=ALU.mult, op1=ALU.add), reads=[Bxr[c], Bvec, Bxc], writes=[Bxc])
                yield
                fw.op("act", lambda e: e.activation(out=xcb[:, 0:N], in_=xc[:, 0:N], func=AF.Copy), reads=[Bxc], writes=[Bxcb])
                psr, Bpsr = mm_bank()
                fw.op("pe", lambda e: e.matmul(psr[:, 0:N], lhsT=wa_bd[:, c, :], rhs=xcb[:, 0:N], start=True, stop=True), reads=[Bwa, Bxcb], writes=[Bpsr])
                psg, Bpsg = mm_bank()
                fw.op("pe", lambda e: e.matmul(psg[:, 0:N], lhsT=wx_bd[:, c, :], rhs=xcb[:, 0:N], start=True, stop=True), reads=[Bwx, Bxcb], writes=[Bpsg])
                psq, Bpsq = pbank[6], Bpb[6]
                for kc in range(8):
                    fw.op("pe", lambda e, kc=kc: e.matmul(psq[:, 0:N], lhsT=w_in_bf[:, kc, 2048 + c * 128:2048 + (c + 1) * 128], rhs=nT[:, kc, 0:N],
                                                          start=(kc == 0), stop=(kc == 7)), reads=[Bwin[4], BnT], writes=[Bpsq], pe_accum=True)
                fw.op("act", lambda e: e.activation(out=t_r[:, 0:N], in_=psr[:, 0:N], func=AF.Tanh, scale=0.5, bias=vec2[:, HBA + c:HBA + c + 1]),
                      reads=[Bpsr, Bvec2], writes=[Btr])
                fw.op("act", lambda e: e.activation(out=t_g[:, 0:N], in_=psg[:, 0:N], func=AF.Tanh, scale=0.5, bias=vec2[:, HBX + c:HBX + c + 1]),
                      reads=[Bpsg, Bvec2], writes=[Btg])
                fw.op("act", lambda e: e.activation(out=gg[:, 0:N], in_=psq[:, 0:N], func=AF.Gelu_apprx_tanh), reads=[Bpsq], writes=[Bgg])
                yield
                fw.op("act", lambda e: e.activation(out=a_t[:, 0:N], in_=t_r[:, 0:N], func=AF.Exp, scale=vec2[:, CH + c:CH + c + 1], bias=vec2[:, CH + c:CH + c + 1]),
                      reads=[Btr, Bvec2], writes=[Bat])
                fw.op("act", lambda e: e.activation(out=tmp[:, 0:N], in_=t_r[:, 0:N], func=AF.Exp, scale=vec2[:, CC + c:CC + c + 1], bias=vec2[:, CC + c:CC + c + 1]),
                      reads=[Btr, Bvec2], writes=[Btmp])
                fw.op("act", lambda e: e.activation(out=tmp[:, 0:N], in_=tmp[:, 0:N], func=AF.Ln, scale=-1.0, bias=cst[:, 2:3]), reads=[Btmp, Bcst], writes=[Btmp])
                fw.op("act", lambda e: e.activation(out=tmp[:, 0:N], in_=tmp[:, 0:N], func=AF.Exp, scale=0.5, bias=cst[:, 1:2]), reads=[Btmp, Bcst], writes=[Btmp])
                yield
                fw.op("dve", lambda e: e.scalar_tensor_tensor(out=t_g[:, 0:N], in0=t_g[:, 0:N], scalar=1.0, in1=xc[:, 0:N], op0=ALU.add, op1=ALU.mult),
                      reads=[Btg, Bxc], writes=[Btg])
                fw.op("dve", lambda e: e.tensor_tensor(out=t_g[:, 0:N], in0=t_g[:, 0:N], in1=tmp[:, 0:N], op=ALU.mult), reads=[Btg, Btmp], writes=[Btg])
                yield
                scan_fn(c)
                fw.op("dve", lambda e: e.tensor_tensor(out=rnn[:, c, 0:N], in0=hs[:, 0:N], in1=gg[:, 0:N], op=ALU.mult), reads=[Bhs, Bgg], writes=[Brnn[c]])
                last_tile_fin(c)
                yield

            def rnn_norm(N, col0):
                grp_norm(rnn, Brnn, RNG, 4, N, col0)

            def grp_norm(src, Bsrc, gcol, dst_c0, N, col0):
                rnn, Brnn, RNG = src, Bsrc, gcol
                ps, Bps = pbank[6], Bpb[6]
                for c in range(4):
                    fw.op("act", lambda e, c=c: e.activation(out=sqb[:, 0:N], in_=rnn[:, c, 0:N], func=AF.Square), reads=[Brnn[c]], writes=[Bsqb])
                    fw.op("pe", lambda e, c=c: e.matmul(ps[:, 0:N], lhsT=ones_f[:], rhs=sqb[:, 0:N], start=(c == 0), stop=(c == 3)),
                          reads=[Bones, Bsqb], writes=[Bps], pe_accum=True)
                fw.op("act", lambda e: e.activation(out=rstdb[:, 0:N], in_=ps[:, 0:N], func=AF.Ln, scale=1.0 / 512, bias=cst[:, 0:1]), reads=[Bps, Bcst], writes=[Brstdb])
                fw.op("act", lambda e: e.activation(out=rstdb[:, 0:N], in_=rstdb[:, 0:N], func=AF.Exp, scale=-0.5), reads=[Brstdb], writes=[Brstdb])
                for c in range(4):
                    fw.op("dve", lambda e, c=c: e.scalar_tensor_tensor(out=mixT[:, dst_c0 + c, col0:col0 + N], in0=rnn[:, c, 0:N], scalar=vec[:, RNG + c:RNG + c + 1],
                                                                      in1=rstdb[:, 0:N], op0=ALU.mult, op1=ALU.mult),
                          reads=[Brnn[c], Bvec, Brstdb], writes=[BmixT])

            def att_norm_block(att_ap, Batt_, ntok, col0):
                rms_rstd(att_ap, Batt_, attn[0:ntok, :], Battn, ss[0:ntok, :], Bss, 512)
                fw.op("act", lambda e: e.activation(out=attn[0:ntok, :], in_=att_ap, func=AF.Copy, scale=ss[0:ntok, 0:1]), reads=[Batt_, Bss], writes=[Battn])
                for cc in range(4):
                    fw.op("pe", lambda e, cc=cc: e.transpose(out=ptp[:, cc * 128:cc * 128 + ntok], in_=attn[0:ntok, cc * 128:(cc + 1) * 128],
                                                             identity=identb[0:ntok, 0:ntok]), reads=[Battn, Bidb], writes=[Bptp], pe_accum=True)
                fw.op("dve", lambda e: e.tensor_tensor(
                    out=mixT[:, 0:4, col0:col0 + ntok], in0=ptp[:, 0:512].rearrange("p (k j) -> p k j", k=4)[:, :, 0:ntok],
                    in1=vec[:, ATG:ATG + 4].unsqueeze(2).to_broadcast([128, 4, ntok]), op=ALU.mult), reads=[Bptp, Bvec], writes=[BmixT])

            def fin_out(ncols, rows_conv, rows_h, conv_dst_fn, h_dst):
                ps, Bps = pbank[6], Bpb[6]
                for c in range(4):
                    fw.op("pe", lambda e, c=c: e.transpose(out=ps[0:ncols, c * 128:(c + 1) * 128], in_=fin[:, c, 0:ncols], identity=identf[:]),
                          reads=[Bfin, Bidf], writes=[Bps], pe_accum=True)
                fw.op("act", lambda e: e.activation(out=kst[0:ncols, :], in_=ps[0:ncols, :], func=AF.Copy), reads=[Bps], writes=[Bkst])
                fns = []
                for (r0, n, dst) in rows_conv:
                    fns.append(lambda e, r0=r0, n=n, dst=dst: e.dma_start(out=dst, in_=kst[r0:r0 + n, :]))
                fns.append(lambda e: e.dma_start(out=h_dst, in_=kst[rows_h[0]:rows_h[0] + rows_h[1], :]))
                fw.dma("sp", fns, Bkst, reads=[Bkst])

            with ExitStack() as pp:
                QT = sbt(pp, "QT", [128, 2, 4, 512], BF16); BQT = Buf("QT")
                fw.op("pool", lambda e: e.memset(QT[:], 0.0), writes=[BQT])
                KT = sbt(pp, "KT", [128, 4, SEQ], BF16); BKT = Buf("KT")
                Vaug = sbt(pp, "Vaug", [128, 16, 8, 66], BF16); BV = Buf("Vaug")
                Mbig = sbt(pp, "Mbig", [128, 8, SEQ], BF16); BM = Buf("Mbig")
                Eb = [sbt(pp, "Eb%d" % i, [128, 512], BF16) for i in range(2)]; BEb = [Buf("Eb%d" % i) for i in range(2)]
                PT = [sbt(pp, "PT%d" % i, [128, 512], BF16) for i in range(3)]; BPT = [Buf("PT%d" % i) for i in range(3)]
                jmat = sbt(pp, "jmat", [128, 128], BF16); Bjm = Buf("jmat")

                def gen_mask():
                    relb_sb, Brelb = tmp2[0][0:32, 0:8], Btmp2[0]
                    oh_sb, Boh = t_r2[0][0:32, :], Btr2[0]
                    mult_sb, Bmult = t_g2[0][0:8, :], Btg2[0]
                    tabf, Btabf = a_t2[0][0:8, :], Bat2[0]
                    tabb, Btabb = mixT[0:8].rearrange("p k t -> p (k t)")[:, 0:TBL], BmixT
                    fw.dma("sp", lambda e: e.dma_start(out=relb_sb, in_=relb), Brelb, writes=[Brelb])
                    fw.dma("sp", lambda e: e.dma_start(out=jmat[:], in_=jmat_d), Bjm, writes=[Bjm])
                    fw.op("pool", lambda e: e.memset(Vaug[:, :, :, 64:65], 1.0), writes=[BV])
                    for i0 in range(0, TBL, 512):
                        n = min(512, TBL - i0)
                        fw.dma("sp", lambda e, i0=i0, n=n: e.dma_start(out=oh_sb[:, 0:n], in_=onehot[:, i0:i0 + n]), Boh, writes=[Boh])
                        fw.dma("sp", lambda e, i0=i0, n=n: e.dma_start(out=mult_sb[:, 0:n], in_=mult[:, i0:i0 + n]), Bmult, writes=[Bmult])
                        ps, Bps = mm_bank()
                        fw.op("pe", lambda e, ps=ps, n=n: e.matmul(ps[0:8, 0:n], lhsT=relb_sb, rhs=oh_sb[:, 0:n], start=True, stop=True),
                              reads=[Brelb, Boh], writes=[Bps])
                        fw.op("act", lambda e, ps=ps, n=n: e.activation(out=tabf[:, 0:n], in_=ps[0:8, 0:n], func=AF.Exp), reads=[Bps], writes=[Btabf])
                        fw.op("dve", lambda e, i0=i0, n=n: e.tensor_tensor(out=tabb[:, i0:i0 + n], in0=tabf[:, 0:n], in1=mult_sb[:, 0:n], op=ALU.mult),
                              reads=[Btabf, Bmult], writes=[Btabb])
                        yield
                    fw.dma("sp", lambda e: e.dma_start(out=tblscr, in_=tabb), Btabb, reads=[Btabb], writes=[Btbl])
                    k = 0
                    for h in range(8):
                        for hf in range(4):
                            mrev, Bmrev = PT[k % 2], BPT[k % 2]
                            k += 1
                            fw.dma("sp", lambda e, h=h, hf=hf, mrev=mrev: e.dma_start(out=mrev[:], in_=bass.AP(tblscr.tensor, h * TBL + 1 + hf * 512, [[1, 128], [1, 512]])),
                                   Bmrev, reads=[Btbl], writes=[Bmrev])
                            ps, Bps = mm_bank()
                            fw.op("pe", lambda e, ps=ps, mrev=mrev: e.matmul(ps[:], lhsT=jmat[:], rhs=mrev[:], start=True, stop=True),
                                  reads=[Bjm, Bmrev], writes=[Bps])
                            fw.op("act", lambda e, ps=ps, h=h, hf=hf: e.activation(out=Mbig[:, h, hf * 512:(hf + 1) * 512], in_=ps[:], func=AF.Copy),
                                  reads=[Bps], writes=[BM])
                            yield
                mask_gen = gen_mask()

                pending_tail = []
                for b in range(DBG['nseq']):
                    fw.op("pool", lambda e: e.memset(xr[:, :, 0:3], 0.0), writes=Bxr)
                    fw.op("pool", lambda e: e.memset(hstate[:], 0.0), writes=[Bhst])
                    for T in range(DBG['ntile']):
                        g0 = b * SEQ + T * 512
                        def gen_norm(g0):
                            for blk in range(4):
                                pre = (blk < 2) and not (g0 == 0)
                                norm_transpose(None if pre else xp[g0 + blk * 128:g0 + (blk + 1) * 128, :], blk % 2, G1, nT, BnT, blk * 128)
                                yield
                            gn = g0 + 512
                            if gn < NTP:
                                for blk in range(2):
                                    fw.dma("sp", lambda e, gn=gn, blk=blk: e.dma_start(out=xblk[blk][:], in_=xp[gn + blk * 128:gn + (blk + 1) * 128, :]),
                                           Bxblk[blk], writes=[Bxblk[blk]])

                        def gen_head(b=b, T=T, g0=g0):
                            if g0 == 0:
                                yield from gen_norm(g0)
                            for c in range(4):
                                def ev_q(ps, Bps, c=c):
                                    for hp in range(2):
                                        fw.op("act", lambda e, hp=hp: e.activation(out=QT[hp * 64:(hp + 1) * 64, hp, c, :], in_=ps[hp * 64:(hp + 1) * 64, :],
                                                                                   func=AF.Copy, scale=0.125), reads=[Bps], writes=[BQT])
                                proj_fm(c * 128, 512, ev_q)
                                yield
                            for c in range(4):
                                def ev_k(ps, Bps, c=c, T=T):
                                    fw.op("dve", lambda e: e.tensor_copy(out=KT[:, c, T * 512:(T + 1) * 512], in_=ps[:]), reads=[Bps], writes=[BKT])
                                proj_fm(512 + c * 128, 512, ev_k)
                                yield

                        gh = gen_head()
                        if b == 0 and T == 0:
                            alive_h, alive_m = True, True
                            while alive_h or alive_m:
                                if alive_h:
                                    try:
                                        next(gh)
                                    except StopIteration:
                                        alive_h = False
                                if alive_m:
                                    try:
                                        next(mask_gen)
                                    except StopIteration:
                                        alive_m = False
                                if not alive_h:
                                    break
                        else:
                            for _ in gh:
                                pass
                        for fn_ in pending_tail:
                            fn_()
                        del pending_tail[:]
                        stop_here('qk')
                        def gen_kv(T=T, g0=g0):
                            for blk in range(4):
                                r0 = g0 + blk * 128
                                ps, Bps = mm_bank()
                                for kc in range(8):
                                    fw.op("pe", lambda e, kc=kc, ps=ps, blk=blk: e.matmul(ps[:], lhsT=nT[:, kc, blk * 128:(blk + 1) * 128], rhs=w_in_bf[:, kc, 512:1024],
                                                                                           start=(kc == 0), stop=(kc == 7)), reads=[Bwin[1], BnT], writes=[Bps], pe_accum=True)
                                fw.op("dve", lambda e, ps=ps: e.tensor_copy(out=kst[:], in_=ps[:]), reads=[Bps], writes=[Bkst])
                                fw.dma("sp", lambda e, r0=r0: e.dma_start(out=nkp[r0:r0 + 128, :], in_=kst[:]), Bkst, reads=[Bkst])
                                yield
                                ps, Bps = mm_bank()
                                for kc in range(8):
                                    fw.op("pe", lambda e, kc=kc, ps=ps, blk=blk: e.matmul(ps[:], lhsT=nT[:, kc, blk * 128:(blk + 1) * 128], rhs=w_in_bf[:, kc, 1024:1536],
                                                                                           start=(kc == 0), stop=(kc == 7)), reads=[Bwin[2], BnT], writes=[Bps], pe_accum=True)
                                fw.op("act", lambda e, ps=ps: e.activation(out=vst[:], in_=ps[:], func=AF.Copy), reads=[Bps], writes=[Bvst])
                                if not DBG.get('novaug'):
                                    fw.op("dve", lambda e, blk=blk, T=T: e.tensor_copy(out=Vaug[:, T * 4 + blk, :, 0:64], in_=vst[:].rearrange("p (h d) -> p h d", h=8)),
                                          reads=[Bvst], writes=[BV])
                                fw.dma("sp", lambda e, r0=r0: e.dma_start(out=nvp[r0:r0 + 128, :], in_=vst[:]), Bvst, reads=[Bvst])
                                yield
                        stop_here('kv')
                        last = (T == 3)

                        def scan_p(c):
                            fw.op("dve", lambda e: e.tensor_tensor_scan(out=hs[:], data0=a_t2[c % 2][:], data1=t_g2[c % 2][:], initial=hstate[:, c:c + 1], op0=ALU.mult, op1=ALU.add),
                                  reads=[Bat2[c % 2], Btg2[c % 2], Bhst], writes=[Bhs])
                            fw.op("dve", lambda e: e.tensor_copy(out=hstate[:, c:c + 1], in_=hs[:, 511:512]), reads=[Bhs], writes=[Bhst])

                        def fin_p(c):
                            if last:
                                fw.op("dve", lambda e: e.tensor_copy(out=fin[:, c, 0:3], in_=xr[:, c, 512:515]), reads=[Bxr[c]], writes=[Bfin])
                                fw.op("dve", lambda e: e.tensor_copy(out=fin[:, c, 3:4], in_=hs[:, 511:512]), reads=[Bhs], writes=[Bfin])
                            else:
                                fw.op("dve", lambda e: e.tensor_copy(out=xr[:, c, 0:3], in_=xr[:, c, 512:515]), reads=[Bxr[c]], writes=[Bxr[c]])
                        def gen_rnn(b=b, last=last):
                            for c0 in (0, 2):
                                gens = [rnn_chunk(c, 512, None, xr[:, c, 3:515], [xr[:, c, j:j + 512] for j in range(4)], scan_p, fin_p)
                                        for c in (c0, c0 + 1)]
                                alive = [True, True]
                                for _ in range(2):
                                    next(gens[0])
                                    yield
                                while alive[0] or alive[1]:
                                    for gi in range(2):
                                        if alive[gi]:
                                            try:
                                                next(gens[gi])
                                            except StopIteration:
                                                alive[gi] = False
                                    yield
                            rnn_norm(512, 0)
                            yield
                            if last:
                                fin_out(4, [(0, 3, ncp[b])], (3, 1), None, nhp[b:b + 1, :])
                                yield

                        def gen_attn(T=T):
                            its = [(h, kb) for h in range(8) for kb in range(4 * T + 4)]

                            def bufs(i):
                                return (pbank[2 + i % 2], Bpb[2 + i % 2], Eb[i % 2], BEb[i % 2], PT[i % 3], BPT[i % 3])

                            def emit_S(i):
                                h, kb = its[i]
                                c, hp = h // 2, h % 2
                                c0 = max(0, 128 * kb - T * 512)
                                s_ps, Bs_ps = bufs(i)[0:2]
                                fw.op("pe", lambda e, s_ps=s_ps, c=c, hp=hp, kb=kb, c0=c0: e.matmul(
                                    s_ps[:, c0:512], lhsT=KT[:, c, kb * 128:(kb + 1) * 128], rhs=QT[:, hp, c, c0:512],
                                    start=True, stop=True), reads=[BKT, BQT], writes=[Bs_ps])

                            def emit_mid(i):
                                h, kb = its[i]
                                c0 = max(0, 128 * kb - T * 512)
                                s_ps, Bs_ps, E_, BE_, P_, BP_ = bufs(i)
                                fw.op("act", lambda e, s_ps=s_ps, E_=E_, c0=c0: e.activation(out=E_[:, c0:512], in_=s_ps[:, c0:512], func=AF.Exp),
                                      reads=[Bs_ps], writes=[BE_])
                                j0 = T * 512 + c0 - 128 * kb
                                fw.op("dve", lambda e, E_=E_, P_=P_, c0=c0, j0=j0, h=h: e.tensor_tensor(
                                    out=P_[:, c0:512], in0=E_[:, c0:512], in1=Mbig[:, h, j0:j0 + 512 - c0], op=ALU.mult), reads=[BE_, BM], writes=[BP_])

                            def emit_pv(i):
                                h, kb = its[i]
                                c0 = max(0, 128 * kb - T * 512)
                                s_ps, Bs_ps, E_, BE_, P_, BP_ = bufs(i)
                                acc, Bacc = pbank[4 + h % 2], Bpb[4 + h % 2]
                                first = (kb == 0)
                                for ii in range(c0 // 128, 4):
                                    fw.op("pe", lambda e, acc=acc, P_=P_, ii=ii, kb=kb, h=h, first=first: e.matmul(
                                        acc[:, ii * 65:(ii + 1) * 65], lhsT=P_[:, ii * 128:(ii + 1) * 128], rhs=Vaug[:, kb, h, 0:65],
                                        start=first, stop=False, skip_group_check=True), reads=[BP_, BV], writes=[Bacc], pe_accum=True)
                                    first = False
                                if kb == 4 * T + 3:
                                    accv = acc[:, 0:260].rearrange("p (i d) -> p i d", d=65)
                                    fw.op("dve", lambda e, accv=accv: e.reciprocal(out=rec[:].unsqueeze(2), in_=accv[:, :, 64:65]), reads=[Bacc], writes=[Brec])
                                    fw.op("dve", lambda e, accv=accv, h=h: e.tensor_tensor(
                                        out=att[:, :, h * 64:(h + 1) * 64], in0=accv[:, :, 0:64], in1=rec[:].unsqueeze(2).to_broadcast([128, 4, 64]), op=ALU.mult),
                                        reads=[Bacc, Brec], writes=Batt)

                            n_it = len(its)
                            emit_S(0)
                            for i in range(n_it + 1):
                                if i + 1 < n_it:
                                    emit_S(i + 1)
                                if i < n_it:
                                    emit_mid(i)
                                if i >= 1:
                                    emit_pv(i - 1)
                                yield

                        if DBG.get('interleave', 1):
                            def gen_side(g0=g0):
                                yield from gen_rnn()
                                if g0 + 512 < NTP:
                                    yield from gen_norm(g0 + 512)
                            gr = gen_side()
                            ga = gen_attn()
                            gk = gen_kv()

                            def step(g):
                                try:
                                    next(g)
                                    return True
                                except StopIteration:
                                    return False
                            if T == 0:
                                while step(gk):
                                    step(gr)
                                    if g0 == 0:
                                        step(mask_gen)
                                        step(mask_gen)
                                        step(mask_gen)
                                if g0 == 0:
                                    while step(mask_gen):
                                        pass
                            else:
                                per = -(-8 // (4 * T))
                                for _ in range(4 * T):
                                    step(ga)
                                    for _ in range(per):
                                        step(gk)
                                    step(gr)
                                while step(gk):
                                    pass
                            n_att = 8 * (4 * T + 4)
                            kstep = 1
                            astep = max(1, n_att // 44)
                            alive_a, alive_r = True, True
                            while alive_a or alive_r:
                                for _ in range(astep):
                                    if alive_a:
                                        try:
                                            next(ga)
                                        except StopIteration:
                                            alive_a = False
                                for _ in range(kstep if alive_a else 4):
                                    if alive_r:
                                        try:
                                            next(gr)
                                        except StopIteration:
                                            alive_r = False
                        else:
                            for _ in gen_kv():
                                pass
                            for _ in gen_rnn():
                                pass
                            for _ in gen_attn():
                                pass
                            if g0 + 512 < NTP:
                                for _ in gen_norm(g0 + 512):
                                    pass
                        stop_here('attn')

                        def tile_tail(g0=g0):
                            for i in range(4):
                                att_norm_block(att[:, i, :], Batt[i], 128, i * 128)
                            fw.dma("sp", lambda e, g0=g0: e.dma_start(out=mixscr[:, :, g0:g0 + 512].rearrange("k p t -> p k t"), in_=mixT[:]),
                                   BmixT, reads=[BmixT], writes=[Bmix])
                        pending_tail.append(tile_tail)
                for fn_ in pending_tail:
                    fn_()
                del pending_tail[:]
                fw.barrier()
                fw.run()

            stop_here('prompt')
            with ExitStack() as sp_:
                Qbd = sbt(sp_, "Qbd", [128, 4, NSS, 16], BF16); BQbd = Buf("Qbd")
                onesbb = sbt(sp_, "onesbb", [128, 128], BF16); Bonesbb = Buf("onesbb")
                attT = sbt(sp_, "attT", [128, 4, 128], F32); BattT = [Buf("attT%d" % c) for c in range(4)]
                rd = sbt(sp_, "rd", [128, 64], F32); Brd = Buf("rd")
                fw.op("pool", lambda e: e.memset(Qbd[:], 0.0), writes=[BQbd])
                fw.op("pool", lambda e: e.memset(onesbb[:], 1.0), writes=[Bonesbb])
                KTn = sbt(sp_, "KTn", [128, 4, 128], BF16); BKTn = Buf("KTn")
                KTs = [sbt(sp_, "KTs%d" % i, [128, 4, NKEEP], BF16) for i in range(2)]; BKTs = [Buf("KTs%d" % i) for i in range(2)]
                Vs = [sbt(sp_, "Vs%d" % i, [128, NKB, 512], BF16) for i in range(2)]; BVs = [Buf("Vs%d" % i) for i in range(2)]
                Vn = sbt(sp_, "Vn", [8, 512], BF16); BVn = Buf("Vn")
                Ms = sbt(sp_, "Ms", [128, 17, 64], BF16); BMs = Buf("Ms")
                Mc = sbt(sp_, "Mc", [128, NKB + 1, 64], BF16); BMc = Buf("Mc")
                sel = [sbt(sp_, "sel%d" % i, [128, 128], BF16) for i in range(2)]; Bsel = [Buf("sel%d" % i) for i in range(2)]
                fw.dma("sp", lambda e: e.dma_start(out=sel[0][:], in_=sel0_d), Bsel[0], writes=[Bsel[0]])
                fw.dma("sp", lambda e: e.dma_start(out=sel[1][:], in_=sel1_d), Bsel[1], writes=[Bsel[1]])
                mrevs = sbt(sp_, "mrevs", [128, 17, 64], BF16); Bmrevs = Buf("mrevs")
                jmat2 = sbt(sp_, "jmat2", [128, 128], BF16); Bjm2 = Buf("jmat2")
                Es = [sbt(sp_, "Es%d" % i, [128, 64], F32) for i in range(2)]; BEs = [Buf("Es%d" % i) for i in range(2)]
                Ps = [sbt(sp_, "Ps%d" % i, [128, 64], BF16) for i in range(2)]; BPs = [Buf("Ps%d" % i) for i in range(2)]
                atts = sbt(sp_, "atts", [8, 512], F32); Batts = Buf("atts")
                recs = sbt(sp_, "recs", [8, 8], F32); Brecs = Buf("recs")

                fw.dma("sp", lambda e: e.dma_start(out=jmat2[:], in_=jmat_d), Bjm2, writes=[Bjm2])
                fns = []
                for kb in range(17):
                    fns.append(lambda e, kb=kb: e.dma_start(out=mrevs[:, kb, :].rearrange("p (h t) -> p h t", t=TS),
                                                            in_=bass.AP(tblscr.tensor, 2049 - 128 * kb, [[1, 128], [TBL, 8], [1, TS]])))
                fw.dma("sp", fns, Bmrevs, reads=[Btbl], writes=[Bmrevs])
                mflat = mrevs[:].rearrange("p k x -> p (k x)")
                Mflat = Ms[:].rearrange("p k x -> p (k x)")
                for i0 in range(0, 17 * 64, 512):
                    n = min(512, 17 * 64 - i0)
                    ps, Bps = mm_bank()
                    fw.op("pe", lambda e, ps=ps, i0=i0, n=n: e.matmul(ps[:, 0:n], lhsT=jmat2[:], rhs=mflat[:, i0:i0 + n], start=True, stop=True),
                          reads=[Bjm2, Bmrevs], writes=[Bps])
                    fw.op("act", lambda e, ps=ps, i0=i0, n=n: e.activation(out=Mflat[:, i0:i0 + n], in_=ps[:, 0:n], func=AF.Copy), reads=[Bps], writes=[BMs])

                stop_here('smask')
                for m in range(6):
                    ps, Bps = mm_bank()
                    for t2 in range(2):
                        fw.op("pe", lambda e, ps=ps, m=m, t2=t2: e.matmul(ps[:, 0:64], lhsT=sel[t2][:], rhs=Ms[:, 2 * m + t2, :], start=(t2 == 0), stop=(t2 == 1)),
                              reads=[Bsel[t2], BMs], writes=[Bps], pe_accum=True)
                    fw.op("act", lambda e, ps=ps, m=m: e.activation(out=Mc[:, m, :], in_=ps[:, 0:64], func=AF.Copy), reads=[Bps], writes=[BMc])
                fw.op("dve", lambda e: e.tensor_copy(out=Mc[:, 6:11, :], in_=Ms[:, 12:17, :]), reads=[BMs], writes=[BMc])
                def load_cache(s):
                    sl = s % 2
                    fw.dma("pool", lambda e: e.dma_start(out=KTs[sl][:], in_=ckT[s].rearrange("c p t -> p c t")), BKTs[sl], writes=[BKTs[sl]])
                    fw.dma("pool", lambda e: e.dma_start(out=Vs[sl][:], in_=cv[s].rearrange("(kb p) f -> p kb f", p=128)), BVs[sl], writes=[BVs[sl]])
                load_cache(0)
                load_cache(1)

                g0 = NTP
                norm_transpose(xs[:, :], 0, G1, nT, BnT, 0)
                for c in range(4):
                    def ev_q(ps, Bps, c=c):
                        for hp in range(2):
                            fw.op("act", lambda e, hp=hp: e.activation(out=Qbd[hp * 64:(hp + 1) * 64, c, :, hp * 8:(hp + 1) * 8],
                                                                       in_=ps[hp * 64:(hp + 1) * 64, 0:128].rearrange("p (s t) -> p s t", t=TS),
                                                                       func=AF.Copy, scale=0.125), reads=[Bps], writes=[BQbd])
                    proj_fm(c * 128, 128, ev_q)
                for c in range(4):
                    def ev_k(ps, Bps, c=c):
                        fw.op("dve", lambda e: e.tensor_copy(out=KTn[:, c, :], in_=ps[:, 0:128]), reads=[Bps], writes=[BKTn])
                    proj_fm(512 + c * 128, 128, ev_k)
                for (w0, dst, stg, Bstg) in ((512, nks, kst, Bkst), (1024, nvs, vst, Bvst)):
                    ps, Bps = mm_bank()
                    for kc in range(8):
                        fw.op("pe", lambda e, kc=kc, ps=ps, w0=w0: e.matmul(ps[:], lhsT=nT[:, kc, 0:128], rhs=w_in_bf[:, kc, w0:w0 + 512],
                                                                           start=(kc == 0), stop=(kc == 7)), reads=[Bwin[w0 // 512], BnT], writes=[Bps], pe_accum=True)
                    fw.op("act", lambda e, ps=ps, stg=stg: e.activation(out=stg[:], in_=ps[:], func=AF.Copy), reads=[Bps], writes=[Bstg])
                    fw.dma("sp", lambda e, dst=dst, stg=stg: e.dma_start(out=dst, in_=stg[:]), Bstg, reads=[Bstg])
                stop_here('sproj')
                scs = sbt(sp_, "scs", [128, 4, NSS, 3], F32); Bscs = Buf("scs")
                fw.dma("sp", lambda e: e.dma_start(out=scs[:], in_=sconvT), Bscs, writes=[Bscs])
                for c in range(4):
                    fw.op("pool", lambda e, c=c: e.tensor_copy(out=xr[:, c, 0:176].rearrange("p (s j) -> p s j", j=11)[:, :, 0:3], in_=scs[:, c, :, :]),
                          reads=[Bscs], writes=[Bxr[c]])
                shs = sbt(sp_, "shs", [128, 4, NSS], F32); Bshs = Buf("shs")
                fw.dma("sp", lambda e: e.dma_start(out=shs[:], in_=shT), Bshs, writes=[Bshs])

                def scan_s(c):
                    for s in range(NSS):
                        fw.op("dve", lambda e, s=s: e.tensor_tensor_scan(out=hs[:, s * 8:(s + 1) * 8], data0=a_t2[c % 2][:, s * 8:(s + 1) * 8], data1=t_g2[c % 2][:, s * 8:(s + 1) * 8],
                                                                        initial=shs[:, c, s:s + 1], op0=ALU.mult, op1=ALU.add),
                              reads=[Bat2[c % 2], Btg2[c % 2], Bshs], writes=[Bhs])

                def fin_s(c):
                    xv = xr[:, c, 0:176].rearrange("p (s j) -> p s j", j=11)
                    fw.op("dve", lambda e: e.tensor_copy(out=fin[:, c, 0:48].rearrange("p (j s) -> p j s", s=NSS), in_=xv[:, :, 8:11].rearrange("p s j -> p j s")),
                          reads=[Bxr[c]], writes=[Bfin])
                    fw.op("dve", lambda e: e.tensor_copy(out=fin[:, c, 48:64], in_=hs[:, 0:128].rearrange("p (s t) -> p s t", t=TS)[:, :, 7]),
                          reads=[Bhs], writes=[Bfin])
                def gen_srnn():
                    for c in range(4):
                        xv = xr[:, c, 0:176].rearrange("p (s j) -> p s j", j=11)
                        yield from rnn_chunk(c, 128, True, xv[:, :, 3:11], [xv[:, :, j:j + 8] for j in range(4)], scan_s, fin_s)
                    rnn_norm(128, 0)
                    yield
                    fin_out(64, [(j * 16, 16, ncs[:, j, :]) for j in range(3)], (48, 16), None, nhs)
                    yield
                srnn = gen_srnn()
                srnn_alive = [True]

                def srnn_step():
                    if srnn_alive[0]:
                        try:
                            next(srnn)
                        except StopIteration:
                            srnn_alive[0] = False
                for s in range(NSS):
                    sl = s % 2
                    ps, Bps = pbank[0], Bpb[0]
                    for kc in range(8):
                        fw.op("pe", lambda e, kc=kc, ps=ps, s=s: e.matmul(ps[0:8, :], lhsT=nT[:, kc, s * 8:(s + 1) * 8], rhs=w_in_bf[:, kc, 1024:1536],
                                                                         start=(kc == 0), stop=(kc == 7)), reads=[Bwin[2], BnT], writes=[Bps], pe_accum=True)
                    fw.op("act", lambda e, ps=ps: e.activation(out=Vn[:], in_=ps[0:8, :], func=AF.Copy), reads=[Bps], writes=[BVn])
                    accb, Baccb = pbank[4 + s % 2], Bpb[4 + s % 2]

                    def s_emit_S(kb, s=s, sl=sl):
                        npart = 128 if kb < NKB else 8
                        s_ps, Bs_ps = pbank[2 + kb % 2], Bpb[2 + kb % 2]
                        for c in range(4):
                            if kb < NKB:
                                lhs = KTs[sl][:, c, kb * 128:(kb + 1) * 128]
                                rdl = [BKTs[sl], BQbd]
                            else:
                                lhs = KTn[:, c, s * 8:(s + 1) * 8]
                                rdl = [BKTn, BQbd]
                            fw.op("pe", lambda e, s_ps=s_ps, lhs=lhs, c=c, s=s, npart=npart: e.matmul(
                                s_ps[0:npart, c * 16:(c + 1) * 16], lhsT=lhs, rhs=Qbd[:, c, s, :],
                                start=True, stop=True, skip_group_check=True), reads=rdl, writes=[Bs_ps], pe_accum=True)

                    def s_emit_rest(kb, s=s, sl=sl, accb=accb, Baccb=Baccb):
                        npart = 128 if kb < NKB else 8
                        s_ps, Bs_ps = pbank[2 + kb % 2], Bpb[2 + kb % 2]
                        E_, BE_ = Es[kb % 2], BEs[kb % 2]
                        P_, BP_ = Ps[kb % 2], BPs[kb % 2]
                        fw.op("act", lambda e, s_ps=s_ps, E_=E_, npart=npart: e.activation(out=E_[0:npart, :], in_=s_ps[0:npart, 0:64], func=AF.Exp),
                              reads=[Bs_ps], writes=[BE_])
                        fw.op("dve", lambda e, E_=E_, P_=P_, kb=kb, npart=npart: e.tensor_tensor(out=P_[0:npart, :], in0=E_[0:npart, :], in1=Mc[0:npart, kb, :], op=ALU.mult),
                              reads=[BE_, BMc], writes=[BP_])
                        for cp in range(4):
                            if kb < NKB:
                                lhs = Vs[sl][:, kb, cp * 128:(cp + 1) * 128]
                                rdv = BVs[sl]
                            else:
                                lhs = Vn[0:8, cp * 128:(cp + 1) * 128]
                                rdv = BVn
                            fw.op("pe", lambda e, P_=P_, cp=cp, lhs=lhs, npart=npart, f=(kb == 0 and cp == 0): e.matmul(
                                accb[:, cp * 64:(cp + 1) * 64], lhsT=lhs, rhs=P_[0:npart, :], start=f, stop=False, skip_group_check=True),
                                reads=[BP_, rdv], writes=[Baccb], pe_accum=True)
                        fw.op("pe", lambda e, P_=P_, npart=npart: e.matmul(
                            accb[:, 256:320], lhsT=onesbb[0:npart, :], rhs=P_[0:npart, :], start=False, stop=False, skip_group_check=True),
                            reads=[BP_, Bonesbb], writes=[Baccb], pe_accum=True)

                    s_emit_S(0)
                    for kb in range(NKB + 1):
                        if kb + 1 < NKB + 1:
                            s_emit_S(kb + 1)
                        s_emit_rest(kb)
                        if kb % 2 == 1:
                            srnn_step()
                    if s + 2 < NSS:
                        load_cache(s + 2)
                    fw.op("dve", lambda e, accb=accb: e.reciprocal(out=rd[:], in_=accb[:, 256:320]), reads=[Baccb], writes=[Brd])
                    for hp in range(2):
                        fw.op("dve", lambda e, accb=accb, hp=hp, s=s: e.tensor_tensor(
                            out=attT[hp * 64:(hp + 1) * 64, :, s * 8:(s + 1) * 8],
                            in0=accb[hp * 64:(hp + 1) * 64, 0:320].rearrange("p (c x) -> p c x", x=80)[:, :, hp * 8:hp * 8 + 8],
                            in1=rd[hp * 64:(hp + 1) * 64, :].rearrange("p (c x) -> p c x", x=16)[:, :, hp * 8:hp * 8 + 8], op=ALU.mult),
                            reads=[Baccb, Brd], writes=BattT)
                while srnn_alive[0]:
                    srnn_step()
                grp_norm(attT, BattT, ATG, 0, 128, 0)
                fw.dma("sp", lambda e: e.dma_start(out=mixscr[:, :, NTP:NTP + 128].rearrange("k p t -> p k t"), in_=mixT[:, :, 0:128]),
                       BmixT, reads=[BmixT], writes=[Bmix])
                fw.barrier()
                fw.run()

        stop_here('sample')
        with ExitStack() as p2:
            TB = 3
            NTK = TB * 128
            w_out_bf = sbt(p2, "w_out_bf", [128, 8, D], BF16); Bwo = Buf("w_out_bf")
            w_mi_bf = sbt(p2, "w_mi_bf", [128, 8, DFF], BF16); Bwmi = [Buf("w_mi_bf%d" % q) for q in range(4)]
            w_mo_bf = sbt(p2, "w_mo_bf", [128, 32, D], BF16); Bwmo = [Buf("w_mo_bf%d" % q) for q in range(4)]
            fgB = sbt(p2, "fgB", [128, D], F32); BfgB = Buf("fgB")
            mix2 = sbt(p2, "mix2", [128, 8, NTK], BF16); Bmix2 = Buf("mix2")
            xmid = sbt(p2, "xmid", [128, TB, D], F32); Bxmid = [Buf("xmid%d" % i) for i in range(TB)]
            ss2 = sbt(p2, "ss2", [128, 2], F32); Bss2 = Buf("ss2")
            xn2 = sbt(p2, "xn2", [128, D], BF16); Bxn2 = Buf("xn2")
            n2T = sbt(p2, "n2T", [128, 8, NTK], BF16); Bn2T = Buf("n2T")
            hT = sbt(p2, "hT", [128, 32, NTK], BF16); BhT = Buf("hT")
            rl = sbt(p2, "rl", [128, NTK], F32); Brl = Buf("rl")
            yst = sbt(p2, "yst", [128, D], F32); Byst = Buf("yst")

            fw.dma("pool", lambda e: e.dma_start(out=w_out_bf[:], in_=w_out.rearrange("(kc p) n -> p kc n", p=128)), Bwo, writes=[Bwo])
            for q in range(4):
                fw.dma("pool", lambda e, q=q: e.dma_start(out=w_mi_bf[:, :, q * 1024:(q + 1) * 1024],
                                                          in_=w_mi.rearrange("(kc p) n -> p kc n", p=128)[:, :, q * 1024:(q + 1) * 1024]), Bwmi[q], writes=[Bwmi[q]])
            for q in range(4):
                fw.dma("pool", lambda e, q=q: e.dma_start(out=w_mo_bf[:, q * 8:(q + 1) * 8, :],
                                                          in_=w_mo.rearrange("(kc p) n -> p kc n", p=128)[:, q * 8:(q + 1) * 8, :]), Bwmo[q], writes=[Bwmo[q]])
            fw.dma("sp", lambda e: e.dma_start(out=fgB[:], in_=fg.partition_broadcast(128)), BfgB, writes=[BfgB])

            NBLK = NTOK // 128

            def xsrc_of(g):
                return xp[g * 128:(g + 1) * 128, :] if g < NTP // 128 else xs[:, :]

            def ydst_of(g):
                return yp[g * 128:(g + 1) * 128, :] if g < NTP // 128 else ys[:, :]

            def load_tile_inputs(i):
                fw.dma("sp", lambda e, i=i: e.dma_start(out=mix2[:], in_=mixscr[:, :, i * NTK:(i + 1) * NTK].rearrange("k p t -> p k t")),
                       Bmix2, reads=[Bmix], writes=[Bmix2])

            def load_x(i, j):
                g = i * TB + j
                fw.dma("sp", lambda e, g=g, j=j: e.dma_start(out=xmid[:, j, :], in_=xsrc_of(g)), Bxmid[j], writes=[Bxmid[j]])

            ntiles = NBLK // TB
            load_tile_inputs(0)
            for j in range(TB):
                load_x(0, j)
            for i in range(ntiles):
                for j in range(TB):
                    for hf in range(2):
                        ps, Bps = pbank[hf], Bpb[hf]
                        for kc in range(8):
                            fw.op("pe", lambda e, ps=ps, kc=kc, j=j, hf=hf: e.matmul(
                                ps[:], lhsT=mix2[:, kc, j * 128:(j + 1) * 128], rhs=w_out_bf[:, kc, hf * 512:(hf + 1) * 512],
                                start=(kc == 0), stop=(kc == 7)), reads=[Bmix2, Bwo], writes=[Bps], pe_accum=True)
                        fw.op("dve", lambda e, ps=ps, j=j, hf=hf: e.tensor_tensor(out=xmid[:, j, hf * 512:(hf + 1) * 512], in0=ps[:],
                                                                                  in1=xmid[:, j, hf * 512:(hf + 1) * 512], op=ALU.add),
                              reads=[Bps, Bxmid[j]], writes=[Bxmid[j]])
                if i + 1 < ntiles:
                    load_tile_inputs(i + 1)
                for j in range(TB):
                    rms_rstd(xmid[:, j, :], Bxmid[j], xn2[:], Bxn2, ss2[:], Bss2, D)
                    fw.op("dve", lambda e, j=j: e.tensor_scalar(out=xn2[:], in0=xmid[:, j, :], scalar1=ss2[:, 0:1], scalar2=None, op0=ALU.mult),
                          reads=[Bxmid[j], Bss2], writes=[Bxn2])
                    for kc in range(8):
                        fw.op("pe", lambda e, kc=kc: e.transpose(out=ptp[:, kc * 128:(kc + 1) * 128], in_=xn2[:, kc * 128:(kc + 1) * 128], identity=identb[:]),
                              reads=[Bxn2, Bidb], writes=[Bptp], pe_accum=True)
                    fw.op("dve", lambda e, j=j: e.tensor_tensor(
                        out=n2T[:, :, j * 128:(j + 1) * 128], in0=ptp[:].rearrange("p (k j) -> p k j", k=8),
                        in1=vec[:, G2:G2 + 8].unsqueeze(2).to_broadcast([128, 8, 128]), op=ALU.mult), reads=[Bptp, Bvec], writes=[Bn2T])
                for fc in range(32):
                    ps, Bps = pbank[2 + fc % 2], Bpb[2 + fc % 2]
                    for kc in range(8):
                        fw.op("pe", lambda e, ps=ps, kc=kc, fc=fc: e.matmul(ps[:, 0:NTK], lhsT=w_mi_bf[:, kc, fc * 128:(fc + 1) * 128], rhs=n2T[:, kc, :],
                                                                            start=(kc == 0), stop=(kc == 7)), reads=[Bwmi[fc // 8], Bn2T], writes=[Bps], pe_accum=True)
                    fw.op("act", lambda e, ps=ps: e.activation(out=rl[:], in_=ps[:, 0:NTK], func=AF.Relu), reads=[Bps], writes=[Brl])
                    fw.op("dve", lambda e, fc=fc: e.tensor_tensor(out=hT[:, fc, :], in0=rl[:], in1=rl[:], op=ALU.mult), reads=[Brl], writes=[BhT])
                for j in range(TB):
                    g = i * TB + j
                    for hf in range(2):
                        ps, Bps = pbank[4 + hf], Bpb[4 + hf]
                        for fc in range(32):
                            fw.op("pe", lambda e, ps=ps, fc=fc, j=j, hf=hf: e.matmul(
                                ps[:], lhsT=hT[:, fc, j * 128:(j + 1) * 128], rhs=w_mo_bf[:, fc, hf * 512:(hf + 1) * 512],
                                start=(fc == 0), stop=(fc == 31)), reads=[BhT, Bwmo[fc // 8]], writes=[Bps], pe_accum=True)
                        fw.op("dve", lambda e, ps=ps, j=j, hf=hf: e.tensor_tensor(out=xmid[:, j, hf * 512:(hf + 1) * 512], in0=ps[:],
                                                                                  in1=xmid[:, j, hf * 512:(hf + 1) * 512], op=ALU.add),
                              reads=[Bps, Bxmid[j]], writes=[Bxmid[j]])
                    rms_rstd(xmid[:, j, :], Bxmid[j], xn2[:], Bxn2, ss2[:], Bss2, D)
                    fw.op("dve", lambda e, j=j: e.scalar_tensor_tensor(out=yst[:], in0=xmid[:, j, :], scalar=ss2[:, 0:1], in1=fgB[:],
                                                                      op0=ALU.mult, op1=ALU.mult), reads=[Bxmid[j], Bss2, BfgB], writes=[Byst])
                    fw.dma("sp", lambda e, g=g: e.dma_start(out=ydst_of(g), in_=yst[:]), Byst, reads=[Byst])
                    if i + 1 < ntiles:
                        load_x(i + 1, j)
            fw.barrier()
            fw.run()
    except _Stop:
        pass
    return nc


def _t5_bucket_np(dist):
    dist = np.asarray(dist, np.int32)
    d_f = np.maximum(dist, 16).astype(np.float32)
    large = 16 + (np.log(d_f / np.float32(16)) / np.float32(math.log(2048 / 16)) * np.float32(16)).astype(np.int32)
    large = np.minimum(large, 31)
    return np.where(dist < 16, dist, large)


def _consts():
    delta = np.arange(TBL) - 128
    valid = delta >= 0
    bucket = _t5_bucket_np(np.maximum(delta, 0))
    onehot = np.zeros((32, TBL), np.float32)
    onehot[bucket, np.arange(TBL)] = 1.0
    onehot[:, ~valid] = 0.0
    m = ((delta >= 0) & (delta <= 128)).astype(np.float32) \
        + ((delta >= 0) & (delta <= 512) & (delta % 4 == 0)).astype(np.float32) \
        + ((delta >= 0) & (delta <= 2048) & (delta % 16 == 0)).astype(np.float32)
    mult = np.broadcast_to(m[None, :], (8, TBL)).astype(np.float32).copy()
    return onehot, mult


_NC_CACHE = {}


def kernel(x_prompt, x_sample, cache_k, cache_v, state_conv, state_h, norm1_g, w_in, rel_bias,
           conv_w, conv_b, gate_a_w, gate_a_b, gate_x_w, gate_x_b, lru_lambda, att_out_g,
           rnn_out_g, w_out, norm2_g, w_mlp_in, w_mlp_out, final_g):
    f32 = np.float32
    x_prompt = np.asarray(x_prompt, f32)
    x_sample = np.asarray(x_sample, f32)
    cache_k = np.asarray(cache_k, f32)
    cache_v = np.asarray(cache_v, f32)
    state_conv = np.asarray(state_conv, f32)
    state_h = np.asarray(state_h, f32)

    def fm(v, n):
        return np.asarray(v, f32).reshape(n, 128).T

    vecs = np.zeros((128, NV), f32)
    vecs[:, 0:8] = fm(norm1_g[0], 8)
    vecs[:, 8:16] = fm(norm2_g[0], 8)
    vecs[:, 16:20] = fm(att_out_g[0], 4)
    vecs[:, 20:24] = fm(rnn_out_g[0], 4)
    cw = np.asarray(conv_w[0], f32)
    for c in range(4):
        for j in range(4):
            vecs[:, 24 + c * 4 + j] = cw[j, c * 128:(c + 1) * 128]
    vecs[:, 40:44] = fm(conv_b[0], 4)
    vecs[:, 44:48] = fm(np.asarray(gate_a_b[0], f32).reshape(512), 4)
    vecs[:, 48:52] = fm(np.asarray(gate_x_b[0], f32).reshape(512), 4)
    vecs[:, 52:56] = fm(lru_lambda[0], 4)
    onehot, mult = _consts()
    shared = dict(
        w_in=np.ascontiguousarray(np.asarray(w_in[0], f32)), w_out=np.ascontiguousarray(np.asarray(w_out[0], f32)),
        w_mi=np.ascontiguousarray(np.asarray(w_mlp_in[0], f32)), w_mo=np.ascontiguousarray(np.asarray(w_mlp_out[0], f32)),
        vecs=vecs, fg=np.asarray(final_g, f32), gaw=np.ascontiguousarray(np.asarray(gate_a_w[0], f32)),
        gxw=np.ascontiguousarray(np.asarray(gate_x_w[0], f32)), relb=np.ascontiguousarray(np.asarray(rel_bias, f32)),
        onehot=onehot, mult=mult, identb=np.eye(128).astype(ml_dtypes.bfloat16), identf=np.eye(128, dtype=f32),
        jmat=np.ascontiguousarray(np.eye(128)[::-1]).astype(ml_dtypes.bfloat16),
    )
    rows_sel = np.array([r for r in range(1536) if r % 16 < 8] + list(range(1536, SEQ)))
    assert len(rows_sel) == NKEEP
    sel0 = np.zeros((128, 128), f32)
    sel1 = np.zeros((128, 128), f32)
    for ik in range(128):
        if ik % 16 < 8:
            sel0[ik, (ik // 16) * 8 + ik % 16] = 1.0
            sel1[ik, (8 + ik // 16) * 8 + ik % 16] = 1.0
    shared["sel0"] = sel0.astype(ml_dtypes.bfloat16)
    shared["sel1"] = sel1.astype(ml_dtypes.bfloat16)
    in_maps = []
    for c in range(NCORES):
        ck = cache_k[0, c * NSS:(c + 1) * NSS][:, rows_sel]
        ckT = np.ascontiguousarray(ck.transpose(0, 2, 3, 1)).reshape(NSS, 4, 128, NKEEP)
        sc = state_conv[0, c * NSS:(c + 1) * NSS]
        sconvT = np.ascontiguousarray(sc.reshape(NSS, 3, 4, 128).transpose(3, 2, 0, 1))
        sh = state_h[0, c * NSS:(c + 1) * NSS]
        shT = np.ascontiguousarray(sh.reshape(NSS, 4, 128).transpose(2, 1, 0))
        m = dict(shared)
        m.update(
            xp=np.ascontiguousarray(x_prompt[c * NPS:(c + 1) * NPS].reshape(NTP, D)),
            xs=np.ascontiguousarray(x_sample[c * NSS:(c + 1) * NSS].reshape(NSS * TS, D)),
            ckT=ckT, cv=np.ascontiguousarray(cache_v[0, c * NSS:(c + 1) * NSS][:, rows_sel].reshape(NSS, NKEEP, 512)),
            sconvT=sconvT, shT=shT,
        )
        in_maps.append(m)
    if "nc" not in _NC_CACHE:
        _NC_CACHE["nc"] = build_program()
    nc = _NC_CACHE["nc"]
    res = run_bass_kernel_spmd(nc, in_maps, core_ids=list(range(NCORES)))
    R = res.results

    def cat(name, shape):
        return np.concatenate([np.asarray(r[name], f32).reshape(shape) for r in R], axis=0)

    y_prompt = cat("yp", (NPS, SEQ, D))
    y_sample = cat("ys", (NSS, TS, D))
    nk_p = cat("nkp", (NPS, SEQ, 8, 64))[None]
    nv_p = cat("nvp", (NPS, SEQ, 8, 64))[None]
    nc_p = cat("ncp", (NPS, 3, 512))[None]
    nh_p = cat("nhp", (NPS, 512))[None]
    nk_s = cat("nks", (NSS, TS, 8, 64))[None]
    nv_s = cat("nvs", (NSS, TS, 8, 64))[None]
    nc_s = cat("ncs", (NSS, 3, 512))[None]
    nh_s = cat("nhs", (NSS, 512))[None]
    return (y_prompt, y_sample, nk_p, nv_p, nc_p, nh_p, nk_s, nv_s, nc_s, nh_s)


if __name__ == "__main__":
    import time
    t0 = time.time()
    nc = build_program()
    print("built in", time.time() - t0, "n_instructions", nc.n_instructions())
```

```python
import math
from contextlib import ExitStack
import numpy as np
import ml_dtypes
import concourse.bass as bass
import concourse.mybir as mybir
from concourse.bass_utils import run_bass_kernel_spmd

F32 = mybir.dt.float32
BF16 = mybir.dt.bfloat16
AF = mybir.ActivationFunctionType
ALU = mybir.AluOpType

NCORES = 8
D = 1024
NIN = 2560
DFF = 4096
SEQ = 2048
NPS = 2
NSS = 16
TS = 8
NTP = NPS * SEQ
NTOK = NTP + NSS * TS
TBL = 2304
NKEEP = 1280
NKB = NKEEP // 128
EPS = 1e-6
NV = 56


class Buf:
    __slots__ = ("name", "writers", "readers", "dsem", "dcount", "multi")

    def __init__(self, name, multi=False):
        self.name = name
        self.writers = []
        self.readers = []
        self.dsem = None
        self.dcount = 0
        self.multi = multi


class Eng:
    def __init__(self, name):
        self.name = name
        self.thunks = []
        self.count = 0
        self.seen = {}


class FW:
    ENG_NAMES = ("pe", "act", "dve", "pool", "sp")

    def __init__(self, nc, stack):
        self.nc = nc
        self.stack = stack
        self.engs = {n: Eng(n) for n in self.ENG_NAMES}
        self.sems = {}
        for n in self.ENG_NAMES:
            self.sems[("eng", n)] = stack.enter_context(nc.semaphore("s_" + n))
        self.dma_bufs = []

    def _dsem(self, buf):
        if buf.dsem is None:
            key = ("dma", len(self.dma_bufs))
            self.sems[key] = self.stack.enter_context(self.nc.semaphore("d%d" % len(self.dma_bufs)))
            buf.dsem = key
            self.dma_bufs.append(buf)
        return buf.dsem

    def _deps(self, reads, writes, pe_accum=False):
        deps = {}

        def add(tok):
            k, v = tok
            if deps.get(k, 0) < v:
                deps[k] = v
        for b in reads:
            for t in b.writers:
                add(t)
        for b in writes:
            if b.multi:
                continue
            for t in b.writers:
                if pe_accum and t[0] == ("eng", "pe"):
                    continue
                add(t)
            for t in b.readers:
                add(t)
        return deps

    def _emit_waits(self, e, deps):
        E = self.engs[e]
        for k, v in deps.items():
            if E.seen.get(k, 0) >= v:
                continue
            E.seen[k] = v
            h = self.sems[k]
            E.thunks.append(lambda eng, h=h, v=v: eng.wait_ge(h, v))

    def _update(self, reads, writes, tok):
        for b in reads:
            b.readers.append(tok)
            if len(b.readers) > 64:
                mx = {}
                for k, v in b.readers:
                    if mx.get(k, 0) < v:
                        mx[k] = v
                b.readers = list(mx.items())
        for b in writes:
            if b.multi:
                b.writers.append(tok)
                if len(b.writers) > 64:
                    mx = {}
                    for k, v in b.writers:
                        if mx.get(k, 0) < v:
                            mx[k] = v
                    b.writers = list(mx.items())
            else:
                b.writers = [tok]
                b.readers = []

    def op(self, e, fn, reads=(), writes=(), pe_accum=False):
        E = self.engs[e]
        self._emit_waits(e, self._deps(reads, writes, pe_accum))
        sem = self.sems[("eng", e)]
        E.count += 1
        tok = (("eng", e), E.count)
        E.thunks.append(lambda eng, fn=fn, sem=sem: fn(eng).then_inc(sem, 1))
        self._update(reads, writes, tok)
        return tok

    def dma(self, q, fns, primary, reads=(), writes=()):
        if not isinstance(fns, (list, tuple)):
            fns = [fns]
        E = self.engs[q]
        self._emit_waits(q, self._deps(reads, writes))
        key = self._dsem(primary)
        sem = self.sems[key]
        for fn in fns:
            primary.dcount += 16
            E.thunks.append(lambda eng, fn=fn, sem=sem: fn(eng).then_inc(sem, 16))
        tok = (key, primary.dcount)
        self._update(reads, writes, tok)
        return tok

    def barrier(self):
        deps = {}
        for b in self.dma_bufs:
            if b.dcount:
                deps[b.dsem] = b.dcount
        for n in self.ENG_NAMES:
            if self.engs[n].count:
                deps[("eng", n)] = self.engs[n].count
        for n in self.ENG_NAMES:
            d = {k: v for k, v in deps.items() if k != ("eng", n)}
            self._emit_waits(n, d)

    def run(self):
        nc = self.nc
        with nc.Block() as block:
            @block.tensor
            def _(eng):
                for t in self.engs["pe"].thunks:
                    t(eng)

            @block.scalar
            def _(eng):
                for t in self.engs["act"].thunks:
                    t(eng)

            @block.vector
            def _(eng):
                for t in self.engs["dve"].thunks:
                    t(eng)

            @block.gpsimd
            def _(eng):
                for t in self.engs["pool"].thunks:
                    t(eng)

            @block.sync
            def _(eng):
                for t in self.engs["sp"].thunks:
                    t(eng)
        for n in self.ENG_NAMES:
            self.engs[n].thunks = []


class _Stop(Exception):
    pass


DBG = dict(stop=None, nseq=NPS, ntile=4, proj=True, kv=True, rnn=True, attn=True, sample=True, sattn=True, phase2=True)

def build_program():
    nc = bass.Bass("TRN2", target_bir_lowering=False)

    def din(name, shape, dt=F32):
        return nc.dram_tensor(name, shape, dt, kind="ExternalInput").ap()

    def dout(name, shape, dt=F32):
        return nc.dram_tensor(name, shape, dt, kind="ExternalOutput").ap()

    xp = din("xp", [NTP, D])
    xs = din("xs", [NSS * TS, D])
    ckT = din("ckT", [NSS, 4, 128, NKEEP])
    cv = din("cv", [NSS, NKEEP, 512])
    sel0_d = din("sel0", [128, 128], BF16)
    sel1_d = din("sel1", [128, 128], BF16)
    sconvT = din("sconvT", [128, 4, NSS, 3])
    shT = din("shT", [128, 4, NSS])
    w_in = din("w_in", [D, NIN])
    w_out = din("w_out", [D, D])
    w_mi = din("w_mi", [D, DFF])
    w_mo = din("w_mo", [DFF, D])
    vecs = din("vecs", [128, NV])
    fg = din("fg", [D])
    gaw = din("gaw", [8, 64, 64])
    gxw = din("gxw", [8, 64, 64])
    relb = din("relb", [32, 8])
    onehot = din("onehot", [32, TBL])
    mult = din("mult", [8, TBL])
    identb_d = din("identb", [128, 128], BF16)
    identf_d = din("identf", [128, 128])
    jmat_d = din("jmat", [128, 128], BF16)

    yp = dout("yp", [NTP, D])
    ys = dout("ys", [NSS * TS, D])
    nkp = dout("nkp", [NTP, 512])
    nvp = dout("nvp", [NTP, 512])
    ncp = dout("ncp", [NPS, 3, 512])
    nhp = dout("nhp", [NPS, 512])
    nks = dout("nks", [NSS * TS, 512])
    nvs = dout("nvs", [NSS * TS, 512])
    ncs = dout("ncs", [NSS, 3, 512])
    nhs = dout("nhs", [NSS, 512])

    mixscr = nc.dram_tensor("mixscr", [8, 128, NTOK], BF16, kind="Internal").ap()
    tblscr = nc.dram_tensor("tblscr", [8, TBL], BF16, kind="Internal").ap()
    Bmix = Buf("mixscr", multi=True)
    Btbl = Buf("tblscr", multi=True)

    try:
      with ExitStack() as top:
        fw = FW(nc, top)

        def stop_here(tag):
            if DBG.get('stop') == tag:
                fw.barrier()
                fw.run()
                raise _Stop()

        def sbt(st, name, shape, dt):
            return st.enter_context(nc.sbuf_tensor("sb_" + name, shape, dt))

        pbank = [top.enter_context(nc.psum_tensor("pb%d" % i, [128, 512], F32)) for i in range(7)]
        ptp = top.enter_context(nc.psum_tensor("ptp", [128, 1024], BF16))
        Bpb = [Buf("pb%d" % i) for i in range(7)]
        Bptp = Buf("ptp")

        identb = sbt(top, "identb", [128, 128], BF16); Bidb = Buf("identb")
        identf = sbt(top, "identf", [128, 128], F32); Bidf = Buf("identf")
        vec = sbt(top, "vec", [128, NV], F32); Bvec = Buf("vec")
        vec2 = sbt(top, "vec2", [128, 24], F32); Bvec2 = Buf("vec2")
        cst = sbt(top, "cst", [128, 4], F32); Bcst = Buf("cst")
        fw.op("pool", lambda e: e.memset(cst[:, 0:1], EPS), writes=[Bcst])
        fw.op("pool", lambda e: e.memset(cst[:, 1:2], math.log(0.5)), writes=[Bcst])
        fw.op("pool", lambda e: e.memset(cst[:, 2:3], 1.0), writes=[Bcst])
        fw.dma("sp", lambda e: e.dma_start(out=identb[:], in_=identb_d), Bidb, writes=[Bidb])
        fw.dma("sp", lambda e: e.dma_start(out=identf[:], in_=identf_d), Bidf, writes=[Bidf])
        fw.dma("sp", lambda e: e.dma_start(out=vec[:], in_=vecs), Bvec, writes=[Bvec])
        G1, G2, ATG, RNG, CW, CB, BA, BX, LAM = 0, 8, 16, 20, 24, 40, 44, 48, 52
        HBA, HBX, CC, CH = 0, 4, 8, 12
        fw.op("dve", lambda e: e.tensor_scalar(out=vec2[:, 0:8], in0=vec[:, BA:BA + 8], scalar1=0.5, scalar2=None, op0=ALU.mult),
              reads=[Bvec], writes=[Bvec2])
        fw.op("act", lambda e: e.activation(out=vec2[:, 16:20], in_=vec[:, LAM:LAM + 4], func=AF.Exp, scale=-1.0), reads=[Bvec], writes=[Bvec2])
        fw.op("act", lambda e: e.activation(out=vec2[:, 16:20], in_=vec2[:, 16:20], func=AF.Ln, scale=1.0, bias=cst[:, 2:3]), reads=[Bvec2, Bcst], writes=[Bvec2])
        fw.op("dve", lambda e: e.tensor_scalar(out=vec2[:, CC:CC + 4], in0=vec2[:, 16:20], scalar1=-8.0, scalar2=None, op0=ALU.mult),
              reads=[Bvec2], writes=[Bvec2])
        fw.op("dve", lambda e: e.tensor_scalar(out=vec2[:, CH:CH + 4], in0=vec2[:, 16:20], scalar1=-4.0, scalar2=None, op0=ALU.mult),
              reads=[Bvec2], writes=[Bvec2])

        def rms_rstd(eng_in_ap, Bin, junk, Bjunk, ss, Bss, n):
            fw.op("act", lambda e: e.activation(out=junk, in_=eng_in_ap, func=AF.Square, accum_out=ss[:, 0:1]),
                  reads=[Bin], writes=[Bjunk, Bss])
            npart = ss.shape[0]
            fw.op("act", lambda e: e.activation(out=ss[:, 1:2], in_=cst[0:npart, 2:3], func=AF.Copy), reads=[Bss, Bcst], writes=[Bss])
            fw.op("act", lambda e: e.activation(out=ss[:, 1:2], in_=ss[:, 0:1], func=AF.Ln, scale=1.0 / n, bias=cst[0:npart, 0:1]), reads=[Bss, Bcst], writes=[Bss])
            fw.op("act", lambda e: e.activation(out=ss[:, 0:1], in_=ss[:, 1:2], func=AF.Exp, scale=-0.5), reads=[Bss], writes=[Bss])

        with ExitStack() as p1:
            w_in_bf = sbt(p1, "w_in_bf", [128, 8, NIN], BF16); Bwin = [Buf("w_in_bf%d" % g) for g in range(5)]
            wa_bd = sbt(p1, "wa_bd", [128, 4, 128], BF16); Bwa = Buf("wa_bd")
            wx_bd = sbt(p1, "wx_bd", [128, 4, 128], BF16); Bwx = Buf("wx_bd")
            ones_f = sbt(p1, "ones_f", [128, 128], F32); Bones = Buf("ones_f")
            onesb = sbt(p1, "onesb", [128, 1], BF16); Bonesb = Buf("onesb")
            xblk = [sbt(p1, "xblk%d" % i, [128, D], F32) for i in range(2)]; Bxblk = [Buf("xblk%d" % i) for i in range(2)]
            ss = sbt(p1, "ss", [128, 2], F32); Bss = Buf("ss")
            xn = [sbt(p1, "xn%d" % i, [128, D], BF16) for i in range(2)]; Bxn = [Buf("xn%d" % i) for i in range(2)]
            nT = sbt(p1, "nT", [128, 8, 512], BF16); BnT = Buf("nT")
            mixT = sbt(p1, "mixT", [128, 8, 512], BF16); BmixT = Buf("mixT")
            kst = sbt(p1, "kst", [128, 512], F32); Bkst = Buf("kst")
            vst = sbt(p1, "vst", [128, 512], F32); Bvst = Buf("vst")
            xr = sbt(p1, "xr", [128, 4, 515], F32); Bxr = [Buf("xr%d" % c) for c in range(4)]
            gg2 = [sbt(p1, "gg%d" % i, [128, 512], F32) for i in range(2)]; Bgg2 = [Buf("gg%d" % i) for i in range(2)]
            xc2 = [sbt(p1, "xc%d" % i, [128, 512], F32) for i in range(2)]; Bxc2 = [Buf("xc%d" % i) for i in range(2)]
            xcb2 = [sbt(p1, "xcb%d" % i, [128, 512], BF16) for i in range(2)]; Bxcb2 = [Buf("xcb%d" % i) for i in range(2)]
            t_r2 = [sbt(p1, "t_r%d" % i, [128, 512], F32) for i in range(2)]; Btr2 = [Buf("t_r%d" % i) for i in range(2)]
            t_g2 = [sbt(p1, "t_g%d" % i, [128, 512], F32) for i in range(2)]; Btg2 = [Buf("t_g%d" % i) for i in range(2)]
            a_t2 = [sbt(p1, "a_t%d" % i, [128, 512], F32) for i in range(2)]; Bat2 = [Buf("a_t%d" % i) for i in range(2)]
            tmp2 = [sbt(p1, "tmp%d" % i, [128, 512], F32) for i in range(2)]; Btmp2 = [Buf("tmp%d" % i) for i in range(2)]
            hs = sbt(p1, "hs", [128, 512], F32); Bhs = Buf("hs")
            rnn = sbt(p1, "rnn", [128, 4, 512], F32); Brnn = [Buf("rnn%d" % c) for c in range(4)]
            sqb = sbt(p1, "sqb", [128, 512], F32); Bsqb = Buf("sqb")
            rstdb = sbt(p1, "rstdb", [128, 512], F32); Brstdb = Buf("rstdb")
            hstate = sbt(p1, "hstate", [128, 4], F32); Bhst = Buf("hstate")
            fin = sbt(p1, "fin", [128, 4, 64], F32); Bfin = Buf("fin")
            att = sbt(p1, "att", [128, 4, 512], BF16); Batt = [Buf("att%d" % i) for i in range(4)]
            attn = sbt(p1, "attn", [128, 512], BF16); Battn = Buf("attn")
            rec = sbt(p1, "rec", [128, 4], F32); Brec = Buf("rec")

            for g in range(5):
                fw.dma("pool", lambda e, g=g: e.dma_start(
                    out=w_in_bf[:, :, g * 512:(g + 1) * 512],
                    in_=w_in.rearrange("(kc p) n -> p kc n", p=128)[:, :, g * 512:(g + 1) * 512]), Bwin[g], writes=[Bwin[g]])
            fw.op("pool", lambda e: e.memset(wa_bd[:], 0.0), writes=[Bwa])
            fw.op("pool", lambda e: e.memset(wx_bd[:], 0.0), writes=[Bwx])
            fw.op("pool", lambda e: e.memset(ones_f[:], 1.0), writes=[Bones])
            fw.op("pool", lambda e: e.memset(onesb[:], 1.0), writes=[Bonesb])
            for (src, dst, Bd) in ((gaw, wa_bd, Bwa), (gxw, wx_bd, Bwx)):
                for hp in range(2):
                    fw.dma("pool", lambda e, src=src, dst=dst, hp=hp: e.dma_start(
                        out=dst[hp * 64:(hp + 1) * 64, :, hp * 64:(hp + 1) * 64],
                        in_=src.rearrange("(c two) i o -> two i c o", two=2)[hp]), Bd, writes=[Bd])

            def norm_transpose(xsrc_ap, slot, gcol, dstT, BdstT, col0, ntok=128):
                xb, Bx_ = xblk[slot], Bxblk[slot]
                if xsrc_ap is not None:
                    fw.dma("sp", lambda e: e.dma_start(out=xb[0:ntok, :], in_=xsrc_ap), Bx_, writes=[Bx_])
                rms_rstd(xb[0:ntok, :], Bx_, xn[slot][0:ntok, :], Bxn[slot], ss[0:ntok, :], Bss, D)
                fw.op("act", lambda e: e.activation(out=xn[slot][0:ntok, :], in_=xb[0:ntok, :], func=AF.Copy, scale=ss[0:ntok, 0:1]),
                      reads=[Bx_, Bss], writes=[Bxn[slot]])
                for kc in range(8):
                    fw.op("pe", lambda e, kc=kc: e.transpose(out=ptp[:, kc * 128:kc * 128 + ntok], in_=xn[slot][0:ntok, kc * 128:(kc + 1) * 128],
                                                             identity=identb[0:ntok, 0:ntok]),
                          reads=[Bxn[slot], Bidb], writes=[Bptp], pe_accum=True)
                fw.op("dve", lambda e: e.tensor_tensor(
                    out=dstT[:, :, col0:col0 + ntok], in0=ptp[:].rearrange("p (k j) -> p k j", k=8)[:, :, 0:ntok],
                    in1=vec[:, gcol:gcol + 8].unsqueeze(2).to_broadcast([128, 8, ntok]), op=ALU.mult),
                    reads=[Bptp, Bvec], writes=[BdstT])

            mmi = [0]

            def mm_bank():
                i = mmi[0] % 2
                mmi[0] += 1
                return pbank[i], Bpb[i]

            def proj_fm(wcol0, N, evac):
                ps, Bps = mm_bank()
                for kc in range(8):
                    fw.op("pe", lambda e, kc=kc, ps=ps: e.matmul(ps[:, 0:N], lhsT=w_in_bf[:, kc, wcol0:wcol0 + 128], rhs=nT[:, kc, 0:N],
                                                                 start=(kc == 0), stop=(kc == 7)),
                          reads=[Bwin[wcol0 // 512], BnT], writes=[Bps], pe_accum=True)
                evac(ps, Bps)

            def rnn_chunk(c, N, seg, xr_view, conv_views, scan_fn, last_tile_fin):
                xc, Bxc = xc2[c % 2], Bxc2[c % 2]
                t_g, Btg = t_g2[c % 2], Btg2[c % 2]
                a_t, Bat = a_t2[c % 2], Bat2[c % 2]
                tmp, Btmp = tmp2[c % 2], Btmp2[c % 2]
                gg, Bgg = gg2[c % 2], Bgg2[c % 2]
                xcb, Bxcb = xcb2[c % 2], Bxcb2[c % 2]
                t_r, Btr = t_r2[c % 2], Btr2[c % 2]

                def ev_xr(ps, Bps):
                    fw.op("act", lambda e: e.activation(out=xr_view, in_=ps[:, 0:N] if seg is None else ps[:, 0:N].rearrange("p (s t) -> p s t", t=TS),
                                                        func=AF.Copy), reads=[Bps], writes=[Bxr[c]])
                proj_fm(1536 + c * 128, N, ev_xr)
                yield
                xcv = xc[:, 0:N] if seg is None else xc[:, 0:N].rearrange("p (s t) -> p s t", t=TS)
                cw = CW + c * 4
                fw.op("dve", lambda e: e.tensor_scalar(out=xcv, in0=conv_views[3], scalar1=vec[:, cw + 3:cw + 4], scalar2=vec[:, CB + c:CB + c + 1],
                                                       op0=ALU.mult, op1=ALU.add), reads=[Bxr[c], Bvec], writes=[Bxc])
                for j in (2, 1, 0):
                    fw.op("dve", lambda e, j=j: e.scalar_tensor_tensor(out=xcv, in0=conv_views[j], scalar=vec[:, cw + j:cw + j + 1], in1=xcv,
                                                                      op0=ALU.mult, op1=ALU.add), reads=[Bxr[c], Bvec, Bxc], writes=[Bxc])
                yield
                fw.op("act", lambda e: e.activation(out=xcb[:, 0:N], in_=xc[:, 0:N], func=AF.Copy), reads=[Bxc], writes=[Bxcb])
                psr, Bpsr = mm_bank()
                fw.op("pe", lambda e: e.matmul(psr[:, 0:N], lhsT=wa_bd[:, c, :], rhs=xcb[:, 0:N], start=True, stop=True), reads=[Bwa, Bxcb], writes=[Bpsr])
                psg, Bpsg = mm_bank()
                fw.op("pe", lambda e: e.matmul(psg[:, 0:N], lhsT=wx_bd[:, c, :], rhs=xcb[:, 0:N], start=True, stop=True), reads=[Bwx, Bxcb], writes=[Bpsg])
                psq, Bpsq = pbank[6], Bpb[6]
                for kc in range(8):
                    fw.op("pe", lambda e, kc=kc: e.matmul(psq[:, 0:N], lhsT=w_in_bf[:, kc, 2048 + c * 128:2048 + (c + 1) * 128], rhs=nT[:, kc, 0:N],
                                                          start=(kc == 0), stop=(kc == 7)), reads=[Bwin[4], BnT], writes=[Bpsq], pe_accum=True)
                fw.op("act", lambda e: e.activation(out=t_r[:, 0:N], in_=psr[:, 0:N], func=AF.Tanh, scale=0.5, bias=vec2[:, HBA + c:HBA + c + 1]),
                      reads=[Bpsr, Bvec2], writes=[Btr])
                fw.op("act", lambda e: e.activation(out=t_g[:, 0:N], in_=psg[:, 0:N], func=AF.Tanh, scale=0.5, bias=vec2[:, HBX + c:HBX + c + 1]),
                      reads=[Bpsg, Bvec2], writes=[Btg])
                fw.op("act", lambda e: e.activation(out=gg[:, 0:N], in_=psq[:, 0:N], func=AF.Gelu_apprx_tanh), reads=[Bpsq], writes=[Bgg])
                yield
                fw.op("act", lambda e: e.activation(out=a_t[:, 0:N], in_=t_r[:, 0:N], func=AF.Exp, scale=vec2[:, CH + c:CH + c + 1], bias=vec2[:, CH + c:CH + c + 1]),
                      reads=[Btr, Bvec2], writes=[Bat])
                fw.op("act", lambda e: e.activation(out=tmp[:, 0:N], in_=t_r[:, 0:N], func=AF.Exp, scale=vec2[:, CC + c:CC + c + 1], bias=vec2[:, CC + c:CC + c + 1]),
                      reads=[Btr, Bvec2], writes=[Btmp])
                fw.op("act", lambda e: e.activation(out=tmp[:, 0:N], in_=tmp[:, 0:N], func=AF.Ln, scale=-1.0, bias=cst[:, 2:3]), reads=[Btmp, Bcst], writes=[Btmp])
                fw.op("act", lambda e: e.activation(out=tmp[:, 0:N], in_=tmp[:, 0:N], func=AF.Exp, scale=0.5, bias=cst[:, 1:2]), reads=[Btmp, Bcst], writes=[Btmp])
                yield
                fw.op("dve", lambda e: e.scalar_tensor_tensor(out=t_g[:, 0:N], in0=t_g[:, 0:N], scalar=1.0, in1=xc[:, 0:N], op0=ALU.add, op1=ALU.mult),
                      reads=[Btg, Bxc], writes=[Btg])
                fw.op("dve", lambda e: e.tensor_tensor(out=t_g[:, 0:N], in0=t_g[:, 0:N], in1=tmp[:, 0:N], op=ALU.mult), reads=[Btg, Btmp], writes=[Btg])
                yield
                scan_fn(c)
                fw.op("dve", lambda e: e.tensor_tensor(out=rnn[:, c, 0:N], in0=hs[:, 0:N], in1=gg[:, 0:N], op=ALU.mult), reads=[Bhs, Bgg], writes=[Brnn[c]])
                last_tile_fin(c)
                yield

            def rnn_norm(N, col0):
                grp_norm(rnn, Brnn, RNG, 4, N, col0)

            def grp_norm(src, Bsrc, gcol, dst_c0, N, col0):
                rnn, Brnn, RNG = src, Bsrc, gcol
                ps, Bps = pbank[6], Bpb[6]
                for c in range(4):
                    fw.op("act", lambda e, c=c: e.activation(out=sqb[:, 0:N], in_=rnn[:, c, 0:N], func=AF.Square), reads=[Brnn[c]], writes=[Bsqb])
                    fw.op("pe", lambda e, c=c: e.matmul(ps[:, 0:N], lhsT=ones_f[:], rhs=sqb[:, 0:N], start=(c == 0), stop=(c == 3)),
                          reads=[Bones, Bsqb], writes=[Bps], pe_accum=True)
                fw.op("act", lambda e: e.activation(out=rstdb[:, 0:N], in_=ps[:, 0:N], func=AF.Ln, scale=1.0 / 512, bias=cst[:, 0:1]), reads=[Bps, Bcst], writes=[Brstdb])
                fw.op("act", lambda e: e.activation(out=rstdb[:, 0:N], in_=rstdb[:, 0:N], func=AF.Exp, scale=-0.5), reads=[Brstdb], writes=[Brstdb])
                for c in range(4):
                    fw.op("dve", lambda e, c=c: e.scalar_tensor_tensor(out=mixT[:, dst_c0 + c, col0:col0 + N], in0=rnn[:, c, 0:N], scalar=vec[:, RNG + c:RNG + c + 1],
                                                                      in1=rstdb[:, 0:N], op0=ALU.mult, op1=ALU.mult),
                          reads=[Brnn[c], Bvec, Brstdb], writes=[BmixT])

            def att_norm_block(att_ap, Batt_, ntok, col0):
                rms_rstd(att_ap, Batt_, attn[0:ntok, :], Battn, ss[0:ntok, :], Bss, 512)
                fw.op("act", lambda e: e.activation(out=attn[0:ntok, :], in_=att_ap, func=AF.Copy, scale=ss[0:ntok, 0:1]), reads=[Batt_, Bss], writes=[Battn])
                for cc in range(4):
                    fw.op("pe", lambda e, cc=cc: e.transpose(out=ptp[:, cc * 128:cc * 128 + ntok], in_=attn[0:ntok, cc * 128:(cc + 1) * 128],
                                                             identity=identb[0:ntok, 0:ntok]), reads=[Battn, Bidb], writes=[Bptp], pe_accum=True)
                fw.op("dve", lambda e: e.tensor_tensor(
                    out=mixT[:, 0:4, col0:col0 + ntok], in0=ptp[:, 0:512].rearrange("p (k j) -> p k j", k=4)[:, :, 0:ntok],
                    in1=vec[:, ATG:ATG + 4].unsqueeze(2).to_broadcast([128, 4, ntok]), op=ALU.mult), reads=[Bptp, Bvec], writes=[BmixT])

            def fin_out(ncols, rows_conv, rows_h, conv_dst_fn, h_dst):
                ps, Bps = pbank[6], Bpb[6]
                for c in range(4):
                    fw.op("pe", lambda e, c=c: e.transpose(out=ps[0:ncols, c * 128:(c + 1) * 128], in_=fin[:, c, 0:ncols], identity=identf[:]),
                          reads=[Bfin, Bidf], writes=[Bps], pe_accum=True)
                fw.op("act", lambda e: e.activation(out=kst[0:ncols, :], in_=ps[0:ncols, :], func=AF.Copy), reads=[Bps], writes=[Bkst])
                fns = []
                for (r0, n, dst) in rows_conv:
                    fns.append(lambda e, r0=r0, n=n, dst=dst: e.dma_start(out=dst, in_=kst[r0:r0 + n, :]))
                fns.append(lambda e: e.dma_start(out=h_dst, in_=kst[rows_h[0]:rows_h[0] + rows_h[1], :]))
                fw.dma("sp", fns, Bkst, reads=[Bkst])

            with ExitStack() as pp:
                QT = sbt(pp, "QT", [128, 2, 4, 512], BF16); BQT = Buf("QT")
                fw.op("pool", lambda e: e.memset(QT[:], 0.0), writes=[BQT])
                KT = sbt(pp, "KT", [128, 4, SEQ], BF16); BKT = Buf("KT")
                Vaug = sbt(pp, "Vaug", [128, 16, 8, 66], BF16); BV = Buf("Vaug")
                Mbig = sbt(pp, "Mbig", [128, 8, SEQ], BF16); BM = Buf("Mbig")
                Eb = [sbt(pp, "Eb%d" % i, [128, 512], BF16) for i in range(2)]; BEb = [Buf("Eb%d" % i) for i in range(2)]
                PT = [sbt(pp, "PT%d" % i, [128, 512], BF16) for i in range(3)]; BPT = [Buf("PT%d" % i) for i in range(3)]
                jmat = sbt(pp, "jmat", [128, 128], BF16); Bjm = Buf("jmat")

                def gen_mask():
                    relb_sb, Brelb = tmp2[0][0:32, 0:8], Btmp2[0]
                    oh_sb, Boh = t_r2[0][0:32, :], Btr2[0]
                    mult_sb, Bmult = t_g2[0][0:8, :], Btg2[0]
                    tabf, Btabf = a_t2[0][0:8, :], Bat2[0]
                    tabb, Btabb = mixT[0:8].rearrange("p k t -> p (k t)")[:, 0:TBL], BmixT
                    fw.dma("sp", lambda e: e.dma_start(out=relb_sb, in_=relb), Brelb, writes=[Brelb])
                    fw.dma("sp", lambda e: e.dma_start(out=jmat[:], in_=jmat_d), Bjm, writes=[Bjm])
                    fw.op("pool", lambda e: e.memset(Vaug[:, :, :, 64:65], 1.0), writes=[BV])
                    for i0 in range(0, TBL, 512):
                        n = min(512, TBL - i0)
                        fw.dma("sp", lambda e, i0=i0, n=n: e.dma_start(out=oh_sb[:, 0:n], in_=onehot[:, i0:i0 + n]), Boh, writes=[Boh])
                        fw.dma("sp", lambda e, i0=i0, n=n: e.dma_start(out=mult_sb[:, 0:n], in_=mult[:, i0:i0 + n]), Bmult, writes=[Bmult])
                        ps, Bps = mm_bank()
                        fw.op("pe", lambda e, ps=ps, n=n: e.matmul(ps[0:8, 0:n], lhsT=relb_sb, rhs=oh_sb[:, 0:n], start=True, stop=True),
                              reads=[Brelb, Boh], writes=[Bps])
                        fw.op("act", lambda e, ps=ps, n=n: e.activation(out=tabf[:, 0:n], in_=ps[0:8, 0:n], func=AF.Exp), reads=[Bps], writes=[Btabf])
                        fw.op("dve", lambda e, i0=i0, n=n: e.tensor_tensor(out=tabb[:, i0:i0 + n], in0=tabf[:, 0:n], in1=mult_sb[:, 0:n], op=ALU.mult),
                              reads=[Btabf, Bmult], writes=[Btabb])
                        yield
                    fw.dma("sp", lambda e: e.dma_start(out=tblscr, in_=tabb), Btabb, reads=[Btabb], writes=[Btbl])
                    k = 0
                    for h in range(8):
                        for hf in range(4):
                            mrev, Bmrev = PT[k % 2], BPT[k % 2]
                            k += 1
                            fw.dma("sp", lambda e, h=h, hf=hf, mrev=mrev: e.dma_start(out=mrev[:], in_=bass.AP(tblscr.tensor, h * TBL + 1 + hf * 512, [[1, 128], [1, 512]])),
                                   Bmrev, reads=[Btbl], writes=[Bmrev])
                            ps, Bps = mm_bank()
                            fw.op("pe", lambda e, ps=ps, mrev=mrev: e.matmul(ps[:], lhsT=jmat[:], rhs=mrev[:], start=True, stop=True),
                                  reads=[Bjm, Bmrev], writes=[Bps])
                            fw.op("act", lambda e, ps=ps, h=h, hf=hf: e.activation(out=Mbig[:, h, hf * 512:(hf + 1) * 512], in_=ps[:], func=AF.Copy),
                                  reads=[Bps], writes=[BM])
                            yield
                mask_gen = gen_mask()

                pending_tail = []
                for b in range(DBG['nseq']):
                    fw.op("pool", lambda e: e.memset(xr[:, :, 0:3], 0.0), writes=Bxr)
                    fw.op("pool", lambda e: e.memset(hstate[:], 0.0), writes=[Bhst])
                    for T in range(DBG['ntile']):
                        g0 = b * SEQ + T * 512
                        def gen_norm(g0):
                            for blk in range(4):
                                pre = (blk < 2) and not (g0 == 0)
                                norm_transpose(None if pre else xp[g0 + blk * 128:g0 + (blk + 1) * 128, :], blk % 2, G1, nT, BnT, blk * 128)
                                yield
                            gn = g0 + 512
                            if gn < NTP:
                                for blk in range(2):
                                    fw.dma("sp", lambda e, gn=gn, blk=blk: e.dma_start(out=xblk[blk][:], in_=xp[gn + blk * 128:gn + (blk + 1) * 128, :]),
                                           Bxblk[blk], writes=[Bxblk[blk]])

                        def gen_head(b=b, T=T, g0=g0):
                            if g0 == 0:
                                yield from gen_norm(g0)
                            for c in range(4):
                                def ev_q(ps, Bps, c=c):
                                    for hp in range(2):
                                        fw.op("act", lambda e, hp=hp: e.activation(out=QT[hp * 64:(hp + 1) * 64, hp, c, :], in_=ps[hp * 64:(hp + 1) * 64, :],
                                                                                   func=AF.Copy, scale=0.125), reads=[Bps], writes=[BQT])
                                proj_fm(c * 128, 512, ev_q)
                                yield
                            for c in range(4):
                                def ev_k(ps, Bps, c=c, T=T):
                                    fw.op("dve", lambda e: e.tensor_copy(out=KT[:, c, T * 512:(T + 1) * 512], in_=ps[:]), reads=[Bps], writes=[BKT])
                                proj_fm(512 + c * 128, 512, ev_k)
                                yield

                        gh = gen_head()
                        if b == 0 and T == 0:
                            alive_h, alive_m = True, True
                            while alive_h or alive_m:
                                if alive_h:
                                    try:
                                        next(gh)
                                    except StopIteration:
                                        alive_h = False
                                for _ in range(2):
                                    if alive_m:
                                        try:
                                            next(mask_gen)
                                        except StopIteration:
                                            alive_m = False
                        else:
                            for _ in gh:
                                pass
                        for fn_ in pending_tail:
                            fn_()
                        del pending_tail[:]
                        stop_here('qk')
                        def gen_kv(T=T, g0=g0):
                            for blk in range(4):
                                r0 = g0 + blk * 128
                                ps, Bps = mm_bank()
                                for kc in range(8):
                                    fw.op("pe", lambda e, kc=kc, ps=ps, blk=blk: e.matmul(ps[:], lhsT=nT[:, kc, blk * 128:(blk + 1) * 128], rhs=w_in_bf[:, kc, 512:1024],
                                                                                           start=(kc == 0), stop=(kc == 7)), reads=[Bwin[1], BnT], writes=[Bps], pe_accum=True)
                                fw.op("dve", lambda e, ps=ps: e.tensor_copy(out=kst[:], in_=ps[:]), reads=[Bps], writes=[Bkst])
                                fw.dma("sp", lambda e, r0=r0: e.dma_start(out=nkp[r0:r0 + 128, :], in_=kst[:]), Bkst, reads=[Bkst])
                                yield
                                ps, Bps = mm_bank()
                                for kc in range(8):
                                    fw.op("pe", lambda e, kc=kc, ps=ps, blk=blk: e.matmul(ps[:], lhsT=nT[:, kc, blk * 128:(blk + 1) * 128], rhs=w_in_bf[:, kc, 1024:1536],
                                                                                           start=(kc == 0), stop=(kc == 7)), reads=[Bwin[2], BnT], writes=[Bps], pe_accum=True)
                                fw.op("act", lambda e, ps=ps: e.activation(out=vst[:], in_=ps[:], func=AF.Copy), reads=[Bps], writes=[Bvst])
                                if not DBG.get('novaug'):
                                    fw.op("dve", lambda e, blk=blk, T=T: e.tensor_copy(out=Vaug[:, T * 4 + blk, :, 0:64], in_=vst[:].rearrange("p (h d) -> p h d", h=8)),
                                          reads=[Bvst], writes=[BV])
                                fw.dma("sp", lambda e, r0=r0: e.dma_start(out=nvp[r0:r0 + 128, :], in_=vst[:]), Bvst, reads=[Bvst])
                                yield
                        stop_here('kv')
                        last = (T == 3)

                        def scan_p(c):
                            fw.op("dve", lambda e: e.tensor_tensor_scan(out=hs[:], data0=a_t2[c % 2][:], data1=t_g2[c % 2][:], initial=hstate[:, c:c + 1], op0=ALU.mult, op1=ALU.add),
                                  reads=[Bat2[c % 2], Btg2[c % 2], Bhst], writes=[Bhs])
                            fw.op("dve", lambda e: e.tensor_copy(out=hstate[:, c:c + 1], in_=hs[:, 511:512]), reads=[Bhs], writes=[Bhst])

                        def fin_p(c):
                            if last:
                                fw.op("dve", lambda e: e.tensor_copy(out=fin[:, c, 0:3], in_=xr[:, c, 512:515]), reads=[Bxr[c]], writes=[Bfin])
                                fw.op("dve", lambda e: e.tensor_copy(out=fin[:, c, 3:4], in_=hs[:, 511:512]), reads=[Bhs], writes=[Bfin])
                            else:
                                fw.op("dve", lambda e: e.tensor_copy(out=xr[:, c, 0:3], in_=xr[:, c, 512:515]), reads=[Bxr[c]], writes=[Bxr[c]])
                        def gen_rnn(b=b, last=last):
                            for c0 in (0, 2):
                                gens = [rnn_chunk(c, 512, None, xr[:, c, 3:515], [xr[:, c, j:j + 512] for j in range(4)], scan_p, fin_p)
                                        for c in (c0, c0 + 1)]
                                alive = [True, True]
                                for _ in range(2):
                                    next(gens[0])
                                    yield
                                while alive[0] or alive[1]:
                                    for gi in range(2):
                                        if alive[gi]:
                                            try:
                                                next(gens[gi])
                                            except StopIteration:
                                                alive[gi] = False
                                    yield
                            rnn_norm(512, 0)
                            yield
                            if last:
                                fin_out(4, [(0, 3, ncp[b])], (3, 1), None, nhp[b:b + 1, :])
                                yield

                        def gen_attn(T=T):
                            its = [(h, kb) for h in range(8) for kb in range(4 * T + 4)]

                            def bufs(i):
                                return (pbank[2 + i % 2], Bpb[2 + i % 2], Eb[i % 2], BEb[i % 2], PT[i % 3], BPT[i % 3])

                            def emit_S(i):
                                h, kb = its[i]
                                c, hp = h // 2, h % 2
                                c0 = max(0, 128 * kb - T * 512)
                                s_ps, Bs_ps = bufs(i)[0:2]
                                fw.op("pe", lambda e, s_ps=s_ps, c=c, hp=hp, kb=kb, c0=c0: e.matmul(
                                    s_ps[:, c0:512], lhsT=KT[:, c, kb * 128:(kb + 1) * 128], rhs=QT[:, hp, c, c0:512],
                                    start=True, stop=True), reads=[BKT, BQT], writes=[Bs_ps])

                            def emit_mid(i):
                                h, kb = its[i]
                                c0 = max(0, 128 * kb - T * 512)
                                s_ps, Bs_ps, E_, BE_, P_, BP_ = bufs(i)
                                fw.op("act", lambda e, s_ps=s_ps, E_=E_, c0=c0: e.activation(out=E_[:, c0:512], in_=s_ps[:, c0:512], func=AF.Exp),
                                      reads=[Bs_ps], writes=[BE_])
                                j0 = T * 512 + c0 - 128 * kb
                                fw.op("dve", lambda e, E_=E_, P_=P_, c0=c0, j0=j0, h=h: e.tensor_tensor(
                                    out=P_[:, c0:512], in0=E_[:, c0:512], in1=Mbig[:, h, j0:j0 + 512 - c0], op=ALU.mult), reads=[BE_, BM], writes=[BP_])

                            def emit_pv(i):
                                h, kb = its[i]
                                c0 = max(0, 128 * kb - T * 512)
                                s_ps, Bs_ps, E_, BE_, P_, BP_ = bufs(i)
                                acc, Bacc = pbank[4 + h % 2], Bpb[4 + h % 2]
                                first = (kb == 0)
                                for ii in range(c0 // 128, 4):
                                    fw.op("pe", lambda e, acc=acc, P_=P_, ii=ii, kb=kb, h=h, first=first: e.matmul(
                                        acc[:, ii * 65:(ii + 1) * 65], lhsT=P_[:, ii * 128:(ii + 1) * 128], rhs=Vaug[:, kb, h, 0:65],
                                        start=first, stop=False, skip_group_check=True), reads=[BP_, BV], writes=[Bacc], pe_accum=True)
                                    first = False
                                if kb == 4 * T + 3:
                                    accv = acc[:, 0:260].rearrange("p (i d) -> p i d", d=65)
                                    fw.op("dve", lambda e, accv=accv: e.reciprocal(out=rec[:].unsqueeze(2), in_=accv[:, :, 64:65]), reads=[Bacc], writes=[Brec])
                                    fw.op("dve", lambda e, accv=accv, h=h: e.tensor_tensor(
                                        out=att[:, :, h * 64:(h + 1) * 64], in0=accv[:, :, 0:64], in1=rec[:].unsqueeze(2).to_broadcast([128, 4, 64]), op=ALU.mult),
                                        reads=[Bacc, Brec], writes=Batt)

                            n_it = len(its)
                            emit_S(0)
                            for i in range(n_it + 1):
                                if i + 1 < n_it:
                                    emit_S(i + 1)
                                if i < n_it:
                                    emit_mid(i)
                                if i >= 1:
                                    emit_pv(i - 1)
                                yield

                        if DBG.get('interleave', 1):
                            def gen_side(g0=g0):
                                yield from gen_rnn()
                                if g0 + 512 < NTP:
                                    yield from gen_norm(g0 + 512)
                            gr = gen_side()
                            ga = gen_attn()
                            gk = gen_kv()

                            def step(g):
                                try:
                                    next(g)
                                    return True
                                except StopIteration:
                                    return False
                            if T == 0:
                                while step(gk):
                                    step(gr)
                            else:
                                per = -(-8 // (4 * T))
                                for _ in range(4 * T):
                                    step(ga)
                                    for _ in range(per):
                                        step(gk)
                                    step(gr)
                                while step(gk):
                                    pass
                            n_att = 8 * (4 * T + 4)
                            kstep = 1
                            astep = max(1, n_att // 30)
                            alive_a, alive_r = True, True
                            while alive_a or alive_r:
                                for _ in range(astep):
                                    if alive_a:
                                        try:
                                            next(ga)
                                        except StopIteration:
                                            alive_a = False
                                for _ in range(kstep if alive_a else 4):
                                    if alive_r:
                                        try:
                                            next(gr)
                                        except StopIteration:
                                            alive_r = False
                        else:
                            for _ in gen_kv():
                                pass
                            for _ in gen_rnn():
                                pass
                            for _ in gen_attn():
                                pass
                            if g0 + 512 < NTP:
                                for _ in gen_norm(g0 + 512):
                                    pass
                        stop_here('attn')

                        def tile_tail(g0=g0):
                            for i in range(4):
                                att_norm_block(att[:, i, :], Batt[i], 128, i * 128)
                            fw.dma("sp", lambda e, g0=g0: e.dma_start(out=mixscr[:, :, g0:g0 + 512].rearrange("k p t -> p k t"), in_=mixT[:]),
                                   BmixT, reads=[BmixT], writes=[Bmix])
                        pending_tail.append(tile_tail)
                for fn_ in pending_tail:
                    fn_()
                del pending_tail[:]
                fw.barrier()
                fw.run()

            stop_here('prompt')
            with ExitStack() as sp_:
                Qbd = sbt(sp_, "Qbd", [128, 4, NSS, 16], BF16); BQbd = Buf("Qbd")
                onesbb = sbt(sp_, "onesbb", [128, 128], BF16); Bonesbb = Buf("onesbb")
                attT = sbt(sp_, "attT", [128, 4, 128], F32); BattT = [Buf("attT%d" % c) for c in range(4)]
                rd = sbt(sp_, "rd", [128, 64], F32); Brd = Buf("rd")
                fw.op("pool", lambda e: e.memset(Qbd[:], 0.0), writes=[BQbd])
                fw.op("pool", lambda e: e.memset(onesbb[:], 1.0), writes=[Bonesbb])
                KTn = sbt(sp_, "KTn", [128, 4, 128], BF16); BKTn = Buf("KTn")
                KTs = [sbt(sp_, "KTs%d" % i, [128, 4, NKEEP], BF16) for i in range(2)]; BKTs = [Buf("KTs%d" % i) for i in range(2)]
                Vs = [sbt(sp_, "Vs%d" % i, [128, NKB, 512], BF16) for i in range(2)]; BVs = [Buf("Vs%d" % i) for i in range(2)]
                Vn = sbt(sp_, "Vn", [8, 512], BF16); BVn = Buf("Vn")
                Ms = sbt(sp_, "Ms", [128, 17, 64], BF16); BMs = Buf("Ms")
                Mc = sbt(sp_, "Mc", [128, NKB + 1, 64], BF16); BMc = Buf("Mc")
                sel = [sbt(sp_, "sel%d" % i, [128, 128], BF16) for i in range(2)]; Bsel = [Buf("sel%d" % i) for i in range(2)]
                fw.dma("sp", lambda e: e.dma_start(out=sel[0][:], in_=sel0_d), Bsel[0], writes=[Bsel[0]])
                fw.dma("sp", lambda e: e.dma_start(out=sel[1][:], in_=sel1_d), Bsel[1], writes=[Bsel[1]])
                mrevs = sbt(sp_, "mrevs", [128, 17, 64], BF16); Bmrevs = Buf("mrevs")
                jmat2 = sbt(sp_, "jmat2", [128, 128], BF16); Bjm2 = Buf("jmat2")
                Es = [sbt(sp_, "Es%d" % i, [128, 64], F32) for i in range(2)]; BEs = [Buf("Es%d" % i) for i in range(2)]
                Ps = [sbt(sp_, "Ps%d" % i, [128, 64], BF16) for i in range(2)]; BPs = [Buf("Ps%d" % i) for i in range(2)]
                atts = sbt(sp_, "atts", [8, 512], F32); Batts = Buf("atts")
                recs = sbt(sp_, "recs", [8, 8], F32); Brecs = Buf("recs")

                fw.dma("sp", lambda e: e.dma_start(out=jmat2[:], in_=jmat_d), Bjm2, writes=[Bjm2])
                fns = []
                for kb in range(17):
                    fns.append(lambda e, kb=kb: e.dma_start(out=mrevs[:, kb, :].rearrange("p (h t) -> p h t", t=TS),
                                                            in_=bass.AP(tblscr.tensor, 2049 - 128 * kb, [[1, 128], [TBL, 8], [1, TS]])))
                fw.dma("sp", fns, Bmrevs, reads=[Btbl], writes=[Bmrevs])
                mflat = mrevs[:].rearrange("p k x -> p (k x)")
                Mflat = Ms[:].rearrange("p k x -> p (k x)")
                for i0 in range(0, 17 * 64, 512):
                    n = min(512, 17 * 64 - i0)
                    ps, Bps = mm_bank()
                    fw.op("pe", lambda e, ps=ps, i0=i0, n=n: e.matmul(ps[:, 0:n], lhsT=jmat2[:], rhs=mflat[:, i0:i0 + n], start=True, stop=True),
                          reads=[Bjm2, Bmrevs], writes=[Bps])
                    fw.op("act", lambda e, ps=ps, i0=i0, n=n: e.activation(out=Mflat[:, i0:i0 + n], in_=ps[:, 0:n], func=AF.Copy), reads=[Bps], writes=[BMs])

                stop_here('smask')
                for m in range(6):
                    ps, Bps = mm_bank()
                    for t2 in range(2):
                        fw.op("pe", lambda e, ps=ps, m=m, t2=t2: e.matmul(ps[:, 0:64], lhsT=sel[t2][:], rhs=Ms[:, 2 * m + t2, :], start=(t2 == 0), stop=(t2 == 1)),
                              reads=[Bsel[t2], BMs], writes=[Bps], pe_accum=True)
                    fw.op("act", lambda e, ps=ps, m=m: e.activation(out=Mc[:, m, :], in_=ps[:, 0:64], func=AF.Copy), reads=[Bps], writes=[BMc])
                fw.op("dve", lambda e: e.tensor_copy(out=Mc[:, 6:11, :], in_=Ms[:, 12:17, :]), reads=[BMs], writes=[BMc])
                def load_cache(s):
                    sl = s % 2
                    fw.dma("pool", lambda e: e.dma_start(out=KTs[sl][:], in_=ckT[s].rearrange("c p t -> p c t")), BKTs[sl], writes=[BKTs[sl]])
                    fw.dma("pool", lambda e: e.dma_start(out=Vs[sl][:], in_=cv[s].rearrange("(kb p) f -> p kb f", p=128)), BVs[sl], writes=[BVs[sl]])
                load_cache(0)
                load_cache(1)

                g0 = NTP
                norm_transpose(xs[:, :], 0, G1, nT, BnT, 0)
                for c in range(4):
                    def ev_q(ps, Bps, c=c):
                        for hp in range(2):
                            fw.op("act", lambda e, hp=hp: e.activation(out=Qbd[hp * 64:(hp + 1) * 64, c, :, hp * 8:(hp + 1) * 8],
                                                                       in_=ps[hp * 64:(hp + 1) * 64, 0:128].rearrange("p (s t) -> p s t", t=TS),
                                                                       func=AF.Copy, scale=0.125), reads=[Bps], writes=[BQbd])
                    proj_fm(c * 128, 128, ev_q)
                for c in range(4):
                    def ev_k(ps, Bps, c=c):
                        fw.op("dve", lambda e: e.tensor_copy(out=KTn[:, c, :], in_=ps[:, 0:128]), reads=[Bps], writes=[BKTn])
                    proj_fm(512 + c * 128, 128, ev_k)
                for (w0, dst, stg, Bstg) in ((512, nks, kst, Bkst), (1024, nvs, vst, Bvst)):
                    ps, Bps = mm_bank()
                    for kc in range(8):
                        fw.op("pe", lambda e, kc=kc, ps=ps, w0=w0: e.matmul(ps[:], lhsT=nT[:, kc, 0:128], rhs=w_in_bf[:, kc, w0:w0 + 512],
                                                                           start=(kc == 0), stop=(kc == 7)), reads=[Bwin[w0 // 512], BnT], writes=[Bps], pe_accum=True)
                    fw.op("act", lambda e, ps=ps, stg=stg: e.activation(out=stg[:], in_=ps[:], func=AF.Copy), reads=[Bps], writes=[Bstg])
                    fw.dma("sp", lambda e, dst=dst, stg=stg: e.dma_start(out=dst, in_=stg[:]), Bstg, reads=[Bstg])
                stop_here('sproj')
                scs = sbt(sp_, "scs", [128, 4, NSS, 3], F32); Bscs = Buf("scs")
                fw.dma("sp", lambda e: e.dma_start(out=scs[:], in_=sconvT), Bscs, writes=[Bscs])
                for c in range(4):
                    fw.op("pool", lambda e, c=c: e.tensor_copy(out=xr[:, c, 0:176].rearrange("p (s j) -> p s j", j=11)[:, :, 0:3], in_=scs[:, c, :, :]),
                          reads=[Bscs], writes=[Bxr[c]])
                shs = sbt(sp_, "shs", [128, 4, NSS], F32); Bshs = Buf("shs")
                fw.dma("sp", lambda e: e.dma_start(out=shs[:], in_=shT), Bshs, writes=[Bshs])

                def scan_s(c):
                    for s in range(NSS):
                        fw.op("dve", lambda e, s=s: e.tensor_tensor_scan(out=hs[:, s * 8:(s + 1) * 8], data0=a_t2[c % 2][:, s * 8:(s + 1) * 8], data1=t_g2[c % 2][:, s * 8:(s + 1) * 8],
                                                                        initial=shs[:, c, s:s + 1], op0=ALU.mult, op1=ALU.add),
                              reads=[Bat2[c % 2], Btg2[c % 2], Bshs], writes=[Bhs])

                def fin_s(c):
                    xv = xr[:, c, 0:176].rearrange("p (s j) -> p s j", j=11)
                    fw.op("dve", lambda e: e.tensor_copy(out=fin[:, c, 0:48].rearrange("p (j s) -> p j s", s=NSS), in_=xv[:, :, 8:11].rearrange("p s j -> p j s")),
                          reads=[Bxr[c]], writes=[Bfin])
                    fw.op("dve", lambda e: e.tensor_copy(out=fin[:, c, 48:64], in_=hs[:, 0:128].rearrange("p (s t) -> p s t", t=TS)[:, :, 7]),
                          reads=[Bhs], writes=[Bfin])
                def gen_srnn():
                    for c in range(4):
                        xv = xr[:, c, 0:176].rearrange("p (s j) -> p s j", j=11)
                        yield from rnn_chunk(c, 128, True, xv[:, :, 3:11], [xv[:, :, j:j + 8] for j in range(4)], scan_s, fin_s)
                    rnn_norm(128, 0)
                    yield
                    fin_out(64, [(j * 16, 16, ncs[:, j, :]) for j in range(3)], (48, 16), None, nhs)
                    yield
                srnn = gen_srnn()
                srnn_alive = [True]

                def srnn_step():
                    if srnn_alive[0]:
                        try:
                            next(srnn)
                        except StopIteration:
                            srnn_alive[0] = False
                for s in range(NSS):
                    sl = s % 2
                    ps, Bps = pbank[0], Bpb[0]
                    for kc in range(8):
                        fw.op("pe", lambda e, kc=kc, ps=ps, s=s: e.matmul(ps[0:8, :], lhsT=nT[:, kc, s * 8:(s + 1) * 8], rhs=w_in_bf[:, kc, 1024:1536],
                                                                         start=(kc == 0), stop=(kc == 7)), reads=[Bwin[2], BnT], writes=[Bps], pe_accum=True)
                    fw.op("act", lambda e, ps=ps: e.activation(out=Vn[:], in_=ps[0:8, :], func=AF.Copy), reads=[Bps], writes=[BVn])
                    accb, Baccb = pbank[4 + s % 2], Bpb[4 + s % 2]

                    def s_emit_S(kb, s=s, sl=sl):
                        npart = 128 if kb < NKB else 8
                        s_ps, Bs_ps = pbank[2 + kb % 2], Bpb[2 + kb % 2]
                        for c in range(4):
                            if kb < NKB:
                                lhs = KTs[sl][:, c, kb * 128:(kb + 1) * 128]
                                rdl = [BKTs[sl], BQbd]
                            else:
                                lhs = KTn[:, c, s * 8:(s + 1) * 8]
                                rdl = [BKTn, BQbd]
                            fw.op("pe", lambda e, s_ps=s_ps, lhs=lhs, c=c, s=s, npart=npart: e.matmul(
                                s_ps[0:npart, c * 16:(c + 1) * 16], lhsT=lhs, rhs=Qbd[:, c, s, :],
                                start=True, stop=True, skip_group_check=True), reads=rdl, writes=[Bs_ps], pe_accum=True)

                    def s_emit_rest(kb, s=s, sl=sl, accb=accb, Baccb=Baccb):
                        npart = 128 if kb < NKB else 8
                        s_ps, Bs_ps = pbank[2 + kb % 2], Bpb[2 + kb % 2]
                        E_, BE_ = Es[kb % 2], BEs[kb % 2]
                        P_, BP_ = Ps[kb % 2], BPs[kb % 2]
                        fw.op("act", lambda e, s_ps=s_ps, E_=E_, npart=npart: e.activation(out=E_[0:npart, :], in_=s_ps[0:npart, 0:64], func=AF.Exp),
                              reads=[Bs_ps], writes=[BE_])
                        fw.op("dve", lambda e, E_=E_, P_=P_, kb=kb, npart=npart: e.tensor_tensor(out=P_[0:npart, :], in0=E_[0:npart, :], in1=Mc[0:npart, kb, :], op=ALU.mult),
                              reads=[BE_, BMc], writes=[BP_])
                        for cp in range(4):
                            if kb < NKB:
                                lhs = Vs[sl][:, kb, cp * 128:(cp + 1) * 128]
                                rdv = BVs[sl]
                            else:
                                lhs = Vn[0:8, cp * 128:(cp + 1) * 128]
                                rdv = BVn
                            fw.op("pe", lambda e, P_=P_, cp=cp, lhs=lhs, npart=npart, f=(kb == 0 and cp == 0): e.matmul(
                                accb[:, cp * 64:(cp + 1) * 64], lhsT=lhs, rhs=P_[0:npart, :], start=f, stop=False, skip_group_check=True),
                                reads=[BP_, rdv], writes=[Baccb], pe_accum=True)
                        fw.op("pe", lambda e, P_=P_, npart=npart: e.matmul(
                            accb[:, 256:320], lhsT=onesbb[0:npart, :], rhs=P_[0:npart, :], start=False, stop=False, skip_group_check=True),
                            reads=[BP_, Bonesbb], writes=[Baccb], pe_accum=True)

                    s_emit_S(0)
                    for kb in range(NKB + 1):
                        if kb + 1 < NKB + 1:
                            s_emit_S(kb + 1)
                        s_emit_rest(kb)
                        if kb % 2 == 1:
                            srnn_step()
                    if s + 2 < NSS:
                        load_cache(s + 2)
                    fw.op("dve", lambda e, accb=accb: e.reciprocal(out=rd[:], in_=accb[:, 256:320]), reads=[Baccb], writes=[Brd])
                    for hp in range(2):
                        fw.op("dve", lambda e, accb=accb, hp=hp, s=s: e.tensor_tensor(
                            out=attT[hp * 64:(hp + 1) * 64, :, s * 8:(s + 1) * 8],
                            in0=accb[hp * 64:(hp + 1) * 64, 0:320].rearrange("p (c x) -> p c x", x=80)[:, :, hp * 8:hp * 8 + 8],
                            in1=rd[hp * 64:(hp + 1) * 64, :].rearrange("p (c x) -> p c x", x=16)[:, :, hp * 8:hp * 8 + 8], op=ALU.mult),
                            reads=[Baccb, Brd], writes=BattT)
                while srnn_alive[0]:
                    srnn_step()
                grp_norm(attT, BattT, ATG, 0, 128, 0)
                fw.dma("sp", lambda e: e.dma_start(out=mixscr[:, :, NTP:NTP + 128].rearrange("k p t -> p k t"), in_=mixT[:, :, 0:128]),
                       BmixT, reads=[BmixT], writes=[Bmix])
                fw.barrier()
                fw.run()

        stop_here('sample')
        with ExitStack() as p2:
            TB = 3
            NTK = TB * 128
            w_out_bf = sbt(p2, "w_out_bf", [128, 8, D], BF16); Bwo = Buf("w_out_bf")
            w_mi_bf = sbt(p2, "w_mi_bf", [128, 8, DFF], BF16); Bwmi = [Buf("w_mi_bf%d" % q) for q in range(4)]
            w_mo_bf = sbt(p2, "w_mo_bf", [128, 32, D], BF16); Bwmo = [Buf("w_mo_bf%d" % q) for q in range(4)]
            fgB = sbt(p2, "fgB", [128, D], F32); BfgB = Buf("fgB")
            mix2 = sbt(p2, "mix2", [128, 8, NTK], BF16); Bmix2 = Buf("mix2")
            xmid = sbt(p2, "xmid", [128, TB, D], F32); Bxmid = [Buf("xmid%d" % i) for i in range(TB)]
            ss2 = sbt(p2, "ss2", [128, 2], F32); Bss2 = Buf("ss2")
            xn2 = sbt(p2, "xn2", [128, D], BF16); Bxn2 = Buf("xn2")
            n2T = sbt(p2, "n2T", [128, 8, NTK], BF16); Bn2T = Buf("n2T")
            hT = sbt(p2, "hT", [128, 32, NTK], BF16); BhT = Buf("hT")
            rl = sbt(p2, "rl", [128, NTK], F32); Brl = Buf("rl")
            yst = sbt(p2, "yst", [128, D], F32); Byst = Buf("yst")

            fw.dma("pool", lambda e: e.dma_start(out=w_out_bf[:], in_=w_out.rearrange("(kc p) n -> p kc n", p=128)), Bwo, writes=[Bwo])
            for q in range(4):
                fw.dma("pool", lambda e, q=q: e.dma_start(out=w_mi_bf[:, :, q * 1024:(q + 1) * 1024],
                                                          in_=w_mi.rearrange("(kc p) n -> p kc n", p=128)[:, :, q * 1024:(q + 1) * 1024]), Bwmi[q], writes=[Bwmi[q]])
            for q in range(4):
                fw.dma("pool", lambda e, q=q: e.dma_start(out=w_mo_bf[:, q * 8:(q + 1) * 8, :],
                                                          in_=w_mo.rearrange("(kc p) n -> p kc n", p=128)[:, q * 8:(q + 1) * 8, :]), Bwmo[q], writes=[Bwmo[q]])
            fw.dma("sp", lambda e: e.dma_start(out=fgB[:], in_=fg.partition_broadcast(128)), BfgB, writes=[BfgB])

            NBLK = NTOK // 128

            def xsrc_of(g):
                return xp[g * 128:(g + 1) * 128, :] if g < NTP // 128 else xs[:, :]

            def ydst_of(g):
                return yp[g * 128:(g + 1) * 128, :] if g < NTP // 128 else ys[:, :]

            def load_tile_inputs(i):
                fw.dma("sp", lambda e, i=i: e.dma_start(out=mix2[:], in_=mixscr[:, :, i * NTK:(i + 1) * NTK].rearrange("k p t -> p k t")),
                       Bmix2, reads=[Bmix], writes=[Bmix2])

            def load_x(i, j):
                g = i * TB + j
                fw.dma("sp", lambda e, g=g, j=j: e.dma_start(out=xmid[:, j, :], in_=xsrc_of(g)), Bxmid[j], writes=[Bxmid[j]])

            ntiles = NBLK // TB
            load_tile_inputs(0)
            for j in range(TB):
                load_x(0, j)
            for i in range(ntiles):
                for j in range(TB):
                    for hf in range(2):
                        ps, Bps = pbank[hf], Bpb[hf]
                        for kc in range(8):
                            fw.op("pe", lambda e, ps=ps, kc=kc, j=j, hf=hf: e.matmul(
                                ps[:], lhsT=mix2[:, kc, j * 128:(j + 1) * 128], rhs=w_out_bf[:, kc, hf * 512:(hf + 1) * 512],
                                start=(kc == 0), stop=(kc == 7)), reads=[Bmix2, Bwo], writes=[Bps], pe_accum=True)
                        fw.op("dve", lambda e, ps=ps, j=j, hf=hf: e.tensor_tensor(out=xmid[:, j, hf * 512:(hf + 1) * 512], in0=ps[:],
                                                                                  in1=xmid[:, j, hf * 512:(hf + 1) * 512], op=ALU.add),
                              reads=[Bps, Bxmid[j]], writes=[Bxmid[j]])
                if i + 1 < ntiles:
                    load_tile_inputs(i + 1)
                for j in range(TB):
                    rms_rstd(xmid[:, j, :], Bxmid[j], xn2[:], Bxn2, ss2[:], Bss2, D)
                    fw.op("dve", lambda e, j=j: e.tensor_scalar(out=xn2[:], in0=xmid[:, j, :], scalar1=ss2[:, 0:1], scalar2=None, op0=ALU.mult),
                          reads=[Bxmid[j], Bss2], writes=[Bxn2])
                    for kc in range(8):
                        fw.op("pe", lambda e, kc=kc: e.transpose(out=ptp[:, kc * 128:(kc + 1) * 128], in_=xn2[:, kc * 128:(kc + 1) * 128], identity=identb[:]),
                              reads=[Bxn2, Bidb], writes=[Bptp], pe_accum=True)
                    fw.op("dve", lambda e, j=j: e.tensor_tensor(
                        out=n2T[:, :, j * 128:(j + 1) * 128], in0=ptp[:].rearrange("p (k j) -> p k j", k=8),
                        in1=vec[:, G2:G2 + 8].unsqueeze(2).to_broadcast([128, 8, 128]), op=ALU.mult), reads=[Bptp, Bvec], writes=[Bn2T])
                for fc in range(32):
                    ps, Bps = pbank[2 + fc % 2], Bpb[2 + fc % 2]
                    for kc in range(8):
                        fw.op("pe", lambda e, ps=ps, kc=kc, fc=fc: e.matmul(ps[:, 0:NTK], lhsT=w_mi_bf[:, kc, fc * 128:(fc + 1) * 128], rhs=n2T[:, kc, :],
                                                                            start=(kc == 0), stop=(kc == 7)), reads=[Bwmi[fc // 8], Bn2T], writes=[Bps], pe_accum=True)
                    fw.op("act", lambda e, ps=ps: e.activation(out=rl[:], in_=ps[:, 0:NTK], func=AF.Relu), reads=[Bps], writes=[Brl])
                    fw.op("dve", lambda e, fc=fc: e.tensor_tensor(out=hT[:, fc, :], in0=rl[:], in1=rl[:], op=ALU.mult), reads=[Brl], writes=[BhT])
                for j in range(TB):
                    g = i * TB + j
                    for hf in range(2):
                        ps, Bps = pbank[4 + hf], Bpb[4 + hf]
                        for fc in range(32):
                            fw.op("pe", lambda e, ps=ps, fc=fc, j=j, hf=hf: e.matmul(
                                ps[:], lhsT=hT[:, fc, j * 128:(j + 1) * 128], rhs=w_mo_bf[:, fc, hf * 512:(hf + 1) * 512],
                                start=(fc == 0), stop=(fc == 31)), reads=[BhT, Bwmo[fc // 8]], writes=[Bps], pe_accum=True)
                        fw.op("dve", lambda e, ps=ps, j=j, hf=hf: e.tensor_tensor(out=xmid[:, j, hf * 512:(hf + 1) * 512], in0=ps[:],
                                                                                  in1=xmid[:, j, hf * 512:(hf + 1) * 512], op=ALU.add),
                              reads=[Bps, Bxmid[j]], writes=[Bxmid[j]])
                    rms_rstd(xmid[:, j, :], Bxmid[j], xn2[:], Bxn2, ss2[:], Bss2, D)
                    fw.op("dve", lambda e, j=j: e.scalar_tensor_tensor(out=yst[:], in0=xmid[:, j, :], scalar=ss2[:, 0:1], in1=fgB[:],
                                                                      op0=ALU.mult, op1=ALU.mult), reads=[Bxmid[j], Bss2, BfgB], writes=[Byst])
                    fw.dma("sp", lambda e, g=g: e.dma_start(out=ydst_of(g), in_=yst[:]), Byst, reads=[Byst])
                    if i + 1 < ntiles:
                        load_x(i + 1, j)
            fw.barrier()
            fw.run()
    except _Stop:
        pass
    return nc


def _t5_bucket_np(dist):
    dist = np.asarray(dist, np.int32)
    d_f = np.maximum(dist, 16).astype(np.float32)
    large = 16 + (np.log(d_f / np.float32(16)) / np.float32(math.log(2048 / 16)) * np.float32(16)).astype(np.int32)
    large = np.minimum(large, 31)
    return np.where(dist < 16, dist, large)


def _consts():
    delta = np.arange(TBL) - 128
    valid = delta >= 0
    bucket = _t5_bucket_np(np.maximum(delta, 0))
    onehot = np.zeros((32, TBL), np.float32)
    onehot[bucket, np.arange(TBL)] = 1.0
    onehot[:, ~valid] = 0.0
    m = ((delta >= 0) & (delta <= 128)).astype(np.float32) \
        + ((delta >= 0) & (delta <= 512) & (delta % 4 == 0)).astype(np.float32) \
        + ((delta >= 0) & (delta <= 2048) & (delta % 16 == 0)).astype(np.float32)
    mult = np.broadcast_to(m[None, :], (8, TBL)).astype(np.float32).copy()
    return onehot, mult


_NC_CACHE = {}


def kernel(x_prompt, x_sample, cache_k, cache_v, state_conv, state_h, norm1_g, w_in, rel_bias,
           conv_w, conv_b, gate_a_w, gate_a_b, gate_x_w, gate_x_b, lru_lambda, att_out_g,
           rnn_out_g, w_out, norm2_g, w_mlp_in, w_mlp_out, final_g):
    f32 = np.float32
    x_prompt = np.asarray(x_prompt, f32)
    x_sample = np.asarray(x_sample, f32)
    cache_k = np.asarray(cache_k, f32)
    cache_v = np.asarray(cache_v, f32)
    state_conv = np.asarray(state_conv, f32)
    state_h = np.asarray(state_h, f32)

    def fm(v, n):
        return np.asarray(v, f32).reshape(n, 128).T

    vecs = np.zeros((128, NV), f32)
    vecs[:, 0:8] = fm(norm1_g[0], 8)
    vecs[:, 8:16] = fm(norm2_g[0], 8)
    vecs[:, 16:20] = fm(att_out_g[0], 4)
    vecs[:, 20:24] = fm(rnn_out_g[0], 4)
    cw = np.asarray(conv_w[0], f32)
    for c in range(4):
        for j in range(4):
            vecs[:, 24 + c * 4 + j] = cw[j, c * 128:(c + 1) * 128]
    vecs[:, 40:44] = fm(conv_b[0], 4)
    vecs[:, 44:48] = fm(np.asarray(gate_a_b[0], f32).reshape(512), 4)
    vecs[:, 48:52] = fm(np.asarray(gate_x_b[0], f32).reshape(512), 4)
    vecs[:, 52:56] = fm(lru_lambda[0], 4)
    onehot, mult = _consts()
    shared = dict(
        w_in=np.ascontiguousarray(np.asarray(w_in[0], f32)), w_out=np.ascontiguousarray(np.asarray(w_out[0], f32)),
        w_mi=np.ascontiguousarray(np.asarray(w_mlp_in[0], f32)), w_mo=np.ascontiguousarray(np.asarray(w_mlp_out[0], f32)),
        vecs=vecs, fg=np.asarray(final_g, f32), gaw=np.ascontiguousarray(np.asarray(gate_a_w[0], f32)),
        gxw=np.ascontiguousarray(np.asarray(gate_x_w[0], f32)), relb=np.ascontiguousarray(np.asarray(rel_bias, f32)),
        onehot=onehot, mult=mult, identb=np.eye(128).astype(ml_dtypes.bfloat16), identf=np.eye(128, dtype=f32),
        jmat=np.ascontiguousarray(np.eye(128)[::-1]).astype(ml_dtypes.bfloat16),
    )
    rows_sel = np.array([r for r in range(1536) if r % 16 < 8] + list(range(1536, SEQ)))
    assert len(rows_sel) == NKEEP
    sel0 = np.zeros((128, 128), f32)
    sel1 = np.zeros((128, 128), f32)
    for ik in range(128):
        if ik % 16 < 8:
            sel0[ik, (ik // 16) * 8 + ik % 16] = 1.0
            sel1[ik, (8 + ik // 16) * 8 + ik % 16] = 1.0
    shared["sel0"] = sel0.astype(ml_dtypes.bfloat16)
    shared["sel1"] = sel1.astype(ml_dtypes.bfloat16)
    in_maps = []
    for c in range(NCORES):
        ck = cache_k[0, c * NSS:(c + 1) * NSS][:, rows_sel]
        ckT = np.ascontiguousarray(ck.transpose(0, 2, 3, 1)).reshape(NSS, 4, 128, NKEEP)
        sc = state_conv[0, c * NSS:(c + 1) * NSS]
        sconvT = np.ascontiguousarray(sc.reshape(NSS, 3, 4, 128).transpose(3, 2, 0, 1))
        sh = state_h[0, c * NSS:(c + 1) * NSS]
        shT = np.ascontiguousarray(sh.reshape(NSS, 4, 128).transpose(2, 1, 0))
        m = dict(shared)
        m.update(
            xp=np.ascontiguousarray(x_prompt[c * NPS:(c + 1) * NPS].reshape(NTP, D)),
            xs=np.ascontiguousarray(x_sample[c * NSS:(c + 1) * NSS].reshape(NSS * TS, D)),
            ckT=ckT, cv=np.ascontiguousarray(cache_v[0, c * NSS:(c + 1) * NSS][:, rows_sel].reshape(NSS, NKEEP, 512)),
            sconvT=sconvT, shT=shT,
        )
        in_maps.append(m)
    if "nc" not in _NC_CACHE:
        _NC_CACHE["nc"] = build_program()
    nc = _NC_CACHE["nc"]
    res = run_bass_kernel_spmd(nc, in_maps, core_ids=list(range(NCORES)))
    R = res.results

    def cat(name, shape):
        return np.concatenate([np.asarray(r[name], f32).reshape(shape) for r in R], axis=0)

    y_prompt = cat("yp", (NPS, SEQ, D))
    y_sample = cat("ys", (NSS, TS, D))
    nk_p = cat("nkp", (NPS, SEQ, 8, 64))[None]
    nv_p = cat("nvp", (NPS, SEQ, 8, 64))[None]
    nc_p = cat("ncp", (NPS, 3, 512))[None]
    nh_p = cat("nhp", (NPS, 512))[None]
    nk_s = cat("nks", (NSS, TS, 8, 64))[None]
    nv_s = cat("nvs", (NSS, TS, 8, 64))[None]
    nc_s = cat("ncs", (NSS, 3, 512))[None]
    nh_s = cat("nhs", (NSS, 512))[None]
    return (y_prompt, y_sample, nk_p, nv_p, nc_p, nh_p, nk_s, nv_s, nc_s, nh_s)


if __name__ == "__main__":
    import time
    t0 = time.time()
    nc = build_program()
    print("built in", time.time() - t0, "n_instructions", nc.n_instructions())
```

```python
import math
from contextlib import ExitStack
import numpy as np
import ml_dtypes
import concourse.bass as bass
import concourse.mybir as mybir
from concourse.bass_utils import run_bass_kernel_spmd

F32 = mybir.dt.float32
BF16 = mybir.dt.bfloat16
AF = mybir.ActivationFunctionType
ALU = mybir.AluOpType

NCORES = 8
D = 1024
NIN = 2560
DFF = 4096
SEQ = 2048
NPS = 2
NSS = 16
TS = 8
NTP = NPS * SEQ
NTOK = NTP + NSS * TS
TBL = 2304
NKEEP = 1280
NKB = NKEEP // 128
EPS = 1e-6
NV = 56


class Buf:
    __slots__ = ("name", "writers", "readers", "dsem", "dcount", "multi")

    def __init__(self, name, multi=False):
        self.name = name
        self.writers = []
        self.readers = []
        self.dsem = None
        self.dcount = 0
        self.multi = multi


class Eng:
    def __init__(self, name):
        self.name = name
        self.thunks = []
        self.count = 0
        self.seen = {}


class FW:
    ENG_NAMES = ("pe", "act", "dve", "pool", "sp")

    def __init__(self, nc, stack):
        self.nc = nc
        self.stack = stack
        self.engs = {n: Eng(n) for n in self.ENG_NAMES}
        self.sems = {}
        for n in self.ENG_NAMES:
            self.sems[("eng", n)] = stack.enter_context(nc.semaphore("s_" + n))
        self.dma_bufs = []

    def _dsem(self, buf):
        if buf.dsem is None:
            key = ("dma", len(self.dma_bufs))
            self.sems[key] = self.stack.enter_context(self.nc.semaphore("d%d" % len(self.dma_bufs)))
            buf.dsem = key
            self.dma_bufs.append(buf)
        return buf.dsem

    def _deps(self, reads, writes, pe_accum=False):
        deps = {}

        def add(tok):
            k, v = tok
            if deps.get(k, 0) < v:
                deps[k] = v
        for b in reads:
            for t in b.writers:
                add(t)
        for b in writes:
            if b.multi:
                continue
            for t in b.writers:
                if pe_accum and t[0] == ("eng", "pe"):
                    continue
                add(t)
            for t in b.readers:
                add(t)
        return deps

    def _emit_waits(self, e, deps):
        E = self.engs[e]
        for k, v in deps.items():
            if E.seen.get(k, 0) >= v:
                continue
            E.seen[k] = v
            h = self.sems[k]
            E.thunks.append(lambda eng, h=h, v=v: eng.wait_ge(h, v))

    def _update(self, reads, writes, tok):
        for b in reads:
            b.readers.append(tok)
            if len(b.readers) > 64:
                mx = {}
                for k, v in b.readers:
                    if mx.get(k, 0) < v:
                        mx[k] = v
                b.readers = list(mx.items())
        for b in writes:
            if b.multi:
                b.writers.append(tok)
                if len(b.writers) > 64:
                    mx = {}
                    for k, v in b.writers:
                        if mx.get(k, 0) < v:
                            mx[k] = v
                    b.writers = list(mx.items())
            else:
                b.writers = [tok]
                b.readers = []

    def op(self, e, fn, reads=(), writes=(), pe_accum=False):
        E = self.engs[e]
        self._emit_waits(e, self._deps(reads, writes, pe_accum))
        sem = self.sems[("eng", e)]
        E.count += 1
        tok = (("eng", e), E.count)
        E.thunks.append(lambda eng, fn=fn, sem=sem: fn(eng).then_inc(sem, 1))
        self._update(reads, writes, tok)
        return tok

    def dma(self, q, fns, primary, reads=(), writes=()):
        if not isinstance(fns, (list, tuple)):
            fns = [fns]
        E = self.engs[q]
        self._emit_waits(q, self._deps(reads, writes))
        key = self._dsem(primary)
        sem = self.sems[key]
        for fn in fns:
            primary.dcount += 16
            E.thunks.append(lambda eng, fn=fn, sem=sem: fn(eng).then_inc(sem, 16))
        tok = (key, primary.dcount)
        self._update(reads, writes, tok)
        return tok

    def barrier(self):
        deps = {}
        for b in self.dma_bufs:
            if b.dcount:
                deps[b.dsem] = b.dcount
        for n in self.ENG_NAMES:
            if self.engs[n].count:
                deps[("eng", n)] = self.engs[n].count
        for n in self.ENG_NAMES:
            d = {k: v for k, v in deps.items() if k != ("eng", n)}
            self._emit_waits(n, d)

    def run(self):
        nc = self.nc
        with nc.Block() as block:
            @block.tensor
            def _(eng):
                for t in self.engs["pe"].thunks:
                    t(eng)

            @block.scalar
            def _(eng):
                for t in self.engs["act"].thunks:
                    t(eng)

            @block.vector
            def _(eng):
                for t in self.engs["dve"].thunks:
                    t(eng)

            @block.gpsimd
            def _(eng):
                for t in self.engs["pool"].thunks:
                    t(eng)

            @block.sync
            def _(eng):
                for t in self.engs["sp"].thunks:
                    t(eng)
        for n in self.ENG_NAMES:
            self.engs[n].thunks = []


class _Stop(Exception):
    pass


DBG = dict(stop=None, nseq=NPS, ntile=4, proj=True, kv=True, rnn=True, attn=True, sample=True, sattn=True, phase2=True)

def build_program():
    nc = bass.Bass("TRN2", target_bir_lowering=False)

    def din(name, shape, dt=F32):
        return nc.dram_tensor(name, shape, dt, kind="ExternalInput").ap()

    def dout(name, shape, dt=F32):
        return nc.dram_tensor(name, shape, dt, kind="ExternalOutput").ap()

    xp = din("xp", [NTP, D])
    xs = din("xs", [NSS * TS, D])
    ckT = din("ckT", [NSS, 4, 128, NKEEP])
    cv = din("cv", [NSS, NKEEP, 512])
    sel0_d = din("sel0", [128, 128], BF16)
    sel1_d = din("sel1", [128, 128], BF16)
    sconvT = din("sconvT", [128, 4, NSS, 3])
    shT = din("shT", [128, 4, NSS])
    w_in = din("w_in", [D, NIN])
    w_out = din("w_out", [D, D])
    w_mi = din("w_mi", [D, DFF])
    w_mo = din("w_mo", [DFF, D])
    vecs = din("vecs", [128, NV])
    fg = din("fg", [D])
    gaw = din("gaw", [8, 64, 64])
    gxw = din("gxw", [8, 64, 64])
    relb = din("relb", [32, 8])
    onehot = din("onehot", [32, TBL])
    mult = din("mult", [8, TBL])
    identb_d = din("identb", [128, 128], BF16)
    identf_d = din("identf", [128, 128])
    jmat_d = din("jmat", [128, 128], BF16)

    yp = dout("yp", [NTP, D])
    ys = dout("ys", [NSS * TS, D])
    nkp = dout("nkp", [NTP, 512])
    nvp = dout("nvp", [NTP, 512])
    ncp = dout("ncp", [NPS, 3, 512])
    nhp = dout("nhp", [NPS, 512])
    nks = dout("nks", [NSS * TS, 512])
    nvs = dout("nvs", [NSS * TS, 512])
    ncs = dout("ncs", [NSS, 3, 512])
    nhs = dout("nhs", [NSS, 512])

    mixscr = nc.dram_tensor("mixscr", [8, 128, NTOK], BF16, kind="Internal").ap()
    tblscr = nc.dram_tensor("tblscr", [8, TBL], BF16, kind="Internal").ap()
    Bmix = Buf("mixscr", multi=True)
    Btbl = Buf("tblscr", multi=True)

    try:
      with ExitStack() as top:
        fw = FW(nc, top)

        def stop_here(tag):
            if DBG.get('stop') == tag:
                fw.barrier()
                fw.run()
                raise _Stop()

        def sbt(st, name, shape, dt):
            return st.enter_context(nc.sbuf_tensor("sb_" + name, shape, dt))

        pbank = [top.enter_context(nc.psum_tensor("pb%d" % i, [128, 512], F32)) for i in range(7)]
        ptp = top.enter_context(nc.psum_tensor("ptp", [128, 1024], BF16))
        Bpb = [Buf("pb%d" % i) for i in range(7)]
        Bptp = Buf("ptp")

        identb = sbt(top, "identb", [128, 128], BF16); Bidb = Buf("identb")
        identf = sbt(top, "identf", [128, 128], F32); Bidf = Buf("identf")
        vec = sbt(top, "vec", [128, NV], F32); Bvec = Buf("vec")
        vec2 = sbt(top, "vec2", [128, 24], F32); Bvec2 = Buf("vec2")
        cst = sbt(top, "cst", [128, 4], F32); Bcst = Buf("cst")
        fw.op("pool", lambda e: e.memset(cst[:, 0:1], EPS), writes=[Bcst])
        fw.op("pool", lambda e: e.memset(cst[:, 1:2], math.log(0.5)), writes=[Bcst])
        fw.op("pool", lambda e: e.memset(cst[:, 2:3], 1.0), writes=[Bcst])
        fw.dma("sp", lambda e: e.dma_start(out=identb[:], in_=identb_d), Bidb, writes=[Bidb])
        fw.dma("sp", lambda e: e.dma_start(out=identf[:], in_=identf_d), Bidf, writes=[Bidf])
        fw.dma("sp", lambda e: e.dma_start(out=vec[:], in_=vecs), Bvec, writes=[Bvec])
        G1, G2, ATG, RNG, CW, CB, BA, BX, LAM = 0, 8, 16, 20, 24, 40, 44, 48, 52
        HBA, HBX, CC, CH = 0, 4, 8, 12
        fw.op("dve", lambda e: e.tensor_scalar(out=vec2[:, 0:8], in0=vec[:, BA:BA + 8], scalar1=0.5, scalar2=None, op0=ALU.mult),
              reads=[Bvec], writes=[Bvec2])
        fw.op("act", lambda e: e.activation(out=vec2[:, 16:20], in_=vec[:, LAM:LAM + 4], func=AF.Exp, scale=-1.0), reads=[Bvec], writes=[Bvec2])
        fw.op("act", lambda e: e.activation(out=vec2[:, 16:20], in_=vec2[:, 16:20], func=AF.Ln, scale=1.0, bias=cst[:, 2:3]), reads=[Bvec2, Bcst], writes=[Bvec2])
        fw.op("dve", lambda e: e.tensor_scalar(out=vec2[:, CC:CC + 4], in0=vec2[:, 16:20], scalar1=-8.0, scalar2=None, op0=ALU.mult),
              reads=[Bvec2], writes=[Bvec2])
        fw.op("dve", lambda e: e.tensor_scalar(out=vec2[:, CH:CH + 4], in0=vec2[:, 16:20], scalar1=-4.0, scalar2=None, op0=ALU.mult),
              reads=[Bvec2], writes=[Bvec2])

        def rms_rstd(eng_in_ap, Bin, junk, Bjunk, ss, Bss, n):
            fw.op("act", lambda e: e.activation(out=junk, in_=eng_in_ap, func=AF.Square, accum_out=ss[:, 0:1]),
                  reads=[Bin], writes=[Bjunk, Bss])
            npart = ss.shape[0]
            fw.op("act", lambda e: e.activation(out=ss[:, 1:2], in_=cst[0:npart, 2:3], func=AF.Copy), reads=[Bss, Bcst], writes=[Bss])
            fw.op("act", lambda e: e.activation(out=ss[:, 1:2], in_=ss[:, 0:1], func=AF.Ln, scale=1.0 / n, bias=cst[0:npart, 0:1]), reads=[Bss, Bcst], writes=[Bss])
            fw.op("act", lambda e: e.activation(out=ss[:, 0:1], in_=ss[:, 1:2], func=AF.Exp, scale=-0.5), reads=[Bss], writes=[Bss])

        with ExitStack() as p1:
            w_in_bf = sbt(p1, "w_in_bf", [128, 8, NIN], BF16); Bwin = [Buf("w_in_bf%d" % g) for g in range(5)]
            wa_bd = sbt(p1, "wa_bd", [128, 4, 128], BF16); Bwa = Buf("wa_bd")
            wx_bd = sbt(p1, "wx_bd", [128, 4, 128], BF16); Bwx = Buf("wx_bd")
            ones_f = sbt(p1, "ones_f", [128, 128], F32); Bones = Buf("ones_f")
            onesb = sbt(p1, "onesb", [128, 1], BF16); Bonesb = Buf("onesb")
            xblk = [sbt(p1, "xblk%d" % i, [128, D], F32) for i in range(2)]; Bxblk = [Buf("xblk%d" % i) for i in range(2)]
            ss = sbt(p1, "ss", [128, 2], F32); Bss = Buf("ss")
            xn = [sbt(p1, "xn%d" % i, [128, D], BF16) for i in range(2)]; Bxn = [Buf("xn%d" % i) for i in range(2)]
            nT = sbt(p1, "nT", [128, 8, 512], BF16); BnT = Buf("nT")
            mixT = sbt(p1, "mixT", [128, 8, 512], BF16); BmixT = Buf("mixT")
            kst = sbt(p1, "kst", [128, 512], F32); Bkst = Buf("kst")
            vst = sbt(p1, "vst", [128, 512], F32); Bvst = Buf("vst")
            xr = sbt(p1, "xr", [128, 4, 515], F32); Bxr = [Buf("xr%d" % c) for c in range(4)]
            gg2 = [sbt(p1, "gg%d" % i, [128, 512], F32) for i in range(2)]; Bgg2 = [Buf("gg%d" % i) for i in range(2)]
            xc2 = [sbt(p1, "xc%d" % i, [128, 512], F32) for i in range(2)]; Bxc2 = [Buf("xc%d" % i) for i in range(2)]
            xcb2 = [sbt(p1, "xcb%d" % i, [128, 512], BF16) for i in range(2)]; Bxcb2 = [Buf("xcb%d" % i) for i in range(2)]
            t_r2 = [sbt(p1, "t_r%d" % i, [128, 512], F32) for i in range(2)]; Btr2 = [Buf("t_r%d" % i) for i in range(2)]
            t_g2 = [sbt(p1, "t_g%d" % i, [128, 512], F32) for i in range(2)]; Btg2 = [Buf("t_g%d" % i) for i in range(2)]
            a_t2 = [sbt(p1, "a_t%d" % i, [128, 512], F32) for i in range(2)]; Bat2 = [Buf("a_t%d" % i) for i in range(2)]
            tmp2 = [sbt(p1, "tmp%d" % i, [128, 512], F32) for i in range(2)]; Btmp2 = [Buf("tmp%d" % i) for i in range(2)]
            hs = sbt(p1, "hs", [128, 512], F32); Bhs = Buf("hs")
            rnn = sbt(p1, "rnn", [128, 4, 512], F32); Brnn = [Buf("rnn%d" % c) for c in range(4)]
            sqb = sbt(p1, "sqb", [128, 512], F32); Bsqb = Buf("sqb")
            rstdb = sbt(p1, "rstdb", [128, 512], F32); Brstdb = Buf("rstdb")
            hstate = sbt(p1, "hstate", [128, 4], F32); Bhst = Buf("hstate")
            fin = sbt(p1, "fin", [128, 4, 64], F32); Bfin = Buf("fin")
            att = sbt(p1, "att", [128, 4, 512], BF16); Batt = [Buf("att%d" % i) for i in range(4)]
            attn = sbt(p1, "attn", [128, 512], BF16); Battn = Buf("attn")
            rec = sbt(p1, "rec", [128, 4], F32); Brec = Buf("rec")

            for g in range(5):
                fw.dma("pool", lambda e, g=g: e.dma_start(
                    out=w_in_bf[:, :, g * 512:(g + 1) * 512],
                    in_=w_in.rearrange("(kc p) n -> p kc n", p=128)[:, :, g * 512:(g + 1) * 512]), Bwin[g], writes=[Bwin[g]])
            fw.op("pool", lambda e: e.memset(wa_bd[:], 0.0), writes=[Bwa])
            fw.op("pool", lambda e: e.memset(wx_bd[:], 0.0), writes=[Bwx])
            fw.op("pool", lambda e: e.memset(ones_f[:], 1.0), writes=[Bones])
            fw.op("pool", lambda e: e.memset(onesb[:], 1.0), writes=[Bonesb])
            for (src, dst, Bd) in ((gaw, wa_bd, Bwa), (gxw, wx_bd, Bwx)):
                for hp in range(2):
                    fw.dma("pool", lambda e, src=src, dst=dst, hp=hp: e.dma_start(
                        out=dst[hp * 64:(hp + 1) * 64, :, hp * 64:(hp + 1) * 64],
                        in_=src.rearrange("(c two) i o -> two i c o", two=2)[hp]), Bd, writes=[Bd])

            def norm_transpose(xsrc_ap, slot, gcol, dstT, BdstT, col0, ntok=128):
                xb, Bx_ = xblk[slot], Bxblk[slot]
                if xsrc_ap is not None:
                    fw.dma("sp", lambda e: e.dma_start(out=xb[0:ntok, :], in_=xsrc_ap), Bx_, writes=[Bx_])
                rms_rstd(xb[0:ntok, :], Bx_, xn[slot][0:ntok, :], Bxn[slot], ss[0:ntok, :], Bss, D)
                fw.op("act", lambda e: e.activation(out=xn[slot][0:ntok, :], in_=xb[0:ntok, :], func=AF.Copy, scale=ss[0:ntok, 0:1]),
                      reads=[Bx_, Bss], writes=[Bxn[slot]])
                for kc in range(8):
                    fw.op("pe", lambda e, kc=kc: e.transpose(out=ptp[:, kc * 128:kc * 128 + ntok], in_=xn[slot][0:ntok, kc * 128:(kc + 1) * 128],
                                                             identity=identb[0:ntok, 0:ntok]),
                          reads=[Bxn[slot], Bidb], writes=[Bptp], pe_accum=True)
                fw.op("dve", lambda e: e.tensor_tensor(
                    out=dstT[:, :, col0:col0 + ntok], in0=ptp[:].rearrange("p (k j) -> p k j", k=8)[:, :, 0:ntok],
                    in1=vec[:, gcol:gcol + 8].unsqueeze(2).to_broadcast([128, 8, ntok]), op=ALU.mult),
                    reads=[Bptp, Bvec], writes=[BdstT])

            mmi = [0]

            def mm_bank():
                i = mmi[0] % 2
                mmi[0] += 1
                return pbank[i], Bpb[i]

            def proj_fm(wcol0, N, evac):
                ps, Bps = mm_bank()
                for kc in range(8):
                    fw.op("pe", lambda e, kc=kc, ps=ps: e.matmul(ps[:, 0:N], lhsT=w_in_bf[:, kc, wcol0:wcol0 + 128], rhs=nT[:, kc, 0:N],
                                                                 start=(kc == 0), stop=(kc == 7)),
                          reads=[Bwin[wcol0 // 512], BnT], writes=[Bps], pe_accum=True)
                evac(ps, Bps)

            def rnn_chunk(c, N, seg, xr_view, conv_views, scan_fn, last_tile_fin):
                xc, Bxc = xc2[c % 2], Bxc2[c % 2]
                t_g, Btg = t_g2[c % 2], Btg2[c % 2]
                a_t, Bat = a_t2[c % 2], Bat2[c % 2]
                tmp, Btmp = tmp2[c % 2], Btmp2[c % 2]
                gg, Bgg = gg2[c % 2], Bgg2[c % 2]
                xcb, Bxcb = xcb2[c % 2], Bxcb2[c % 2]
                t_r, Btr = t_r2[c % 2], Btr2[c % 2]

                def ev_xr(ps, Bps):
                    fw.op("dve", lambda e: e.tensor_copy(out=xr_view, in_=ps[:, 0:N] if seg is None else ps[:, 0:N].rearrange("p (s t) -> p s t", t=TS)),
                          reads=[Bps], writes=[Bxr[c]])
                proj_fm(1536 + c * 128, N, ev_xr)
                yield
                xcv = xc[:, 0:N] if seg is None else xc[:, 0:N].rearrange("p (s t) -> p s t", t=TS)
                cw = CW + c * 4
                fw.op("dve", lambda e: e.tensor_scalar(out=xcv, in0=conv_views[3], scalar1=vec[:, cw + 3:cw + 4], scalar2=vec[:, CB + c:CB + c + 1],
                                                       op0=ALU.mult, op1=ALU.add), reads=[Bxr[c], Bvec], writes=[Bxc])
                for j in (2, 1, 0):
                    fw.op("dve", lambda e, j=j: e.scalar_tensor_tensor(out=xcv, in0=conv_views[j], scalar=vec[:, cw + j:cw + j + 1], in1=xcv,
                                                                      op0=ALU.mult, op1=ALU.add), reads=[Bxr[c], Bvec, Bxc], writes=[Bxc])
                yield
                fw.op("act", lambda e: e.activation(out=xcb[:, 0:N], in_=xc[:, 0:N], func=AF.Copy), reads=[Bxc], writes=[Bxcb])
                psr, Bpsr = mm_bank()
                fw.op("pe", lambda e: e.matmul(psr[:, 0:N], lhsT=wa_bd[:, c, :], rhs=xcb[:, 0:N], start=True, stop=True), reads=[Bwa, Bxcb], writes=[Bpsr])
                psg, Bpsg = mm_bank()
                fw.op("pe", lambda e: e.matmul(psg[:, 0:N], lhsT=wx_bd[:, c, :], rhs=xcb[:, 0:N], start=True, stop=True), reads=[Bwx, Bxcb], writes=[Bpsg])
                psq, Bpsq = pbank[6], Bpb[6]
                for kc in range(8):
                    fw.op("pe", lambda e, kc=kc: e.matmul(psq[:, 0:N], lhsT=w_in_bf[:, kc, 2048 + c * 128:2048 + (c + 1) * 128], rhs=nT[:, kc, 0:N],
                                                          start=(kc == 0), stop=(kc == 7)), reads=[Bwin[4], BnT], writes=[Bpsq], pe_accum=True)
                fw.op("act", lambda e: e.activation(out=t_r[:, 0:N], in_=psr[:, 0:N], func=AF.Tanh, scale=0.5, bias=vec2[:, HBA + c:HBA + c + 1]),
                      reads=[Bpsr, Bvec2], writes=[Btr])
                fw.op("act", lambda e: e.activation(out=t_g[:, 0:N], in_=psg[:, 0:N], func=AF.Tanh, scale=0.5, bias=vec2[:, HBX + c:HBX + c + 1]),
                      reads=[Bpsg, Bvec2], writes=[Btg])
                fw.op("act", lambda e: e.activation(out=gg[:, 0:N], in_=psq[:, 0:N], func=AF.Gelu_apprx_tanh), reads=[Bpsq], writes=[Bgg])
                yield
                fw.op("act", lambda e: e.activation(out=a_t[:, 0:N], in_=t_r[:, 0:N], func=AF.Exp, scale=vec2[:, CH + c:CH + c + 1], bias=vec2[:, CH + c:CH + c + 1]),
                      reads=[Btr, Bvec2], writes=[Bat])
                fw.op("act", lambda e: e.activation(out=tmp[:, 0:N], in_=t_r[:, 0:N], func=AF.Exp, scale=vec2[:, CC + c:CC + c + 1], bias=vec2[:, CC + c:CC + c + 1]),
                      reads=[Btr, Bvec2], writes=[Btmp])
                fw.op("act", lambda e: e.activation(out=tmp[:, 0:N], in_=tmp[:, 0:N], func=AF.Ln, scale=-1.0, bias=cst[:, 2:3]), reads=[Btmp, Bcst], writes=[Btmp])
                fw.op("act", lambda e: e.activation(out=tmp[:, 0:N], in_=tmp[:, 0:N], func=AF.Exp, scale=0.5, bias=cst[:, 1:2]), reads=[Btmp, Bcst], writes=[Btmp])
                yield
                fw.op("dve", lambda e: e.scalar_tensor_tensor(out=t_g[:, 0:N], in0=t_g[:, 0:N], scalar=1.0, in1=xc[:, 0:N], op0=ALU.add, op1=ALU.mult),
                      reads=[Btg, Bxc], writes=[Btg])
                fw.op("dve", lambda e: e.tensor_tensor(out=t_g[:, 0:N], in0=t_g[:, 0:N], in1=tmp[:, 0:N], op=ALU.mult), reads=[Btg, Btmp], writes=[Btg])
                yield
                scan_fn(c)
                fw.op("dve", lambda e: e.tensor_tensor(out=rnn[:, c, 0:N], in0=hs[:, 0:N], in1=gg[:, 0:N], op=ALU.mult), reads=[Bhs, Bgg], writes=[Brnn[c]])
                last_tile_fin(c)
                yield

            def rnn_norm(N, col0):
                grp_norm(rnn, Brnn, RNG, 4, N, col0)

            def grp_norm(src, Bsrc, gcol, dst_c0, N, col0):
                rnn, Brnn, RNG = src, Bsrc, gcol
                ps, Bps = pbank[6], Bpb[6]
                for c in range(4):
                    fw.op("act", lambda e, c=c: e.activation(out=sqb[:, 0:N], in_=rnn[:, c, 0:N], func=AF.Square), reads=[Brnn[c]], writes=[Bsqb])
                    fw.op("pe", lambda e, c=c: e.matmul(ps[:, 0:N], lhsT=ones_f[:], rhs=sqb[:, 0:N], start=(c == 0), stop=(c == 3)),
                          reads=[Bones, Bsqb], writes=[Bps], pe_accum=True)
                fw.op("act", lambda e: e.activation(out=rstdb[:, 0:N], in_=ps[:, 0:N], func=AF.Ln, scale=1.0 / 512, bias=cst[:, 0:1]), reads=[Bps, Bcst], writes=[Brstdb])
                fw.op("act", lambda e: e.activation(out=rstdb[:, 0:N], in_=rstdb[:, 0:N], func=AF.Exp, scale=-0.5), reads=[Brstdb], writes=[Brstdb])
                for c in range(4):
                    fw.op("dve", lambda e, c=c: e.scalar_tensor_tensor(out=mixT[:, dst_c0 + c, col0:col0 + N], in0=rnn[:, c, 0:N], scalar=vec[:, RNG + c:RNG + c + 1],
                                                                      in1=rstdb[:, 0:N], op0=ALU.mult, op1=ALU.mult),
                          reads=[Brnn[c], Bvec, Brstdb], writes=[BmixT])

            def att_norm_block(att_ap, Batt_, ntok, col0):
                rms_rstd(att_ap, Batt_, attn[0:ntok, :], Battn, ss[0:ntok, :], Bss, 512)
                fw.op("act", lambda e: e.activation(out=attn[0:ntok, :], in_=att_ap, func=AF.Copy, scale=ss[0:ntok, 0:1]), reads=[Batt_, Bss], writes=[Battn])
                for cc in range(4):
                    fw.op("pe", lambda e, cc=cc: e.transpose(out=ptp[:, cc * 128:cc * 128 + ntok], in_=attn[0:ntok, cc * 128:(cc + 1) * 128],
                                                             identity=identb[0:ntok, 0:ntok]), reads=[Battn, Bidb], writes=[Bptp], pe_accum=True)
                fw.op("dve", lambda e: e.tensor_tensor(
                    out=mixT[:, 0:4, col0:col0 + ntok], in0=ptp[:, 0:512].rearrange("p (k j) -> p k j", k=4)[:, :, 0:ntok],
                    in1=vec[:, ATG:ATG + 4].unsqueeze(2).to_broadcast([128, 4, ntok]), op=ALU.mult), reads=[Bptp, Bvec], writes=[BmixT])

            def fin_out(ncols, rows_conv, rows_h, conv_dst_fn, h_dst):
                ps, Bps = pbank[6], Bpb[6]
                for c in range(4):
                    fw.op("pe", lambda e, c=c: e.transpose(out=ps[0:ncols, c * 128:(c + 1) * 128], in_=fin[:, c, 0:ncols], identity=identf[:]),
                          reads=[Bfin, Bidf], writes=[Bps], pe_accum=True)
                fw.op("act", lambda e: e.activation(out=kst[0:ncols, :], in_=ps[0:ncols, :], func=AF.Copy), reads=[Bps], writes=[Bkst])
                fns = []
                for (r0, n, dst) in rows_conv:
                    fns.append(lambda e, r0=r0, n=n, dst=dst: e.dma_start(out=dst, in_=kst[r0:r0 + n, :]))
                fns.append(lambda e: e.dma_start(out=h_dst, in_=kst[rows_h[0]:rows_h[0] + rows_h[1], :]))
                fw.dma("sp", fns, Bkst, reads=[Bkst])

            with ExitStack() as pp:
                QT = sbt(pp, "QT", [128, 2, 4, 512], BF16); BQT = Buf("QT")
                fw.op("pool", lambda e: e.memset(QT[:], 0.0), writes=[BQT])
                KT = sbt(pp, "KT", [128, 4, SEQ], BF16); BKT = Buf("KT")
                Vaug = sbt(pp, "Vaug", [128, 16, 8, 66], BF16); BV = Buf("Vaug")
                Mbig = sbt(pp, "Mbig", [128, 8, SEQ], BF16); BM = Buf("Mbig")
                Eb = [sbt(pp, "Eb%d" % i, [128, 512], BF16) for i in range(2)]; BEb = [Buf("Eb%d" % i) for i in range(2)]
                PT = [sbt(pp, "PT%d" % i, [128, 512], BF16) for i in range(3)]; BPT = [Buf("PT%d" % i) for i in range(3)]
                jmat = sbt(pp, "jmat", [128, 128], BF16); Bjm = Buf("jmat")

                def gen_mask():
                    relb_sb, Brelb = tmp2[0][0:32, 0:8], Btmp2[0]
                    oh_sb, Boh = t_r2[0][0:32, :], Btr2[0]
                    mult_sb, Bmult = t_g2[0][0:8, :], Btg2[0]
                    tabf, Btabf = a_t2[0][0:8, :], Bat2[0]
                    tabb, Btabb = mixT[0:8].rearrange("p k t -> p (k t)")[:, 0:TBL], BmixT
                    fw.dma("sp", lambda e: e.dma_start(out=relb_sb, in_=relb), Brelb, writes=[Brelb])
                    fw.dma("sp", lambda e: e.dma_start(out=jmat[:], in_=jmat_d), Bjm, writes=[Bjm])
                    fw.op("pool", lambda e: e.memset(Vaug[:, :, :, 64:65], 1.0), writes=[BV])
                    for i0 in range(0, TBL, 512):
                        n = min(512, TBL - i0)
                        fw.dma("sp", lambda e, i0=i0, n=n: e.dma_start(out=oh_sb[:, 0:n], in_=onehot[:, i0:i0 + n]), Boh, writes=[Boh])
                        fw.dma("sp", lambda e, i0=i0, n=n: e.dma_start(out=mult_sb[:, 0:n], in_=mult[:, i0:i0 + n]), Bmult, writes=[Bmult])
                        ps, Bps = mm_bank()
                        fw.op("pe", lambda e, ps=ps, n=n: e.matmul(ps[0:8, 0:n], lhsT=relb_sb, rhs=oh_sb[:, 0:n], start=True, stop=True),
                              reads=[Brelb, Boh], writes=[Bps])
                        fw.op("act", lambda e, ps=ps, n=n: e.activation(out=tabf[:, 0:n], in_=ps[0:8, 0:n], func=AF.Exp), reads=[Bps], writes=[Btabf])
                        fw.op("dve", lambda e, i0=i0, n=n: e.tensor_tensor(out=tabb[:, i0:i0 + n], in0=tabf[:, 0:n], in1=mult_sb[:, 0:n], op=ALU.mult),
                              reads=[Btabf, Bmult], writes=[Btabb])
                        yield
                    fw.dma("sp", lambda e: e.dma_start(out=tblscr, in_=tabb), Btabb, reads=[Btabb], writes=[Btbl])
                    k = 0
                    for h in range(8):
                        for hf in range(4):
                            mrev, Bmrev = PT[k % 2], BPT[k % 2]
                            k += 1
                            fw.dma("sp", lambda e, h=h, hf=hf, mrev=mrev: e.dma_start(out=mrev[:], in_=bass.AP(tblscr.tensor, h * TBL + 1 + hf * 512, [[1, 128], [1, 512]])),
                                   Bmrev, reads=[Btbl], writes=[Bmrev])
                            ps, Bps = mm_bank()
                            fw.op("pe", lambda e, ps=ps, mrev=mrev: e.matmul(ps[:], lhsT=jmat[:], rhs=mrev[:], start=True, stop=True),
                                  reads=[Bjm, Bmrev], writes=[Bps])
                            fw.op("act", lambda e, ps=ps, h=h, hf=hf: e.activation(out=Mbig[:, h, hf * 512:(hf + 1) * 512], in_=ps[:], func=AF.Copy),
                                  reads=[Bps], writes=[BM])
                            yield
                mask_gen = gen_mask()

                pending_tail = []
                for b in range(DBG['nseq']):
                    fw.op("pool", lambda e: e.memset(xr[:, :, 0:3], 0.0), writes=Bxr)
                    fw.op("pool", lambda e: e.memset(hstate[:], 0.0), writes=[Bhst])
                    for T in range(DBG['ntile']):
                        g0 = b * SEQ + T * 512
                        def gen_norm(g0):
                            for blk in range(4):
                                pre = (blk < 2) and not (g0 == 0)
                                norm_transpose(None if pre else xp[g0 + blk * 128:g0 + (blk + 1) * 128, :], blk % 2, G1, nT, BnT, blk * 128)
                                yield
                            gn = g0 + 512
                            if gn < NTP:
                                for blk in range(2):
                                    fw.dma("sp", lambda e, gn=gn, blk=blk: e.dma_start(out=xblk[blk][:], in_=xp[gn + blk * 128:gn + (blk + 1) * 128, :]),
                                           Bxblk[blk], writes=[Bxblk[blk]])

                        def gen_head(b=b, T=T, g0=g0):
                            if g0 == 0:
                                yield from gen_norm(g0)
                            for c in range(4):
                                def ev_q(ps, Bps, c=c):
                                    for hp in range(2):
                                        fw.op("dve", lambda e, hp=hp: e.tensor_scalar(out=QT[hp * 64:(hp + 1) * 64, hp, c, :], in0=ps[hp * 64:(hp + 1) * 64, :],
                                                                                      scalar1=0.125, scalar2=None, op0=ALU.mult), reads=[Bps], writes=[BQT])
                                proj_fm(c * 128, 512, ev_q)
                                yield
                            for c in range(4):
                                def ev_k(ps, Bps, c=c, T=T):
                                    fw.op("dve", lambda e: e.tensor_copy(out=KT[:, c, T * 512:(T + 1) * 512], in_=ps[:]), reads=[Bps], writes=[BKT])
                                proj_fm(512 + c * 128, 512, ev_k)
                                yield

                        gh = gen_head()
                        if b == 0 and T == 0:
                            alive_h, alive_m = True, True
                            while alive_h or alive_m:
                                if alive_h:
                                    try:
                                        next(gh)
                                    except StopIteration:
                                        alive_h = False
                                for _ in range(2):
                                    if alive_m:
                                        try:
                                            next(mask_gen)
                                        except StopIteration:
                                            alive_m = False
                        else:
                            for _ in gh:
                                pass
                        for fn_ in pending_tail:
                            fn_()
                        del pending_tail[:]
                        stop_here('qk')
                        def gen_kv(T=T, g0=g0):
                            for blk in range(4):
                                r0 = g0 + blk * 128
                                ps, Bps = mm_bank()
                                for kc in range(8):
                                    fw.op("pe", lambda e, kc=kc, ps=ps, blk=blk: e.matmul(ps[:], lhsT=nT[:, kc, blk * 128:(blk + 1) * 128], rhs=w_in_bf[:, kc, 512:1024],
                                                                                           start=(kc == 0), stop=(kc == 7)), reads=[Bwin[1], BnT], writes=[Bps], pe_accum=True)
                                fw.op("dve", lambda e, ps=ps: e.tensor_copy(out=kst[:], in_=ps[:]), reads=[Bps], writes=[Bkst])
                                fw.dma("sp", lambda e, r0=r0: e.dma_start(out=nkp[r0:r0 + 128, :], in_=kst[:]), Bkst, reads=[Bkst])
                                yield
                                ps, Bps = mm_bank()
                                for kc in range(8):
                                    fw.op("pe", lambda e, kc=kc, ps=ps, blk=blk: e.matmul(ps[:], lhsT=nT[:, kc, blk * 128:(blk + 1) * 128], rhs=w_in_bf[:, kc, 1024:1536],
                                                                                           start=(kc == 0), stop=(kc == 7)), reads=[Bwin[2], BnT], writes=[Bps], pe_accum=True)
                                fw.op("act", lambda e, ps=ps: e.activation(out=vst[:], in_=ps[:], func=AF.Copy), reads=[Bps], writes=[Bvst])
                                if not DBG.get('novaug'):
                                    fw.op("dve", lambda e, blk=blk, T=T: e.tensor_copy(out=Vaug[:, T * 4 + blk, :, 0:64], in_=vst[:].rearrange("p (h d) -> p h d", h=8)),
                                          reads=[Bvst], writes=[BV])
                                fw.dma("sp", lambda e, r0=r0: e.dma_start(out=nvp[r0:r0 + 128, :], in_=vst[:]), Bvst, reads=[Bvst])
                                yield
                        stop_here('kv')
                        last = (T == 3)

                        def scan_p(c):
                            fw.op("dve", lambda e: e.tensor_tensor_scan(out=hs[:], data0=a_t2[c % 2][:], data1=t_g2[c % 2][:], initial=hstate[:, c:c + 1], op0=ALU.mult, op1=ALU.add),
                                  reads=[Bat2[c % 2], Btg2[c % 2], Bhst], writes=[Bhs])
                            fw.op("dve", lambda e: e.tensor_copy(out=hstate[:, c:c + 1], in_=hs[:, 511:512]), reads=[Bhs], writes=[Bhst])

                        def fin_p(c):
                            if last:
                                fw.op("dve", lambda e: e.tensor_copy(out=fin[:, c, 0:3], in_=xr[:, c, 512:515]), reads=[Bxr[c]], writes=[Bfin])
                                fw.op("dve", lambda e: e.tensor_copy(out=fin[:, c, 3:4], in_=hs[:, 511:512]), reads=[Bhs], writes=[Bfin])
                            else:
                                fw.op("dve", lambda e: e.tensor_copy(out=xr[:, c, 0:3], in_=xr[:, c, 512:515]), reads=[Bxr[c]], writes=[Bxr[c]])
                        def gen_rnn(b=b, last=last):
                            for c0 in (0, 2):
                                gens = [rnn_chunk(c, 512, None, xr[:, c, 3:515], [xr[:, c, j:j + 512] for j in range(4)], scan_p, fin_p)
                                        for c in (c0, c0 + 1)]
                                alive = [True, True]
                                for _ in range(2):
                                    next(gens[0])
                                    yield
                                while alive[0] or alive[1]:
                                    for gi in range(2):
                                        if alive[gi]:
                                            try:
                                                next(gens[gi])
                                            except StopIteration:
                                                alive[gi] = False
                                    yield
                            rnn_norm(512, 0)
                            yield
                            if last:
                                fin_out(4, [(0, 3, ncp[b])], (3, 1), None, nhp[b:b + 1, :])
                                yield

                        def gen_attn(T=T):
                            its = [(h, kb) for h in range(8) for kb in range(4 * T + 4)]

                            def bufs(i):
                                return (pbank[2 + i % 2], Bpb[2 + i % 2], Eb[i % 2], BEb[i % 2], PT[i % 3], BPT[i % 3])

                            def emit_S(i):
                                h, kb = its[i]
                                c, hp = h // 2, h % 2
                                c0 = max(0, 128 * kb - T * 512)
                                s_ps, Bs_ps = bufs(i)[0:2]
                                fw.op("pe", lambda e, s_ps=s_ps, c=c, hp=hp, kb=kb, c0=c0: e.matmul(
                                    s_ps[:, c0:512], lhsT=KT[:, c, kb * 128:(kb + 1) * 128], rhs=QT[:, hp, c, c0:512],
                                    start=True, stop=True), reads=[BKT, BQT], writes=[Bs_ps])

                            def emit_mid(i):
                                h, kb = its[i]
                                c0 = max(0, 128 * kb - T * 512)
                                s_ps, Bs_ps, E_, BE_, P_, BP_ = bufs(i)
                                fw.op("act", lambda e, s_ps=s_ps, E_=E_, c0=c0: e.activation(out=E_[:, c0:512], in_=s_ps[:, c0:512], func=AF.Exp),
                                      reads=[Bs_ps], writes=[BE_])
                                j0 = T * 512 + c0 - 128 * kb
                                fw.op("dve", lambda e, E_=E_, P_=P_, c0=c0, j0=j0, h=h: e.tensor_tensor(
                                    out=P_[:, c0:512], in0=E_[:, c0:512], in1=Mbig[:, h, j0:j0 + 512 - c0], op=ALU.mult), reads=[BE_, BM], writes=[BP_])

                            def emit_pv(i):
                                h, kb = its[i]
                                c0 = max(0, 128 * kb - T * 512)
                                s_ps, Bs_ps, E_, BE_, P_, BP_ = bufs(i)
                                acc, Bacc = pbank[4 + h % 2], Bpb[4 + h % 2]
                                first = (kb == 0)
                                for ii in range(c0 // 128, 4):
                                    fw.op("pe", lambda e, acc=acc, P_=P_, ii=ii, kb=kb, h=h, first=first: e.matmul(
                                        acc[:, ii * 65:(ii + 1) * 65], lhsT=P_[:, ii * 128:(ii + 1) * 128], rhs=Vaug[:, kb, h, 0:65],
                                        start=first, stop=False, skip_group_check=True), reads=[BP_, BV], writes=[Bacc], pe_accum=True)
                                    first = False
                                if kb == 4 * T + 3:
                                    accv = acc[:, 0:260].rearrange("p (i d) -> p i d", d=65)
                                    fw.op("dve", lambda e, accv=accv: e.reciprocal(out=rec[:].unsqueeze(2), in_=accv[:, :, 64:65]), reads=[Bacc], writes=[Brec])
                                    fw.op("dve", lambda e, accv=accv, h=h: e.tensor_tensor(
                                        out=att[:, :, h * 64:(h + 1) * 64], in0=accv[:, :, 0:64], in1=rec[:].unsqueeze(2).to_broadcast([128, 4, 64]), op=ALU.mult),
                                        reads=[Bacc, Brec], writes=Batt)

                            n_it = len(its)
                            emit_S(0)
                            for i in range(n_it + 1):
                                if i + 1 < n_it:
                                    emit_S(i + 1)
                                if i < n_it:
                                    emit_mid(i)
                                if i >= 1:
                                    emit_pv(i - 1)
                                yield

                        if DBG.get('interleave', 1):
                            def gen_side(g0=g0):
                                yield from gen_rnn()
                                if g0 + 512 < NTP:
                                    yield from gen_norm(g0 + 512)
                            gr = gen_side()
                            ga = gen_attn()
                            gk = gen_kv()

                            def step(g):
                                try:
                                    next(g)
                                    return True
                                except StopIteration:
                                    return False
                            if T == 0:
                                while step(gk):
                                    step(gr)
                            else:
                                per = -(-8 // (4 * T))
                                for _ in range(4 * T):
                                    step(ga)
                                    for _ in range(per):
                                        step(gk)
                                    step(gr)
                                while step(gk):
                                    pass
                            n_att = 8 * (4 * T + 4)
                            kstep = 1
                            astep = max(1, n_att // 30)
                            alive_a, alive_r = True, True
                            while alive_a or alive_r:
                                for _ in range(astep):
                                    if alive_a:
                                        try:
                                            next(ga)
                                        except StopIteration:
                                            alive_a = False
                                for _ in range(kstep if alive_a else 4):
                                    if alive_r:
                                        try:
                                            next(gr)
                                        except StopIteration:
                                            alive_r = False
                        else:
                            for _ in gen_kv():
                                pass
                            for _ in gen_rnn():
                                pass
                            for _ in gen_attn():
                                pass
                            if g0 + 512 < NTP:
                                for _ in gen_norm(g0 + 512):
                                    pass
                        stop_here('attn')

                        def tile_tail(g0=g0):
                            for i in range(4):
                                att_norm_block(att[:, i, :], Batt[i], 128, i * 128)
                            fw.dma("sp", lambda e, g0=g0: e.dma_start(out=mixscr[:, :, g0:g0 + 512].rearrange("k p t -> p k t"), in_=mixT[:]),
                                   BmixT, reads=[BmixT], writes=[Bmix])
                        pending_tail.append(tile_tail)
                for fn_ in pending_tail:
                    fn_()
                del pending_tail[:]
                fw.barrier()
                fw.run()

            stop_here('prompt')
            with ExitStack() as sp_:
                Qbd = sbt(sp_, "Qbd", [128, 4, NSS, 16], BF16); BQbd = Buf("Qbd")
                onesbb = sbt(sp_, "onesbb", [128, 128], BF16); Bonesbb = Buf("onesbb")
                attT = sbt(sp_, "attT", [128, 4, 128], F32); BattT = [Buf("attT%d" % c) for c in range(4)]
                rd = sbt(sp_, "rd", [128, 64], F32); Brd = Buf("rd")
                fw.op("pool", lambda e: e.memset(Qbd[:], 0.0), writes=[BQbd])
                fw.op("pool", lambda e: e.memset(onesbb[:], 1.0), writes=[Bonesbb])
                KTn = sbt(sp_, "KTn", [128, 4, 128], BF16); BKTn = Buf("KTn")
                KTs = [sbt(sp_, "KTs%d" % i, [128, 4, NKEEP], BF16) for i in range(2)]; BKTs = [Buf("KTs%d" % i) for i in range(2)]
                Vs = [sbt(sp_, "Vs%d" % i, [128, NKB, 512], BF16) for i in range(2)]; BVs = [Buf("Vs%d" % i) for i in range(2)]
                Vn = sbt(sp_, "Vn", [8, 512], BF16); BVn = Buf("Vn")
                Ms = sbt(sp_, "Ms", [128, 17, 64], BF16); BMs = Buf("Ms")
                Mc = sbt(sp_, "Mc", [128, NKB + 1, 64], BF16); BMc = Buf("Mc")
                sel = [sbt(sp_, "sel%d" % i, [128, 128], BF16) for i in range(2)]; Bsel = [Buf("sel%d" % i) for i in range(2)]
                fw.dma("sp", lambda e: e.dma_start(out=sel[0][:], in_=sel0_d), Bsel[0], writes=[Bsel[0]])
                fw.dma("sp", lambda e: e.dma_start(out=sel[1][:], in_=sel1_d), Bsel[1], writes=[Bsel[1]])
                mrevs = sbt(sp_, "mrevs", [128, 17, 64], BF16); Bmrevs = Buf("mrevs")
                jmat2 = sbt(sp_, "jmat2", [128, 128], BF16); Bjm2 = Buf("jmat2")
                Es = [sbt(sp_, "Es%d" % i, [128, 64], F32) for i in range(2)]; BEs = [Buf("Es%d" % i) for i in range(2)]
                Ps = [sbt(sp_, "Ps%d" % i, [128, 64], BF16) for i in range(2)]; BPs = [Buf("Ps%d" % i) for i in range(2)]
                atts = sbt(sp_, "atts", [8, 512], F32); Batts = Buf("atts")
                recs = sbt(sp_, "recs", [8, 8], F32); Brecs = Buf("recs")

                fw.dma("sp", lambda e: e.dma_start(out=jmat2[:], in_=jmat_d), Bjm2, writes=[Bjm2])
                fns = []
                for kb in range(17):
                    fns.append(lambda e, kb=kb: e.dma_start(out=mrevs[:, kb, :].rearrange("p (h t) -> p h t", t=TS),
                                                            in_=bass.AP(tblscr.tensor, 2049 - 128 * kb, [[1, 128], [TBL, 8], [1, TS]])))
                fw.dma("sp", fns, Bmrevs, reads=[Btbl], writes=[Bmrevs])
                mflat = mrevs[:].rearrange("p k x -> p (k x)")
                Mflat = Ms[:].rearrange("p k x -> p (k x)")
                for i0 in range(0, 17 * 64, 512):
                    n = min(512, 17 * 64 - i0)
                    ps, Bps = mm_bank()
                    fw.op("pe", lambda e, ps=ps, i0=i0, n=n: e.matmul(ps[:, 0:n], lhsT=jmat2[:], rhs=mflat[:, i0:i0 + n], start=True, stop=True),
                          reads=[Bjm2, Bmrevs], writes=[Bps])
                    fw.op("act", lambda e, ps=ps, i0=i0, n=n: e.activation(out=Mflat[:, i0:i0 + n], in_=ps[:, 0:n], func=AF.Copy), reads=[Bps], writes=[BMs])

                stop_here('smask')
                for m in range(6):
                    ps, Bps = mm_bank()
                    for t2 in range(2):
                        fw.op("pe", lambda e, ps=ps, m=m, t2=t2: e.matmul(ps[:, 0:64], lhsT=sel[t2][:], rhs=Ms[:, 2 * m + t2, :], start=(t2 == 0), stop=(t2 == 1)),
                              reads=[Bsel[t2], BMs], writes=[Bps], pe_accum=True)
                    fw.op("act", lambda e, ps=ps, m=m: e.activation(out=Mc[:, m, :], in_=ps[:, 0:64], func=AF.Copy), reads=[Bps], writes=[BMc])
                fw.op("dve", lambda e: e.tensor_copy(out=Mc[:, 6:11, :], in_=Ms[:, 12:17, :]), reads=[BMs], writes=[BMc])
                def load_cache(s):
                    sl = s % 2
                    fw.dma("pool", lambda e: e.dma_start(out=KTs[sl][:], in_=ckT[s].rearrange("c p t -> p c t")), BKTs[sl], writes=[BKTs[sl]])
                    fw.dma("pool", lambda e: e.dma_start(out=Vs[sl][:], in_=cv[s].rearrange("(kb p) f -> p kb f", p=128)), BVs[sl], writes=[BVs[sl]])
                load_cache(0)
                load_cache(1)

                g0 = NTP
                norm_transpose(xs[:, :], 0, G1, nT, BnT, 0)
                for c in range(4):
                    def ev_q(ps, Bps, c=c):
                        for hp in range(2):
                            fw.op("act", lambda e, hp=hp: e.activation(out=Qbd[hp * 64:(hp + 1) * 64, c, :, hp * 8:(hp + 1) * 8],
                                                                       in_=ps[hp * 64:(hp + 1) * 64, 0:128].rearrange("p (s t) -> p s t", t=TS),
                                                                       func=AF.Copy, scale=0.125), reads=[Bps], writes=[BQbd])
                    proj_fm(c * 128, 128, ev_q)
                for c in range(4):
                    def ev_k(ps, Bps, c=c):
                        fw.op("dve", lambda e: e.tensor_copy(out=KTn[:, c, :], in_=ps[:, 0:128]), reads=[Bps], writes=[BKTn])
                    proj_fm(512 + c * 128, 128, ev_k)
                for (w0, dst, stg, Bstg) in ((512, nks, kst, Bkst), (1024, nvs, vst, Bvst)):
                    ps, Bps = mm_bank()
                    for kc in range(8):
                        fw.op("pe", lambda e, kc=kc, ps=ps, w0=w0: e.matmul(ps[:], lhsT=nT[:, kc, 0:128], rhs=w_in_bf[:, kc, w0:w0 + 512],
                                                                           start=(kc == 0), stop=(kc == 7)), reads=[Bwin[w0 // 512], BnT], writes=[Bps], pe_accum=True)
                    fw.op("act", lambda e, ps=ps, stg=stg: e.activation(out=stg[:], in_=ps[:], func=AF.Copy), reads=[Bps], writes=[Bstg])
                    fw.dma("sp", lambda e, dst=dst, stg=stg: e.dma_start(out=dst, in_=stg[:]), Bstg, reads=[Bstg])
                stop_here('sproj')
                scs = sbt(sp_, "scs", [128, 4, NSS, 3], F32); Bscs = Buf("scs")
                fw.dma("sp", lambda e: e.dma_start(out=scs[:], in_=sconvT), Bscs, writes=[Bscs])
                for c in range(4):
                    fw.op("pool", lambda e, c=c: e.tensor_copy(out=xr[:, c, 0:176].rearrange("p (s j) -> p s j", j=11)[:, :, 0:3], in_=scs[:, c, :, :]),
                          reads=[Bscs], writes=[Bxr[c]])
                shs = sbt(sp_, "shs", [128, 4, NSS], F32); Bshs = Buf("shs")
                fw.dma("sp", lambda e: e.dma_start(out=shs[:], in_=shT), Bshs, writes=[Bshs])

                def scan_s(c):
                    for s in range(NSS):
                        fw.op("dve", lambda e, s=s: e.tensor_tensor_scan(out=hs[:, s * 8:(s + 1) * 8], data0=a_t2[c % 2][:, s * 8:(s + 1) * 8], data1=t_g2[c % 2][:, s * 8:(s + 1) * 8],
                                                                        initial=shs[:, c, s:s + 1], op0=ALU.mult, op1=ALU.add),
                              reads=[Bat2[c % 2], Btg2[c % 2], Bshs], writes=[Bhs])

                def fin_s(c):
                    xv = xr[:, c, 0:176].rearrange("p (s j) -> p s j", j=11)
                    fw.op("dve", lambda e: e.tensor_copy(out=fin[:, c, 0:48].rearrange("p (j s) -> p j s", s=NSS), in_=xv[:, :, 8:11].rearrange("p s j -> p j s")),
                          reads=[Bxr[c]], writes=[Bfin])
                    fw.op("dve", lambda e: e.tensor_copy(out=fin[:, c, 48:64], in_=hs[:, 0:128].rearrange("p (s t) -> p s t", t=TS)[:, :, 7]),
                          reads=[Bhs], writes=[Bfin])
                def gen_srnn():
                    for c in range(4):
                        xv = xr[:, c, 0:176].rearrange("p (s j) -> p s j", j=11)
                        yield from rnn_chunk(c, 128, True, xv[:, :, 3:11], [xv[:, :, j:j + 8] for j in range(4)], scan_s, fin_s)
                    rnn_norm(128, 0)
                    yield
                    fin_out(64, [(j * 16, 16, ncs[:, j, :]) for j in range(3)], (48, 16), None, nhs)
                    yield
                srnn = gen_srnn()
                srnn_alive = [True]

                def srnn_step():
                    if srnn_alive[0]:
                        try:
                            next(srnn)
                        except StopIteration:
                            srnn_alive[0] = False
                for s in range(NSS):
                    sl = s % 2
                    ps, Bps = pbank[0], Bpb[0]
                    for kc in range(8):
                        fw.op("pe", lambda e, kc=kc, ps=ps, s=s: e.matmul(ps[0:8, :], lhsT=nT[:, kc, s * 8:(s + 1) * 8], rhs=w_in_bf[:, kc, 1024:1536],
                                                                         start=(kc == 0), stop=(kc == 7)), reads=[Bwin[2], BnT], writes=[Bps], pe_accum=True)
                    fw.op("act", lambda e, ps=ps: e.activation(out=Vn[:], in_=ps[0:8, :], func=AF.Copy), reads=[Bps], writes=[BVn])
                    accb, Baccb = pbank[4 + s % 2], Bpb[4 + s % 2]

                    def s_emit_S(kb, s=s, sl=sl):
                        npart = 128 if kb < NKB else 8
                        s_ps, Bs_ps = pbank[2 + kb % 2], Bpb[2 + kb % 2]
                        for c in range(4):
                            if kb < NKB:
                                lhs = KTs[sl][:, c, kb * 128:(kb + 1) * 128]
                                rdl = [BKTs[sl], BQbd]
                            else:
                                lhs = KTn[:, c, s * 8:(s + 1) * 8]
                                rdl = [BKTn, BQbd]
                            fw.op("pe", lambda e, s_ps=s_ps, lhs=lhs, c=c, s=s, npart=npart: e.matmul(
                                s_ps[0:npart, c * 16:(c + 1) * 16], lhsT=lhs, rhs=Qbd[:, c, s, :],
                                start=True, stop=True, skip_group_check=True), reads=rdl, writes=[Bs_ps], pe_accum=True)

                    def s_emit_rest(kb, s=s, sl=sl, accb=accb, Baccb=Baccb):
                        npart = 128 if kb < NKB else 8
                        s_ps, Bs_ps = pbank[2 + kb % 2], Bpb[2 + kb % 2]
                        E_, BE_ = Es[kb % 2], BEs[kb % 2]
                        P_, BP_ = Ps[kb % 2], BPs[kb % 2]
                        fw.op("act", lambda e, s_ps=s_ps, E_=E_, npart=npart: e.activation(out=E_[0:npart, :], in_=s_ps[0:npart, 0:64], func=AF.Exp),
                              reads=[Bs_ps], writes=[BE_])
                        fw.op("dve", lambda e, E_=E_, P_=P_, kb=kb, npart=npart: e.tensor_tensor(out=P_[0:npart, :], in0=E_[0:npart, :], in1=Mc[0:npart, kb, :], op=ALU.mult),
                              reads=[BE_, BMc], writes=[BP_])
                        for cp in range(4):
                            if kb < NKB:
                                lhs = Vs[sl][:, kb, cp * 128:(cp + 1) * 128]
                                rdv = BVs[sl]
                            else:
                                lhs = Vn[0:8, cp * 128:(cp + 1) * 128]
                                rdv = BVn
                            fw.op("pe", lambda e, P_=P_, cp=cp, lhs=lhs, npart=npart, f=(kb == 0 and cp == 0): e.matmul(
                                accb[:, cp * 64:(cp + 1) * 64], lhsT=lhs, rhs=P_[0:npart, :], start=f, stop=False, skip_group_check=True),
                                reads=[BP_, rdv], writes=[Baccb], pe_accum=True)
                        fw.op("pe", lambda e, P_=P_, npart=npart: e.matmul(
                            accb[:, 256:320], lhsT=onesbb[0:npart, :], rhs=P_[0:npart, :], start=False, stop=False, skip_group_check=True),
                            reads=[BP_, Bonesbb], writes=[Baccb], pe_accum=True)

                    s_emit_S(0)
                    for kb in range(NKB + 1):
                        if kb + 1 < NKB + 1:
                            s_emit_S(kb + 1)
                        s_emit_rest(kb)
                        if kb % 2 == 1:
                            srnn_step()
                    if s + 2 < NSS:
                        load_cache(s + 2)
                    fw.op("dve", lambda e, accb=accb: e.reciprocal(out=rd[:], in_=accb[:, 256:320]), reads=[Baccb], writes=[Brd])
                    for hp in range(2):
                        fw.op("dve", lambda e, accb=accb, hp=hp, s=s: e.tensor_tensor(
                            out=attT[hp * 64:(hp + 1) * 64, :, s * 8:(s + 1) * 8],
                            in0=accb[hp * 64:(hp + 1) * 64, 0:320].rearrange("p (c x) -> p c x", x=80)[:, :, hp * 8:hp * 8 + 8],
                            in1=rd[hp * 64:(hp + 1) * 64, :].rearrange("p (c x) -> p c x", x=16)[:, :, hp * 8:hp * 8 + 8], op=ALU.mult),
                            reads=[Baccb, Brd], writes=BattT)
                while srnn_alive[0]:
                    srnn_step()
                grp_norm(attT, BattT, ATG, 0, 128, 0)
                fw.dma("sp", lambda e: e.dma_start(out=mixscr[:, :, NTP:NTP + 128].rearrange("k p t -> p k t"), in_=mixT[:, :, 0:128]),
                       BmixT, reads=[BmixT], writes=[Bmix])
                fw.barrier()
                fw.run()

        stop_here('sample')
        with ExitStack() as p2:
            TB = 3
            NTK = TB * 128
            w_out_bf = sbt(p2, "w_out_bf", [128, 8, D], BF16); Bwo = Buf("w_out_bf")
            w_mi_bf = sbt(p2, "w_mi_bf", [128, 8, DFF], BF16); Bwmi = [Buf("w_mi_bf%d" % q) for q in range(4)]
            w_mo_bf = sbt(p2, "w_mo_bf", [128, 32, D], BF16); Bwmo = [Buf("w_mo_bf%d" % q) for q in range(4)]
            fgB = sbt(p2, "fgB", [128, D], F32); BfgB = Buf("fgB")
            mix2 = sbt(p2, "mix2", [128, 8, NTK], BF16); Bmix2 = Buf("mix2")
            xmid = sbt(p2, "xmid", [128, TB, D], F32); Bxmid = [Buf("xmid%d" % i) for i in range(TB)]
            ss2 = sbt(p2, "ss2", [128, 2], F32); Bss2 = Buf("ss2")
            xn2 = sbt(p2, "xn2", [128, D], BF16); Bxn2 = Buf("xn2")
            n2T = sbt(p2, "n2T", [128, 8, NTK], BF16); Bn2T = Buf("n2T")
            hT = sbt(p2, "hT", [128, 32, NTK], BF16); BhT = Buf("hT")
            rl = sbt(p2, "rl", [128, NTK], F32); Brl = Buf("rl")
            yst = sbt(p2, "yst", [128, D], F32); Byst = Buf("yst")

            fw.dma("pool", lambda e: e.dma_start(out=w_out_bf[:], in_=w_out.rearrange("(kc p) n -> p kc n", p=128)), Bwo, writes=[Bwo])
            for q in range(4):
                fw.dma("pool", lambda e, q=q: e.dma_start(out=w_mi_bf[:, :, q * 1024:(q + 1) * 1024],
                                                          in_=w_mi.rearrange("(kc p) n -> p kc n", p=128)[:, :, q * 1024:(q + 1) * 1024]), Bwmi[q], writes=[Bwmi[q]])
            for q in range(4):
                fw.dma("pool", lambda e, q=q: e.dma_start(out=w_mo_bf[:, q * 8:(q + 1) * 8, :],
                                                          in_=w_mo.rearrange("(kc p) n -> p kc n", p=128)[:, q * 8:(q + 1) * 8, :]), Bwmo[q], writes=[Bwmo[q]])
            fw.dma("sp", lambda e: e.dma_start(out=fgB[:], in_=fg.partition_broadcast(128)), BfgB, writes=[BfgB])

            NBLK = NTOK // 128

            def xsrc_of(g):
                return xp[g * 128:(g + 1) * 128, :] if g < NTP // 128 else xs[:, :]

            def ydst_of(g):
                return yp[g * 128:(g + 1) * 128, :] if g < NTP // 128 else ys[:, :]

            def load_tile_inputs(i):
                fw.dma("sp", lambda e, i=i: e.dma_start(out=mix2[:], in_=mixscr[:, :, i * NTK:(i + 1) * NTK].rearrange("k p t -> p k t")),
                       Bmix2, reads=[Bmix], writes=[Bmix2])

            def load_x(i, j):
                g = i * TB + j
                fw.dma("sp", lambda e, g=g, j=j: e.dma_start(out=xmid[:, j, :], in_=xsrc_of(g)), Bxmid[j], writes=[Bxmid[j]])

            ntiles = NBLK // TB
            load_tile_inputs(0)
            for j in range(TB):
                load_x(0, j)
            for i in range(ntiles):
                for j in range(TB):
                    for hf in range(2):
                        ps, Bps = pbank[hf], Bpb[hf]
                        for kc in range(8):
                            fw.op("pe", lambda e, ps=ps, kc=kc, j=j, hf=hf: e.matmul(
                                ps[:], lhsT=mix2[:, kc, j * 128:(j + 1) * 128], rhs=w_out_bf[:, kc, hf * 512:(hf + 1) * 512],
                                start=(kc == 0), stop=(kc == 7)), reads=[Bmix2, Bwo], writes=[Bps], pe_accum=True)
                        fw.op("dve", lambda e, ps=ps, j=j, hf=hf: e.tensor_tensor(out=xmid[:, j, hf * 512:(hf + 1) * 512], in0=ps[:],
                                                                                  in1=xmid[:, j, hf * 512:(hf + 1) * 512], op=ALU.add),
                              reads=[Bps, Bxmid[j]], writes=[Bxmid[j]])
                if i + 1 < ntiles:
                    load_tile_inputs(i + 1)
                for j in range(TB):
                    rms_rstd(xmid[:, j, :], Bxmid[j], xn2[:], Bxn2, ss2[:], Bss2, D)
                    fw.op("dve", lambda e, j=j: e.tensor_scalar(out=xn2[:], in0=xmid[:, j, :], scalar1=ss2[:, 0:1], scalar2=None, op0=ALU.mult),
                          reads=[Bxmid[j], Bss2], writes=[Bxn2])
                    for kc in range(8):
                        fw.op("pe", lambda e, kc=kc: e.transpose(out=ptp[:, kc * 128:(kc + 1) * 128], in_=xn2[:, kc * 128:(kc + 1) * 128], identity=identb[:]),
                              reads=[Bxn2, Bidb], writes=[Bptp], pe_accum=True)
                    fw.op("dve", lambda e, j=j: e.tensor_tensor(
                        out=n2T[:, :, j * 128:(j + 1) * 128], in0=ptp[:].rearrange("p (k j) -> p k j", k=8),
                        in1=vec[:, G2:G2 + 8].unsqueeze(2).to_broadcast([128, 8, 128]), op=ALU.mult), reads=[Bptp, Bvec], writes=[Bn2T])
                for fc in range(32):
                    ps, Bps = pbank[2 + fc % 2], Bpb[2 + fc % 2]
                    for kc in range(8):
                        fw.op("pe", lambda e, ps=ps, kc=kc, fc=fc: e.matmul(ps[:, 0:NTK], lhsT=w_mi_bf[:, kc, fc * 128:(fc + 1) * 128], rhs=n2T[:, kc, :],
                                                                            start=(kc == 0), stop=(kc == 7)), reads=[Bwmi[fc // 8], Bn2T], writes=[Bps], pe_accum=True)
                    fw.op("act", lambda e, ps=ps: e.activation(out=rl[:], in_=ps[:, 0:NTK], func=AF.Relu), reads=[Bps], writes=[Brl])
                    fw.op("dve", lambda e, fc=fc: e.tensor_tensor(out=hT[:, fc, :], in0=rl[:], in1=rl[:], op=ALU.mult), reads=[Brl], writes=[BhT])
                for j in range(TB):
                    g = i * TB + j
                    for hf in range(2):
                        ps, Bps = pbank[4 + hf], Bpb[4 + hf]
                        for fc in range(32):
                            fw.op("pe", lambda e, ps=ps, fc=fc, j=j, hf=hf: e.matmul(
                                ps[:], lhsT=hT[:, fc, j * 128:(j + 1) * 128], rhs=w_mo_bf[:, fc, hf * 512:(hf + 1) * 512],
                                start=(fc == 0), stop=(fc == 31)), reads=[BhT, Bwmo[fc // 8]], writes=[Bps], pe_accum=True)
                        fw.op("dve", lambda e, ps=ps, j=j, hf=hf: e.tensor_tensor(out=xmid[:, j, hf * 512:(hf + 1) * 512], in0=ps[:],
                                                                                  in1=xmid[:, j, hf * 512:(hf + 1) * 512], op=ALU.add),
                              reads=[Bps, Bxmid[j]], writes=[Bxmid[j]])
                    rms_rstd(xmid[:, j, :], Bxmid[j], xn2[:], Bxn2, ss2[:], Bss2, D)
                    fw.op("dve", lambda e, j=j: e.scalar_tensor_tensor(out=yst[:], in0=xmid[:, j, :], scalar=ss2[:, 0:1], in1=fgB[:],
                                                                      op0=ALU.mult, op1=ALU.mult), reads=[Bxmid[j], Bss2, BfgB], writes=[Byst])
                    fw.dma("sp", lambda e, g=g: e.dma_start(out=ydst_of(g), in_=yst[:]), Byst, reads=[Byst])
                    if i + 1 < ntiles:
                        load_x(i + 1, j)
            fw.barrier()
            fw.run()
    except _Stop:
        pass
    return nc


def _t5_bucket_np(dist):
    dist = np.asarray(dist, np.int32)
    d_f = np.maximum(dist, 16).astype(np.float32)
    large = 16 + (np.log(d_f / np.float32(16)) / np.float32(math.log(2048 / 16)) * np.float32(16)).astype(np.int32)
    large = np.minimum(large, 31)
    return np.where(dist < 16, dist, large)


def _consts():
    delta = np.arange(TBL) - 128
    valid = delta >= 0
    bucket = _t5_bucket_np(np.maximum(delta, 0))
    onehot = np.zeros((32, TBL), np.float32)
    onehot[bucket, np.arange(TBL)] = 1.0
    onehot[:, ~valid] = 0.0
    m = ((delta >= 0) & (delta <= 128)).astype(np.float32) \
        + ((delta >= 0) & (delta <= 512) & (delta % 4 == 0)).astype(np.float32) \
        + ((delta >= 0) & (delta <= 2048) & (delta % 16 == 0)).astype(np.float32)
    mult = np.broadcast_to(m[None, :], (8, TBL)).astype(np.float32).copy()
    return onehot, mult


_NC_CACHE = {}


def kernel(x_prompt, x_sample, cache_k, cache_v, state_conv, state_h, norm1_g, w_in, rel_bias,
           conv_w, conv_b, gate_a_w, gate_a_b, gate_x_w, gate_x_b, lru_lambda, att_out_g,
           rnn_out_g, w_out, norm2_g, w_mlp_in, w_mlp_out, final_g):
    f32 = np.float32
    x_prompt = np.asarray(x_prompt, f32)
    x_sample = np.asarray(x_sample, f32)
    cache_k = np.asarray(cache_k, f32)
    cache_v = np.asarray(cache_v, f32)
    state_conv = np.asarray(state_conv, f32)
    state_h = np.asarray(state_h, f32)

    def fm(v, n):
        return np.asarray(v, f32).reshape(n, 128).T

    vecs = np.zeros((128, NV), f32)
    vecs[:, 0:8] = fm(norm1_g[0], 8)
    vecs[:, 8:16] = fm(norm2_g[0], 8)
    vecs[:, 16:20] = fm(att_out_g[0], 4)
    vecs[:, 20:24] = fm(rnn_out_g[0], 4)
    cw = np.asarray(conv_w[0], f32)
    for c in range(4):
        for j in range(4):
            vecs[:, 24 + c * 4 + j] = cw[j, c * 128:(c + 1) * 128]
    vecs[:, 40:44] = fm(conv_b[0], 4)
    vecs[:, 44:48] = fm(np.asarray(gate_a_b[0], f32).reshape(512), 4)
    vecs[:, 48:52] = fm(np.asarray(gate_x_b[0], f32).reshape(512), 4)
    vecs[:, 52:56] = fm(lru_lambda[0], 4)
    onehot, mult = _consts()
    shared = dict(
        w_in=np.ascontiguousarray(np.asarray(w_in[0], f32)), w_out=np.ascontiguousarray(np.asarray(w_out[0], f32)),
        w_mi=np.ascontiguousarray(np.asarray(w_mlp_in[0], f32)), w_mo=np.ascontiguousarray(np.asarray(w_mlp_out[0], f32)),
        vecs=vecs, fg=np.asarray(final_g, f32), gaw=np.ascontiguousarray(np.asarray(gate_a_w[0], f32)),
        gxw=np.ascontiguousarray(np.asarray(gate_x_w[0], f32)), relb=np.ascontiguousarray(np.asarray(rel_bias, f32)),
        onehot=onehot, mult=mult, identb=np.eye(128).astype(ml_dtypes.bfloat16), identf=np.eye(128, dtype=f32),
        jmat=np.ascontiguousarray(np.eye(128)[::-1]).astype(ml_dtypes.bfloat16),
    )
    rows_sel = np.array([r for r in range(1536) if r % 16 < 8] + list(range(1536, SEQ)))
    assert len(rows_sel) == NKEEP
    sel0 = np.zeros((128, 128), f32)
    sel1 = np.zeros((128, 128), f32)
    for ik in range(128):
        if ik % 16 < 8:
            sel0[ik, (ik // 16) * 8 + ik % 16] = 1.0
            sel1[ik, (8 + ik // 16) * 8 + ik % 16] = 1.0
    shared["sel0"] = sel0.astype(ml_dtypes.bfloat16)
    shared["sel1"] = sel1.astype(ml_dtypes.bfloat16)
    in_maps = []
    for c in range(NCORES):
        ck = cache_k[0, c * NSS:(c + 1) * NSS][:, rows_sel]
        ckT = np.ascontiguousarray(ck.transpose(0, 2, 3, 1)).reshape(NSS, 4, 128, NKEEP)
        sc = state_conv[0, c * NSS:(c + 1) * NSS]
        sconvT = np.ascontiguousarray(sc.reshape(NSS, 3, 4, 128).transpose(3, 2, 0, 1))
        sh = state_h[0, c * NSS:(c + 1) * NSS]
        shT = np.ascontiguousarray(sh.reshape(NSS, 4, 128).transpose(2, 1, 0))
        m = dict(shared)
        m.update(
            xp=np.ascontiguousarray(x_prompt[c * NPS:(c + 1) * NPS].reshape(NTP, D)),
            xs=np.ascontiguousarray(x_sample[c * NSS:(c + 1) * NSS].reshape(NSS * TS, D)),
            ckT=ckT, cv=np.ascontiguousarray(cache_v[0, c * NSS:(c + 1) * NSS][:, rows_sel].reshape(NSS, NKEEP, 512)),
            sconvT=sconvT, shT=shT,
        )
        in_maps.append(m)
    if "nc" not in _NC_CACHE:
        _NC_CACHE["nc"] = build_program()
    nc = _NC_CACHE["nc"]
    res = run_bass_kernel_spmd(nc, in_maps, core_ids=list(range(NCORES)))
    R = res.results

    def cat(name, shape):
        return np.concatenate([np.asarray(r[name], f32).reshape(shape) for r in R], axis=0)

    y_prompt = cat("yp", (NPS, SEQ, D))
    y_sample = cat("ys", (NSS, TS, D))
    nk_p = cat("nkp", (NPS, SEQ, 8, 64))[None]
    nv_p = cat("nvp", (NPS, SEQ, 8, 64))[None]
    nc_p = cat("ncp", (NPS, 3, 512))[None]
    nh_p = cat("nhp", (NPS, 512))[None]
    nk_s = cat("nks", (NSS, TS, 8, 64))[None]
    nv_s = cat("nvs", (NSS, TS, 8, 64))[None]
    nc_s = cat("ncs", (NSS, 3, 512))[None]
    nh_s = cat("nhs", (NSS, 512))[None]
    return (y_prompt, y_sample, nk_p, nv_p, nc_p, nh_p, nk_s, nv_s, nc_s, nh_s)


if __name__ == "__main__":
    import time
    t0 = time.time()
    nc = build_program()
    print("built in", time.time() - t0, "n_instructions", nc.n_instructions())
```

```python
import math
from contextlib import ExitStack
import numpy as np
import ml_dtypes
import concourse.bass as bass
import concourse.mybir as mybir
from concourse.bass_utils import run_bass_kernel_spmd

F32 = mybir.dt.float32
BF16 = mybir.dt.bfloat16
AF = mybir.ActivationFunctionType
ALU = mybir.AluOpType

NCORES = 8
D = 1024
NIN = 2560
DFF = 4096
SEQ = 2048
NPS = 2
NSS = 16
TS = 8
NTP = NPS * SEQ
NTOK = NTP + NSS * TS
TBL = 2304
NKEEP = 1280
NKB = NKEEP // 128
EPS = 1e-6
NV = 56


class Buf:
    __slots__ = ("name", "writers", "readers", "dsem", "dcount", "multi")

    def __init__(self, name, multi=False):
        self.name = name
        self.writers = []
        self.readers = []
        self.dsem = None
        self.dcount = 0
        self.multi = multi


class Eng:
    def __init__(self, name):
        self.name = name
        self.thunks = []
        self.count = 0
        self.seen = {}


class FW:
    ENG_NAMES = ("pe", "act", "dve", "pool", "sp")

    def __init__(self, nc, stack):
        self.nc = nc
        self.stack = stack
        self.engs = {n: Eng(n) for n in self.ENG_NAMES}
        self.sems = {}
        for n in self.ENG_NAMES:
            self.sems[("eng", n)] = stack.enter_context(nc.semaphore("s_" + n))
        self.dma_bufs = []

    def _dsem(self, buf):
        if buf.dsem is None:
            key = ("dma", len(self.dma_bufs))
            self.sems[key] = self.stack.enter_context(self.nc.semaphore("d%d" % len(self.dma_bufs)))
            buf.dsem = key
            self.dma_bufs.append(buf)
        return buf.dsem

    def _deps(self, reads, writes, pe_accum=False):
        deps = {}

        def add(tok):
            k, v = tok
            if deps.get(k, 0) < v:
                deps[k] = v
        for b in reads:
            for t in b.writers:
                add(t)
        for b in writes:
            if b.multi:
                continue
            for t in b.writers:
                if pe_accum and t[0] == ("eng", "pe"):
                    continue
                add(t)
            for t in b.readers:
                add(t)
        return deps

    def _emit_waits(self, e, deps):
        E = self.engs[e]
        for k, v in deps.items():
            if E.seen.get(k, 0) >= v:
                continue
            E.seen[k] = v
            h = self.sems[k]
            E.thunks.append(lambda eng, h=h, v=v: eng.wait_ge(h, v))

    def _update(self, reads, writes, tok):
        for b in reads:
            b.readers.append(tok)
            if len(b.readers) > 64:
                mx = {}
                for k, v in b.readers:
                    if mx.get(k, 0) < v:
                        mx[k] = v
                b.readers = list(mx.items())
        for b in writes:
            if b.multi:
                b.writers.append(tok)
                if len(b.writers) > 64:
                    mx = {}
                    for k, v in b.writers:
                        if mx.get(k, 0) < v:
                            mx[k] = v
                    b.writers = list(mx.items())
            else:
                b.writers = [tok]
                b.readers = []

    def op(self, e, fn, reads=(), writes=(), pe_accum=False):
        E = self.engs[e]
        self._emit_waits(e, self._deps(reads, writes, pe_accum))
        sem = self.sems[("eng", e)]
        E.count += 1
        tok = (("eng", e), E.count)
        E.thunks.append(lambda eng, fn=fn, sem=sem: fn(eng).then_inc(sem, 1))
        self._update(reads, writes, tok)
        return tok

    def dma(self, q, fns, primary, reads=(), writes=()):
        if not isinstance(fns, (list, tuple)):
            fns = [fns]
        E = self.engs[q]
        self._emit_waits(q, self._deps(reads, writes))
        key = self._dsem(primary)
        sem = self.sems[key]
        for fn in fns:
            primary.dcount += 16
            E.thunks.append(lambda eng, fn=fn, sem=sem: fn(eng).then_inc(sem, 16))
        tok = (key, primary.dcount)
        self._update(reads, writes, tok)
        return tok

    def barrier(self):
        deps = {}
        for b in self.dma_bufs:
            if b.dcount:
                deps[b.dsem] = b.dcount
        for n in self.ENG_NAMES:
            if self.engs[n].count:
                deps[("eng", n)] = self.engs[n].count
        for n in self.ENG_NAMES:
            d = {k: v for k, v in deps.items() if k != ("eng", n)}
            self._emit_waits(n, d)

    def run(self):
        nc = self.nc
        with nc.Block() as block:
            @block.tensor
            def _(eng):
                for t in self.engs["pe"].thunks:
                    t(eng)

            @block.scalar
            def _(eng):
                for t in self.engs["act"].thunks:
                    t(eng)

            @block.vector
            def _(eng):
                for t in self.engs["dve"].thunks:
                    t(eng)

            @block.gpsimd
            def _(eng):
                for t in self.engs["pool"].thunks:
                    t(eng)

            @block.sync
            def _(eng):
                for t in self.engs["sp"].thunks:
                    t(eng)
        for n in self.ENG_NAMES:
            self.engs[n].thunks = []


class _Stop(Exception):
    pass


DBG = dict(stop=None, nseq=NPS, ntile=4, proj=True, kv=True, rnn=True, attn=True, sample=True, sattn=True, phase2=True)

def build_program():
    nc = bass.Bass("TRN2", target_bir_lowering=False)

    def din(name, shape, dt=F32):
        return nc.dram_tensor(name, shape, dt, kind="ExternalInput").ap()

    def dout(name, shape, dt=F32):
        return nc.dram_tensor(name, shape, dt, kind="ExternalOutput").ap()

    xp = din("xp", [NTP, D])
    xs = din("xs", [NSS * TS, D])
    ckT = din("ckT", [NSS, 4, 128, NKEEP])
    cv = din("cv", [NSS, NKEEP, 512])
    sel0_d = din("sel0", [128, 128], BF16)
    sel1_d = din("sel1", [128, 128], BF16)
    sconvT = din("sconvT", [128, 4, NSS, 3])
    shT = din("shT", [128, 4, NSS])
    w_in = din("w_in", [D, NIN])
    w_out = din("w_out", [D, D])
    w_mi = din("w_mi", [D, DFF])
    w_mo = din("w_mo", [DFF, D])
    vecs = din("vecs", [128, NV])
    fg = din("fg", [D])
    gaw = din("gaw", [8, 64, 64])
    gxw = din("gxw", [8, 64, 64])
    relb = din("relb", [32, 8])
    onehot = din("onehot", [32, TBL])
    mult = din("mult", [8, TBL])
    identb_d = din("identb", [128, 128], BF16)
    identf_d = din("identf", [128, 128])
    jmat_d = din("jmat", [128, 128], BF16)

    yp = dout("yp", [NTP, D])
    ys = dout("ys", [NSS * TS, D])
    nkp = dout("nkp", [NTP, 512])
    nvp = dout("nvp", [NTP, 512])
    ncp = dout("ncp", [NPS, 3, 512])
    nhp = dout("nhp", [NPS, 512])
    nks = dout("nks", [NSS * TS, 512])
    nvs = dout("nvs", [NSS * TS, 512])
    ncs = dout("ncs", [NSS, 3, 512])
    nhs = dout("nhs", [NSS, 512])

    mixscr = nc.dram_tensor("mixscr", [8, 128, NTOK], BF16, kind="Internal").ap()
    tblscr = nc.dram_tensor("tblscr", [8, TBL], BF16, kind="Internal").ap()
    Bmix = Buf("mixscr", multi=True)
    Btbl = Buf("tblscr", multi=True)

    try:
      with ExitStack() as top:
        fw = FW(nc, top)

        def stop_here(tag):
            if DBG.get('stop') == tag:
                fw.barrier()
                fw.run()
                raise _Stop()

        def sbt(st, name, shape, dt):
            return st.enter_context(nc.sbuf_tensor("sb_" + name, shape, dt))

        pbank = [top.enter_context(nc.psum_tensor("pb%d" % i, [128, 512], F32)) for i in range(7)]
        ptp = top.enter_context(nc.psum_tensor("ptp", [128, 1024], BF16))
        Bpb = [Buf("pb%d" % i) for i in range(7)]
        Bptp = Buf("ptp")

        identb = sbt(top, "identb", [128, 128], BF16); Bidb = Buf("identb")
        identf = sbt(top, "identf", [128, 128], F32); Bidf = Buf("identf")
        vec = sbt(top, "vec", [128, NV], F32); Bvec = Buf("vec")
        vec2 = sbt(top, "vec2", [128, 24], F32); Bvec2 = Buf("vec2")
        cst = sbt(top, "cst", [128, 4], F32); Bcst = Buf("cst")
        fw.op("pool", lambda e: e.memset(cst[:, 0:1], EPS), writes=[Bcst])
        fw.op("pool", lambda e: e.memset(cst[:, 1:2], math.log(0.5)), writes=[Bcst])
        fw.op("pool", lambda e: e.memset(cst[:, 2:3], 1.0), writes=[Bcst])
        fw.dma("sp", lambda e: e.dma_start(out=identb[:], in_=identb_d), Bidb, writes=[Bidb])
        fw.dma("sp", lambda e: e.dma_start(out=identf[:], in_=identf_d), Bidf, writes=[Bidf])
        fw.dma("sp", lambda e: e.dma_start(out=vec[:], in_=vecs), Bvec, writes=[Bvec])
        G1, G2, ATG, RNG, CW, CB, BA, BX, LAM = 0, 8, 16, 20, 24, 40, 44, 48, 52
        HBA, HBX, CC, CH = 0, 4, 8, 12
        fw.op("dve", lambda e: e.tensor_scalar(out=vec2[:, 0:8], in0=vec[:, BA:BA + 8], scalar1=0.5, scalar2=None, op0=ALU.mult),
              reads=[Bvec], writes=[Bvec2])
        fw.op("act", lambda e: e.activation(out=vec2[:, 16:20], in_=vec[:, LAM:LAM + 4], func=AF.Exp, scale=-1.0), reads=[Bvec], writes=[Bvec2])
        fw.op("act", lambda e: e.activation(out=vec2[:, 16:20], in_=vec2[:, 16:20], func=AF.Ln, scale=1.0, bias=cst[:, 2:3]), reads=[Bvec2, Bcst], writes=[Bvec2])
        fw.op("dve", lambda e: e.tensor_scalar(out=vec2[:, CC:CC + 4], in0=vec2[:, 16:20], scalar1=-8.0, scalar2=None, op0=ALU.mult),
              reads=[Bvec2], writes=[Bvec2])
        fw.op("dve", lambda e: e.tensor_scalar(out=vec2[:, CH:CH + 4], in0=vec2[:, 16:20], scalar1=-4.0, scalar2=None, op0=ALU.mult),
              reads=[Bvec2], writes=[Bvec2])

        def rms_rstd(eng_in_ap, Bin, junk, Bjunk, ss, Bss, n):
            fw.op("act", lambda e: e.activation(out=junk, in_=eng_in_ap, func=AF.Square, accum_out=ss[:, 0:1]),
                  reads=[Bin], writes=[Bjunk, Bss])
            npart = ss.shape[0]
            fw.op("act", lambda e: e.activation(out=ss[:, 1:2], in_=cst[0:npart, 2:3], func=AF.Copy), reads=[Bss, Bcst], writes=[Bss])
            fw.op("act", lambda e: e.activation(out=ss[:, 1:2], in_=ss[:, 0:1], func=AF.Ln, scale=1.0 / n, bias=cst[0:npart, 0:1]), reads=[Bss, Bcst], writes=[Bss])
            fw.op("act", lambda e: e.activation(out=ss[:, 0:1], in_=ss[:, 1:2], func=AF.Exp, scale=-0.5), reads=[Bss], writes=[Bss])

        with ExitStack() as p1:
            w_in_bf = sbt(p1, "w_in_bf", [128, 8, NIN], BF16); Bwin = [Buf("w_in_bf%d" % g) for g in range(5)]
            wa_bd = sbt(p1, "wa_bd", [128, 4, 128], BF16); Bwa = Buf("wa_bd")
            wx_bd = sbt(p1, "wx_bd", [128, 4, 128], BF16); Bwx = Buf("wx_bd")
            ones_f = sbt(p1, "ones_f", [128, 128], F32); Bones = Buf("ones_f")
            onesb = sbt(p1, "onesb", [128, 1], BF16); Bonesb = Buf("onesb")
            xblk = [sbt(p1, "xblk%d" % i, [128, D], F32) for i in range(2)]; Bxblk = [Buf("xblk%d" % i) for i in range(2)]
            ss = sbt(p1, "ss", [128, 2], F32); Bss = Buf("ss")
            xn = [sbt(p1, "xn%d" % i, [128, D], BF16) for i in range(2)]; Bxn = [Buf("xn%d" % i) for i in range(2)]
            nT = sbt(p1, "nT", [128, 8, 512], BF16); BnT = Buf("nT")
            mixT = sbt(p1, "mixT", [128, 8, 512], BF16); BmixT = Buf("mixT")
            kst = sbt(p1, "kst", [128, 512], F32); Bkst = Buf("kst")
            vst = sbt(p1, "vst", [128, 512], F32); Bvst = Buf("vst")
            xr = sbt(p1, "xr", [128, 4, 515], F32); Bxr = [Buf("xr%d" % c) for c in range(4)]
            gg2 = [sbt(p1, "gg%d" % i, [128, 512], F32) for i in range(2)]; Bgg2 = [Buf("gg%d" % i) for i in range(2)]
            xc2 = [sbt(p1, "xc%d" % i, [128, 512], F32) for i in range(2)]; Bxc2 = [Buf("xc%d" % i) for i in range(2)]
            xcb2 = [sbt(p1, "xcb%d" % i, [128, 512], BF16) for i in range(2)]; Bxcb2 = [Buf("xcb%d" % i) for i in range(2)]
            t_r2 = [sbt(p1, "t_r%d" % i, [128, 512], F32) for i in range(2)]; Btr2 = [Buf("t_r%d" % i) for i in range(2)]
            t_g2 = [sbt(p1, "t_g%d" % i, [128, 512], F32) for i in range(2)]; Btg2 = [Buf("t_g%d" % i) for i in range(2)]
            a_t2 = [sbt(p1, "a_t%d" % i, [128, 512], F32) for i in range(2)]; Bat2 = [Buf("a_t%d" % i) for i in range(2)]
            tmp2 = [sbt(p1, "tmp%d" % i, [128, 512], F32) for i in range(2)]; Btmp2 = [Buf("tmp%d" % i) for i in range(2)]
            hs = sbt(p1, "hs", [128, 512], F32); Bhs = Buf("hs")
            rnn = sbt(p1, "rnn", [128, 4, 512], F32); Brnn = [Buf("rnn%d" % c) for c in range(4)]
            sqb = sbt(p1, "sqb", [128, 512], F32); Bsqb = Buf("sqb")
            rstdb = sbt(p1, "rstdb", [128, 512], F32); Brstdb = Buf("rstdb")
            hstate = sbt(p1, "hstate", [128, 4], F32); Bhst = Buf("hstate")
            fin = sbt(p1, "fin", [128, 4, 64], F32); Bfin = Buf("fin")
            att = sbt(p1, "att", [128, 4, 512], BF16); Batt = [Buf("att%d" % i) for i in range(4)]
            attn = sbt(p1, "attn", [128, 512], BF16); Battn = Buf("attn")
            rec = sbt(p1, "rec", [128, 4], F32); Brec = Buf("rec")

            for g in range(5):
                fw.dma("pool", lambda e, g=g: e.dma_start(
                    out=w_in_bf[:, :, g * 512:(g + 1) * 512],
                    in_=w_in.rearrange("(kc p) n -> p kc n", p=128)[:, :, g * 512:(g + 1) * 512]), Bwin[g], writes=[Bwin[g]])
            fw.op("pool", lambda e: e.memset(wa_bd[:], 0.0), writes=[Bwa])
            fw.op("pool", lambda e: e.memset(wx_bd[:], 0.0), writes=[Bwx])
            fw.op("pool", lambda e: e.memset(ones_f[:], 1.0), writes=[Bones])
            fw.op("pool", lambda e: e.memset(onesb[:], 1.0), writes=[Bonesb])
            for (src, dst, Bd) in ((gaw, wa_bd, Bwa), (gxw, wx_bd, Bwx)):
                for hp in range(2):
                    fw.dma("pool", lambda e, src=src, dst=dst, hp=hp: e.dma_start(
                        out=dst[hp * 64:(hp + 1) * 64, :, hp * 64:(hp + 1) * 64],
                        in_=src.rearrange("(c two) i o -> two i c o", two=2)[hp]), Bd, writes=[Bd])

            def norm_transpose(xsrc_ap, slot, gcol, dstT, BdstT, col0, ntok=128):
                xb, Bx_ = xblk[slot], Bxblk[slot]
                if xsrc_ap is not None:
                    fw.dma("sp", lambda e: e.dma_start(out=xb[0:ntok, :], in_=xsrc_ap), Bx_, writes=[Bx_])
                rms_rstd(xb[0:ntok, :], Bx_, xn[slot][0:ntok, :], Bxn[slot], ss[0:ntok, :], Bss, D)
                fw.op("act", lambda e: e.activation(out=xn[slot][0:ntok, :], in_=xb[0:ntok, :], func=AF.Copy, scale=ss[0:ntok, 0:1]),
                      reads=[Bx_, Bss], writes=[Bxn[slot]])
                for kc in range(8):
                    fw.op("pe", lambda e, kc=kc: e.transpose(out=ptp[:, kc * 128:kc * 128 + ntok], in_=xn[slot][0:ntok, kc * 128:(kc + 1) * 128],
                                                             identity=identb[0:ntok, 0:ntok]),
                          reads=[Bxn[slot], Bidb], writes=[Bptp], pe_accum=True)
                fw.op("dve", lambda e: e.tensor_tensor(
                    out=dstT[:, :, col0:col0 + ntok], in0=ptp[:].rearrange("p (k j) -> p k j", k=8)[:, :, 0:ntok],
                    in1=vec[:, gcol:gcol + 8].unsqueeze(2).to_broadcast([128, 8, ntok]), op=ALU.mult),
                    reads=[Bptp, Bvec], writes=[BdstT])

            mmi = [0]

            def mm_bank():
                i = mmi[0] % 2
                mmi[0] += 1
                return pbank[i], Bpb[i]

            def proj_fm(wcol0, N, evac):
                ps, Bps = mm_bank()
                for kc in range(8):
                    fw.op("pe", lambda e, kc=kc, ps=ps: e.matmul(ps[:, 0:N], lhsT=w_in_bf[:, kc, wcol0:wcol0 + 128], rhs=nT[:, kc, 0:N],
                                                                 start=(kc == 0), stop=(kc == 7)),
                          reads=[Bwin[wcol0 // 512], BnT], writes=[Bps], pe_accum=True)
                evac(ps, Bps)

            def rnn_chunk(c, N, seg, xr_view, conv_views, scan_fn, last_tile_fin):
                xc, Bxc = xc2[c % 2], Bxc2[c % 2]
                t_g, Btg = t_g2[c % 2], Btg2[c % 2]
                a_t, Bat = a_t2[c % 2], Bat2[c % 2]
                tmp, Btmp = tmp2[c % 2], Btmp2[c % 2]
                gg, Bgg = gg2[c % 2], Bgg2[c % 2]
                xcb, Bxcb = xcb2[c % 2], Bxcb2[c % 2]
                t_r, Btr = t_r2[c % 2], Btr2[c % 2]

                def ev_xr(ps, Bps):
                    fw.op("act", lambda e: e.activation(out=xr_view, in_=ps[:, 0:N] if seg is None else ps[:, 0:N].rearrange("p (s t) -> p s t", t=TS),
                                                        func=AF.Copy), reads=[Bps], writes=[Bxr[c]])
                proj_fm(1536 + c * 128, N, ev_xr)
                yield
                xcv = xc[:, 0:N] if seg is None else xc[:, 0:N].rearrange("p (s t) -> p s t", t=TS)
                cw = CW + c * 4
                fw.op("dve", lambda e: e.tensor_scalar(out=xcv, in0=conv_views[3], scalar1=vec[:, cw + 3:cw + 4], scalar2=vec[:, CB + c:CB + c + 1],
                                                       op0=ALU.mult, op1=ALU.add), reads=[Bxr[c], Bvec], writes=[Bxc])
                for j in (2, 1, 0):
                    fw.op("dve", lambda e, j=j: e.scalar_tensor_tensor(out=xcv, in0=conv_views[j], scalar=vec[:, cw + j:cw + j + 1], in1=xcv,
                                                                      op0=ALU.mult, op1=ALU.add), reads=[Bxr[c], Bvec, Bxc], writes=[Bxc])
                yield
                fw.op("act", lambda e: e.activation(out=xcb[:, 0:N], in_=xc[:, 0:N], func=AF.Copy), reads=[Bxc], writes=[Bxcb])
                psr, Bpsr = mm_bank()
                fw.op("pe", lambda e: e.matmul(psr[:, 0:N], lhsT=wa_bd[:, c, :], rhs=xcb[:, 0:N], start=True, stop=True), reads=[Bwa, Bxcb], writes=[Bpsr])
                psg, Bpsg = mm_bank()
                fw.op("pe", lambda e: e.matmul(psg[:, 0:N], lhsT=wx_bd[:, c, :], rhs=xcb[:, 0:N], start=True, stop=True), reads=[Bwx, Bxcb], writes=[Bpsg])
                psq, Bpsq = pbank[6], Bpb[6]
                for kc in range(8):
                    fw.op("pe", lambda e, kc=kc: e.matmul(psq[:, 0:N], lhsT=w_in_bf[:, kc, 2048 + c * 128:2048 + (c + 1) * 128], rhs=nT[:, kc, 0:N],
                                                          start=(kc == 0), stop=(kc == 7)), reads=[Bwin[4], BnT], writes=[Bpsq], pe_accum=True)
                fw.op("act", lambda e: e.activation(out=t_r[:, 0:N], in_=psr[:, 0:N], func=AF.Tanh, scale=0.5, bias=vec2[:, HBA + c:HBA + c + 1]),
                      reads=[Bpsr, Bvec2], writes=[Btr])
                fw.op("act", lambda e: e.activation(out=t_g[:, 0:N], in_=psg[:, 0:N], func=AF.Tanh, scale=0.5, bias=vec2[:, HBX + c:HBX + c + 1]),
                      reads=[Bpsg, Bvec2], writes=[Btg])
                fw.op("act", lambda e: e.activation(out=gg[:, 0:N], in_=psq[:, 0:N], func=AF.Gelu_apprx_tanh), reads=[Bpsq], writes=[Bgg])
                yield
                fw.op("act", lambda e: e.activation(out=a_t[:, 0:N], in_=t_r[:, 0:N], func=AF.Exp, scale=vec2[:, CH + c:CH + c + 1], bias=vec2[:, CH + c:CH + c + 1]),
                      reads=[Btr, Bvec2], writes=[Bat])
                fw.op("act", lambda e: e.activation(out=tmp[:, 0:N], in_=t_r[:, 0:N], func=AF.Exp, scale=vec2[:, CC + c:CC + c + 1], bias=vec2[:, CC + c:CC + c + 1]),
                      reads=[Btr, Bvec2], writes=[Btmp])
                fw.op("act", lambda e: e.activation(out=tmp[:, 0:N], in_=tmp[:, 0:N], func=AF.Ln, scale=-1.0, bias=cst[:, 2:3]), reads=[Btmp, Bcst], writes=[Btmp])
                fw.op("act", lambda e: e.activation(out=tmp[:, 0:N], in_=tmp[:, 0:N], func=AF.Exp, scale=0.5, bias=cst[:, 1:2]), reads=[Btmp, Bcst], writes=[Btmp])
                yield
                fw.op("dve", lambda e: e.scalar_tensor_tensor(out=t_g[:, 0:N], in0=t_g[:, 0:N], scalar=1.0, in1=xc[:, 0:N], op0=ALU.add, op1=ALU.mult),
                      reads=[Btg, Bxc], writes=[Btg])
                fw.op("dve", lambda e: e.tensor_tensor(out=t_g[:, 0:N], in0=t_g[:, 0:N], in1=tmp[:, 0:N], op=ALU.mult), reads=[Btg, Btmp], writes=[Btg])
                yield
                scan_fn(c)
                fw.op("dve", lambda e: e.tensor_tensor(out=rnn[:, c, 0:N], in0=hs[:, 0:N], in1=gg[:, 0:N], op=ALU.mult), reads=[Bhs, Bgg], writes=[Brnn[c]])
                last_tile_fin(c)
                yield

            def rnn_norm(N, col0):
                grp_norm(rnn, Brnn, RNG, 4, N, col0)

            def grp_norm(src, Bsrc, gcol, dst_c0, N, col0):
                rnn, Brnn, RNG = src, Bsrc, gcol
                ps, Bps = pbank[6], Bpb[6]
                for c in range(4):
                    fw.op("act", lambda e, c=c: e.activation(out=sqb[:, 0:N], in_=rnn[:, c, 0:N], func=AF.Square), reads=[Brnn[c]], writes=[Bsqb])
                    fw.op("pe", lambda e, c=c: e.matmul(ps[:, 0:N], lhsT=ones_f[:], rhs=sqb[:, 0:N], start=(c == 0), stop=(c == 3)),
                          reads=[Bones, Bsqb], writes=[Bps], pe_accum=True)
                fw.op("act", lambda e: e.activation(out=rstdb[:, 0:N], in_=ps[:, 0:N], func=AF.Ln, scale=1.0 / 512, bias=cst[:, 0:1]), reads=[Bps, Bcst], writes=[Brstdb])
                fw.op("act", lambda e: e.activation(out=rstdb[:, 0:N], in_=rstdb[:, 0:N], func=AF.Exp, scale=-0.5), reads=[Brstdb], writes=[Brstdb])
                for c in range(4):
                    fw.op("dve", lambda e, c=c: e.scalar_tensor_tensor(out=mixT[:, dst_c0 + c, col0:col0 + N], in0=rnn[:, c, 0:N], scalar=vec[:, RNG + c:RNG + c + 1],
                                                                      in1=rstdb[:, 0:N], op0=ALU.mult, op1=ALU.mult),
                          reads=[Brnn[c], Bvec, Brstdb], writes=[BmixT])

            def att_norm_block(att_ap, Batt_, ntok, col0):
                rms_rstd(att_ap, Batt_, attn[0:ntok, :], Battn, ss[0:ntok, :], Bss, 512)
                fw.op("act", lambda e: e.activation(out=attn[0:ntok, :], in_=att_ap, func=AF.Copy, scale=ss[0:ntok, 0:1]), reads=[Batt_, Bss], writes=[Battn])
                for cc in range(4):
                    fw.op("pe", lambda e, cc=cc: e.transpose(out=ptp[:, cc * 128:cc * 128 + ntok], in_=attn[0:ntok, cc * 128:(cc + 1) * 128],
                                                             identity=identb[0:ntok, 0:ntok]), reads=[Battn, Bidb], writes=[Bptp], pe_accum=True)
                fw.op("dve", lambda e: e.tensor_tensor(
                    out=mixT[:, 0:4, col0:col0 + ntok], in0=ptp[:, 0:512].rearrange("p (k j) -> p k j", k=4)[:, :, 0:ntok],
                    in1=vec[:, ATG:ATG + 4].unsqueeze(2).to_broadcast([128, 4, ntok]), op=ALU.mult), reads=[Bptp, Bvec], writes=[BmixT])

            def fin_out(ncols, rows_conv, rows_h, conv_dst_fn, h_dst):
                ps, Bps = pbank[6], Bpb[6]
                for c in range(4):
                    fw.op("pe", lambda e, c=c: e.transpose(out=ps[0:ncols, c * 128:(c + 1) * 128], in_=fin[:, c, 0:ncols], identity=identf[:]),
                          reads=[Bfin, Bidf], writes=[Bps], pe_accum=True)
                fw.op("act", lambda e: e.activation(out=kst[0:ncols, :], in_=ps[0:ncols, :], func=AF.Copy), reads=[Bps], writes=[Bkst])
                fns = []
                for (r0, n, dst) in rows_conv:
                    fns.append(lambda e, r0=r0, n=n, dst=dst: e.dma_start(out=dst, in_=kst[r0:r0 + n, :]))
                fns.append(lambda e: e.dma_start(out=h_dst, in_=kst[rows_h[0]:rows_h[0] + rows_h[1], :]))
                fw.dma("sp", fns, Bkst, reads=[Bkst])

            with ExitStack() as pp:
                QT = sbt(pp, "QT", [128, 2, 4, 512], BF16); BQT = Buf("QT")
                fw.op("pool", lambda e: e.memset(QT[:], 0.0), writes=[BQT])
                KT = sbt(pp, "KT", [128, 4, SEQ], BF16); BKT = Buf("KT")
                Vaug = sbt(pp, "Vaug", [128, 16, 8, 66], BF16); BV = Buf("Vaug")
                Mbig = sbt(pp, "Mbig", [128, 8, SEQ], BF16); BM = Buf("Mbig")
                Eb = [sbt(pp, "Eb%d" % i, [128, 512], BF16) for i in range(2)]; BEb = [Buf("Eb%d" % i) for i in range(2)]
                PT = [sbt(pp, "PT%d" % i, [128, 512], BF16) for i in range(3)]; BPT = [Buf("PT%d" % i) for i in range(3)]
                jmat = sbt(pp, "jmat", [128, 128], BF16); Bjm = Buf("jmat")

                def gen_mask():
                    relb_sb, Brelb = tmp2[0][0:32, 0:8], Btmp2[0]
                    oh_sb, Boh = t_r2[0][0:32, :], Btr2[0]
                    mult_sb, Bmult = t_g2[0][0:8, :], Btg2[0]
                    tabf, Btabf = a_t2[0][0:8, :], Bat2[0]
                    tabb, Btabb = mixT[0:8].rearrange("p k t -> p (k t)")[:, 0:TBL], BmixT
                    fw.dma("sp", lambda e: e.dma_start(out=relb_sb, in_=relb), Brelb, writes=[Brelb])
                    fw.dma("sp", lambda e: e.dma_start(out=jmat[:], in_=jmat_d), Bjm, writes=[Bjm])
                    fw.op("pool", lambda e: e.memset(Vaug[:, :, :, 64:65], 1.0), writes=[BV])
                    for i0 in range(0, TBL, 512):
                        n = min(512, TBL - i0)
                        fw.dma("sp", lambda e, i0=i0, n=n: e.dma_start(out=oh_sb[:, 0:n], in_=onehot[:, i0:i0 + n]), Boh, writes=[Boh])
                        fw.dma("sp", lambda e, i0=i0, n=n: e.dma_start(out=mult_sb[:, 0:n], in_=mult[:, i0:i0 + n]), Bmult, writes=[Bmult])
                        ps, Bps = mm_bank()
                        fw.op("pe", lambda e, ps=ps, n=n: e.matmul(ps[0:8, 0:n], lhsT=relb_sb, rhs=oh_sb[:, 0:n], start=True, stop=True),
                              reads=[Brelb, Boh], writes=[Bps])
                        fw.op("act", lambda e, ps=ps, n=n: e.activation(out=tabf[:, 0:n], in_=ps[0:8, 0:n], func=AF.Exp), reads=[Bps], writes=[Btabf])
                        fw.op("dve", lambda e, i0=i0, n=n: e.tensor_tensor(out=tabb[:, i0:i0 + n], in0=tabf[:, 0:n], in1=mult_sb[:, 0:n], op=ALU.mult),
                              reads=[Btabf, Bmult], writes=[Btabb])
                        yield
                    fw.dma("sp", lambda e: e.dma_start(out=tblscr, in_=tabb), Btabb, reads=[Btabb], writes=[Btbl])
                    k = 0
                    for h in range(8):
                        for hf in range(4):
                            mrev, Bmrev = PT[k % 2], BPT[k % 2]
                            k += 1
                            fw.dma("sp", lambda e, h=h, hf=hf, mrev=mrev: e.dma_start(out=mrev[:], in_=bass.AP(tblscr.tensor, h * TBL + 1 + hf * 512, [[1, 128], [1, 512]])),
                                   Bmrev, reads=[Btbl], writes=[Bmrev])
                            ps, Bps = mm_bank()
                            fw.op("pe", lambda e, ps=ps, mrev=mrev: e.matmul(ps[:], lhsT=jmat[:], rhs=mrev[:], start=True, stop=True),
                                  reads=[Bjm, Bmrev], writes=[Bps])
                            fw.op("act", lambda e, ps=ps, h=h, hf=hf: e.activation(out=Mbig[:, h, hf * 512:(hf + 1) * 512], in_=ps[:], func=AF.Copy),
                                  reads=[Bps], writes=[BM])
                            yield
                mask_gen = gen_mask()

                pending_tail = []
                for b in range(DBG['nseq']):
                    fw.op("pool", lambda e: e.memset(xr[:, :, 0:3], 0.0), writes=Bxr)
                    fw.op("pool", lambda e: e.memset(hstate[:], 0.0), writes=[Bhst])
                    for T in range(DBG['ntile']):
                        g0 = b * SEQ + T * 512
                        def gen_norm(g0):
                            for blk in range(4):
                                pre = (blk < 2) and not (g0 == 0)
                                norm_transpose(None if pre else xp[g0 + blk * 128:g0 + (blk + 1) * 128, :], blk % 2, G1, nT, BnT, blk * 128)
                                yield
                            gn = g0 + 512
                            if gn < NTP:
                                for blk in range(2):
                                    fw.dma("sp", lambda e, gn=gn, blk=blk: e.dma_start(out=xblk[blk][:], in_=xp[gn + blk * 128:gn + (blk + 1) * 128, :]),
                                           Bxblk[blk], writes=[Bxblk[blk]])

                        def gen_head(b=b, T=T, g0=g0):
                            if g0 == 0:
                                yield from gen_norm(g0)
                            for c in range(4):
                                def ev_q(ps, Bps, c=c):
                                    for hp in range(2):
                                        fw.op("act", lambda e, hp=hp: e.activation(out=QT[hp * 64:(hp + 1) * 64, hp, c, :], in_=ps[hp * 64:(hp + 1) * 64, :],
                                                                                   func=AF.Copy, scale=0.125), reads=[Bps], writes=[BQT])
                                proj_fm(c * 128, 512, ev_q)
                                yield
                            for c in range(4):
                                def ev_k(ps, Bps, c=c, T=T):
                                    fw.op("dve", lambda e: e.tensor_copy(out=KT[:, c, T * 512:(T + 1) * 512], in_=ps[:]), reads=[Bps], writes=[BKT])
                                proj_fm(512 + c * 128, 512, ev_k)
                                yield

                        gh = gen_head()
                        if b == 0 and T == 0:
                            alive_h, alive_m = True, True
                            while alive_h or alive_m:
                                if alive_h:
                                    try:
                                        next(gh)
                                    except StopIteration:
                                        alive_h = False
                                for _ in range(2):
                                    if alive_m:
                                        try:
                                            next(mask_gen)
                                        except StopIteration:
                                            alive_m = False
                        else:
                            for _ in gh:
                                pass
                        for fn_ in pending_tail:
                            fn_()
                        del pending_tail[:]
                        stop_here('qk')
                        def gen_kv(T=T, g0=g0):
                            for blk in range(4):
                                r0 = g0 + blk * 128
                                ps, Bps = mm_bank()
                                for kc in range(8):
                                    fw.op("pe", lambda e, kc=kc, ps=ps, blk=blk: e.matmul(ps[:], lhsT=nT[:, kc, blk * 128:(blk + 1) * 128], rhs=w_in_bf[:, kc, 512:1024],
                                                                                           start=(kc == 0), stop=(kc == 7)), reads=[Bwin[1], BnT], writes=[Bps], pe_accum=True)
                                fw.op("dve", lambda e, ps=ps: e.tensor_copy(out=kst[:], in_=ps[:]), reads=[Bps], writes=[Bkst])
                                fw.dma("sp", lambda e, r0=r0: e.dma_start(out=nkp[r0:r0 + 128, :], in_=kst[:]), Bkst, reads=[Bkst])
                                yield
                                ps, Bps = mm_bank()
                                for kc in range(8):
                                    fw.op("pe", lambda e, kc=kc, ps=ps, blk=blk: e.matmul(ps[:], lhsT=nT[:, kc, blk * 128:(blk + 1) * 128], rhs=w_in_bf[:, kc, 1024:1536],
                                                                                           start=(kc == 0), stop=(kc == 7)), reads=[Bwin[2], BnT], writes=[Bps], pe_accum=True)
                                fw.op("act", lambda e, ps=ps: e.activation(out=vst[:], in_=ps[:], func=AF.Copy), reads=[Bps], writes=[Bvst])
                                if not DBG.get('novaug'):
                                    fw.op("dve", lambda e, blk=blk, T=T: e.tensor_copy(out=Vaug[:, T * 4 + blk, :, 0:64], in_=vst[:].rearrange("p (h d) -> p h d", h=8)),
                                          reads=[Bvst], writes=[BV])
                                fw.dma("sp", lambda e, r0=r0: e.dma_start(out=nvp[r0:r0 + 128, :], in_=vst[:]), Bvst, reads=[Bvst])
                                yield
                        stop_here('kv')
                        last = (T == 3)

                        def scan_p(c):
                            fw.op("dve", lambda e: e.tensor_tensor_scan(out=hs[:], data0=a_t2[c % 2][:], data1=t_g2[c % 2][:], initial=hstate[:, c:c + 1], op0=ALU.mult, op1=ALU.add),
                                  reads=[Bat2[c % 2], Btg2[c % 2], Bhst], writes=[Bhs])
                            fw.op("dve", lambda e: e.tensor_copy(out=hstate[:, c:c + 1], in_=hs[:, 511:512]), reads=[Bhs], writes=[Bhst])

                        def fin_p(c):
                            if last:
                                fw.op("dve", lambda e: e.tensor_copy(out=fin[:, c, 0:3], in_=xr[:, c, 512:515]), reads=[Bxr[c]], writes=[Bfin])
                                fw.op("dve", lambda e: e.tensor_copy(out=fin[:, c, 3:4], in_=hs[:, 511:512]), reads=[Bhs], writes=[Bfin])
                            else:
                                fw.op("dve", lambda e: e.tensor_copy(out=xr[:, c, 0:3], in_=xr[:, c, 512:515]), reads=[Bxr[c]], writes=[Bxr[c]])
                        def gen_rnn(b=b, last=last):
                            for c0 in (0, 2):
                                gens = [rnn_chunk(c, 512, None, xr[:, c, 3:515], [xr[:, c, j:j + 512] for j in range(4)], scan_p, fin_p)
                                        for c in (c0, c0 + 1)]
                                alive = [True, True]
                                for _ in range(2):
                                    next(gens[0])
                                    yield
                                while alive[0] or alive[1]:
                                    for gi in range(2):
                                        if alive[gi]:
                                            try:
                                                next(gens[gi])
                                            except StopIteration:
                                                alive[gi] = False
                                    yield
                            rnn_norm(512, 0)
                            yield
                            if last:
                                fin_out(4, [(0, 3, ncp[b])], (3, 1), None, nhp[b:b + 1, :])
                                yield

                        def gen_attn(T=T):
                            its = [(h, kb) for h in range(8) for kb in range(4 * T + 4)]

                            def bufs(i):
                                return (pbank[2 + i % 2], Bpb[2 + i % 2], Eb[i % 2], BEb[i % 2], PT[i % 3], BPT[i % 3])

                            def emit_S(i):
                                h, kb = its[i]
                                c, hp = h // 2, h % 2
                                c0 = max(0, 128 * kb - T * 512)
                                s_ps, Bs_ps = bufs(i)[0:2]
                                fw.op("pe", lambda e, s_ps=s_ps, c=c, hp=hp, kb=kb, c0=c0: e.matmul(
                                    s_ps[:, c0:512], lhsT=KT[:, c, kb * 128:(kb + 1) * 128], rhs=QT[:, hp, c, c0:512],
                                    start=True, stop=True), reads=[BKT, BQT], writes=[Bs_ps])

                            def emit_mid(i):
                                h, kb = its[i]
                                c0 = max(0, 128 * kb - T * 512)
                                s_ps, Bs_ps, E_, BE_, P_, BP_ = bufs(i)
                                fw.op("act", lambda e, s_ps=s_ps, E_=E_, c0=c0: e.activation(out=E_[:, c0:512], in_=s_ps[:, c0:512], func=AF.Exp),
                                      reads=[Bs_ps], writes=[BE_])
                                j0 = T * 512 + c0 - 128 * kb
                                fw.op("dve", lambda e, E_=E_, P_=P_, c0=c0, j0=j0, h=h: e.tensor_tensor(
                                    out=P_[:, c0:512], in0=E_[:, c0:512], in1=Mbig[:, h, j0:j0 + 512 - c0], op=ALU.mult), reads=[BE_, BM], writes=[BP_])

                            def emit_pv(i):
                                h, kb = its[i]
                                c0 = max(0, 128 * kb - T * 512)
                                s_ps, Bs_ps, E_, BE_, P_, BP_ = bufs(i)
                                acc, Bacc = pbank[4 + h % 2], Bpb[4 + h % 2]
                                first = (kb == 0)
                                for ii in range(c0 // 128, 4):
                                    fw.op("pe", lambda e, acc=acc, P_=P_, ii=ii, kb=kb, h=h, first=first: e.matmul(
                                        acc[:, ii * 65:(ii + 1) * 65], lhsT=P_[:, ii * 128:(ii + 1) * 128], rhs=Vaug[:, kb, h, 0:65],
                                        start=first, stop=False, skip_group_check=True), reads=[BP_, BV], writes=[Bacc], pe_accum=True)
                                    first = False
                                if kb == 4 * T + 3:
                                    accv = acc[:, 0:260].rearrange("p (i d) -> p i d", d=65)
                                    fw.op("dve", lambda e, accv=accv: e.reciprocal(out=rec[:].unsqueeze(2), in_=accv[:, :, 64:65]), reads=[Bacc], writes=[Brec])
                                    fw.op("dve", lambda e, accv=accv, h=h: e.tensor_tensor(
                                        out=att[:, :, h * 64:(h + 1) * 64], in0=accv[:, :, 0:64], in1=rec[:].unsqueeze(2).to_broadcast([128, 4, 64]), op=ALU.mult),
                                        reads=[Bacc, Brec], writes=Batt)

                            n_it = len(its)
                            emit_S(0)
                            for i in range(n_it + 1):
                                if i + 1 < n_it:
                                    emit_S(i + 1)
                                if i < n_it:
                                    emit_mid(i)
                                if i >= 1:
                                    emit_pv(i - 1)
                                yield

                        if DBG.get('interleave', 1):
                            def gen_side(g0=g0):
                                yield from gen_rnn()
                                if g0 + 512 < NTP:
                                    yield from gen_norm(g0 + 512)
                            gr = gen_side()
                            ga = gen_attn()
                            gk = gen_kv()

                            def step(g):
                                try:
                                    next(g)
                                    return True
                                except StopIteration:
                                    return False
                            if T == 0:
                                while step(gk):
                                    step(gr)
                            else:
                                per = -(-8 // (4 * T))
                                for _ in range(4 * T):
                                    step(ga)
                                    for _ in range(per):
                                        step(gk)
                                    step(gr)
                                while step(gk):
                                    pass
                            n_att = 8 * (4 * T + 4)
                            kstep = 1
                            astep = max(1, n_att // 30)
                            alive_a, alive_r = True, True
                            while alive_a or alive_r:
                                for _ in range(astep):
                                    if alive_a:
                                        try:
                                            next(ga)
                                        except StopIteration:
                                            alive_a = False
                                for _ in range(kstep if alive_a else 4):
                                    if alive_r:
                                        try:
                                            next(gr)
                                        except StopIteration:
                                            alive_r = False
                        else:
                            for _ in gen_kv():
                                pass
                            for _ in gen_rnn():
                                pass
                            for _ in gen_attn():
                                pass
                            if g0 + 512 < NTP:
                                for _ in gen_norm(g0 + 512):
                                    pass
                        stop_here('attn')

                        def tile_tail(g0=g0):
                            for i in range(4):
                                att_norm_block(att[:, i, :], Batt[i], 128, i * 128)
                            fw.dma("sp", lambda e, g0=g0: e.dma_start(out=mixscr[:, :, g0:g0 + 512].rearrange("k p t -> p k t"), in_=mixT[:]),
                                   BmixT, reads=[BmixT], writes=[Bmix])
                        pending_tail.append(tile_tail)
                for fn_ in pending_tail:
                    fn_()
                del pending_tail[:]
                fw.barrier()
                fw.run()

            stop_here('prompt')
            with ExitStack() as sp_:
                Qbd = sbt(sp_, "Qbd", [128, 4, NSS, 16], BF16); BQbd = Buf("Qbd")
                onesbb = sbt(sp_, "onesbb", [128, 128], BF16); Bonesbb = Buf("onesbb")
                attT = sbt(sp_, "attT", [128, 4, 128], F32); BattT = [Buf("attT%d" % c) for c in range(4)]
                rd = sbt(sp_, "rd", [128, 64], F32); Brd = Buf("rd")
                fw.op("pool", lambda e: e.memset(Qbd[:], 0.0), writes=[BQbd])
                fw.op("pool", lambda e: e.memset(onesbb[:], 1.0), writes=[Bonesbb])
                KTn = sbt(sp_, "KTn", [128, 4, 128], BF16); BKTn = Buf("KTn")
                KTs = [sbt(sp_, "KTs%d" % i, [128, 4, NKEEP], BF16) for i in range(2)]; BKTs = [Buf("KTs%d" % i) for i in range(2)]
                Vs = [sbt(sp_, "Vs%d" % i, [128, NKB, 512], BF16) for i in range(2)]; BVs = [Buf("Vs%d" % i) for i in range(2)]
                Vn = sbt(sp_, "Vn", [8, 512], BF16); BVn = Buf("Vn")
                Ms = sbt(sp_, "Ms", [128, 17, 64], BF16); BMs = Buf("Ms")
                Mc = sbt(sp_, "Mc", [128, NKB + 1, 64], BF16); BMc = Buf("Mc")
                sel = [sbt(sp_, "sel%d" % i, [128, 128], BF16) for i in range(2)]; Bsel = [Buf("sel%d" % i) for i in range(2)]
                fw.dma("sp", lambda e: e.dma_start(out=sel[0][:], in_=sel0_d), Bsel[0], writes=[Bsel[0]])
                fw.dma("sp", lambda e: e.dma_start(out=sel[1][:], in_=sel1_d), Bsel[1], writes=[Bsel[1]])
                mrevs = sbt(sp_, "mrevs", [128, 17, 64], BF16); Bmrevs = Buf("mrevs")
                jmat2 = sbt(sp_, "jmat2", [128, 128], BF16); Bjm2 = Buf("jmat2")
                Es = [sbt(sp_, "Es%d" % i, [128, 64], F32) for i in range(2)]; BEs = [Buf("Es%d" % i) for i in range(2)]
                Ps = [sbt(sp_, "Ps%d" % i, [128, 64], BF16) for i in range(2)]; BPs = [Buf("Ps%d" % i) for i in range(2)]
                atts = sbt(sp_, "atts", [8, 512], F32); Batts = Buf("atts")
                recs = sbt(sp_, "recs", [8, 8], F32); Brecs = Buf("recs")

                fw.dma("sp", lambda e: e.dma_start(out=jmat2[:], in_=jmat_d), Bjm2, writes=[Bjm2])
                fns = []
                for kb in range(17):
                    fns.append(lambda e, kb=kb: e.dma_start(out=mrevs[:, kb, :].rearrange("p (h t) -> p h t", t=TS),
                                                            in_=bass.AP(tblscr.tensor, 2049 - 128 * kb, [[1, 128], [TBL, 8], [1, TS]])))
                fw.dma("sp", fns, Bmrevs, reads=[Btbl], writes=[Bmrevs])
                mflat = mrevs[:].rearrange("p k x -> p (k x)")
                Mflat = Ms[:].rearrange("p k x -> p (k x)")
                for i0 in range(0, 17 * 64, 512):
                    n = min(512, 17 * 64 - i0)
                    ps, Bps = mm_bank()
                    fw.op("pe", lambda e, ps=ps, i0=i0, n=n: e.matmul(ps[:, 0:n], lhsT=jmat2[:], rhs=mflat[:, i0:i0 + n], start=True, stop=True),
                          reads=[Bjm2, Bmrevs], writes=[Bps])
                    fw.op("act", lambda e, ps=ps, i0=i0, n=n: e.activation(out=Mflat[:, i0:i0 + n], in_=ps[:, 0:n], func=AF.Copy), reads=[Bps], writes=[BMs])

                stop_here('smask')
                for m in range(6):
                    ps, Bps = mm_bank()
                    for t2 in range(2):
                        fw.op("pe", lambda e, ps=ps, m=m, t2=t2: e.matmul(ps[:, 0:64], lhsT=sel[t2][:], rhs=Ms[:, 2 * m + t2, :], start=(t2 == 0), stop=(t2 == 1)),
                              reads=[Bsel[t2], BMs], writes=[Bps], pe_accum=True)
                    fw.op("act", lambda e, ps=ps, m=m: e.activation(out=Mc[:, m, :], in_=ps[:, 0:64], func=AF.Copy), reads=[Bps], writes=[BMc])
                fw.op("dve", lambda e: e.tensor_copy(out=Mc[:, 6:11, :], in_=Ms[:, 12:17, :]), reads=[BMs], writes=[BMc])
                def load_cache(s):
                    sl = s % 2
                    fw.dma("pool", lambda e: e.dma_start(out=KTs[sl][:], in_=ckT[s].rearrange("c p t -> p c t")), BKTs[sl], writes=[BKTs[sl]])
                    fw.dma("pool", lambda e: e.dma_start(out=Vs[sl][:], in_=cv[s].rearrange("(kb p) f -> p kb f", p=128)), BVs[sl], writes=[BVs[sl]])
                load_cache(0)
                load_cache(1)

                g0 = NTP
                norm_transpose(xs[:, :], 0, G1, nT, BnT, 0)
                for c in range(4):
                    def ev_q(ps, Bps, c=c):
                        for hp in range(2):
                            fw.op("act", lambda e, hp=hp: e.activation(out=Qbd[hp * 64:(hp + 1) * 64, c, :, hp * 8:(hp + 1) * 8],
                                                                       in_=ps[hp * 64:(hp + 1) * 64, 0:128].rearrange("p (s t) -> p s t", t=TS),
                                                                       func=AF.Copy, scale=0.125), reads=[Bps], writes=[BQbd])
                    proj_fm(c * 128, 128, ev_q)
                for c in range(4):
                    def ev_k(ps, Bps, c=c):
                        fw.op("dve", lambda e: e.tensor_copy(out=KTn[:, c, :], in_=ps[:, 0:128]), reads=[Bps], writes=[BKTn])
                    proj_fm(512 + c * 128, 128, ev_k)
                for (w0, dst, stg, Bstg) in ((512, nks, kst, Bkst), (1024, nvs, vst, Bvst)):
                    ps, Bps = mm_bank()
                    for kc in range(8):
                        fw.op("pe", lambda e, kc=kc, ps=ps, w0=w0: e.matmul(ps[:], lhsT=nT[:, kc, 0:128], rhs=w_in_bf[:, kc, w0:w0 + 512],
                                                                           start=(kc == 0), stop=(kc == 7)), reads=[Bwin[w0 // 512], BnT], writes=[Bps], pe_accum=True)
                    fw.op("act", lambda e, ps=ps, stg=stg: e.activation(out=stg[:], in_=ps[:], func=AF.Copy), reads=[Bps], writes=[Bstg])
                    fw.dma("sp", lambda e, dst=dst, stg=stg: e.dma_start(out=dst, in_=stg[:]), Bstg, reads=[Bstg])
                stop_here('sproj')
                scs = sbt(sp_, "scs", [128, 4, NSS, 3], F32); Bscs = Buf("scs")
                fw.dma("sp", lambda e: e.dma_start(out=scs[:], in_=sconvT), Bscs, writes=[Bscs])
                for c in range(4):
                    fw.op("pool", lambda e, c=c: e.tensor_copy(out=xr[:, c, 0:176].rearrange("p (s j) -> p s j", j=11)[:, :, 0:3], in_=scs[:, c, :, :]),
                          reads=[Bscs], writes=[Bxr[c]])
                shs = sbt(sp_, "shs", [128, 4, NSS], F32); Bshs = Buf("shs")
                fw.dma("sp", lambda e: e.dma_start(out=shs[:], in_=shT), Bshs, writes=[Bshs])

                def scan_s(c):
                    for s in range(NSS):
                        fw.op("dve", lambda e, s=s: e.tensor_tensor_scan(out=hs[:, s * 8:(s + 1) * 8], data0=a_t2[c % 2][:, s * 8:(s + 1) * 8], data1=t_g2[c % 2][:, s * 8:(s + 1) * 8],
                                                                        initial=shs[:, c, s:s + 1], op0=ALU.mult, op1=ALU.add),
                              reads=[Bat2[c % 2], Btg2[c % 2], Bshs], writes=[Bhs])

                def fin_s(c):
                    xv = xr[:, c, 0:176].rearrange("p (s j) -> p s j", j=11)
                    fw.op("dve", lambda e: e.tensor_copy(out=fin[:, c, 0:48].rearrange("p (j s) -> p j s", s=NSS), in_=xv[:, :, 8:11].rearrange("p s j -> p j s")),
                          reads=[Bxr[c]], writes=[Bfin])
                    fw.op("dve", lambda e: e.tensor_copy(out=fin[:, c, 48:64], in_=hs[:, 0:128].rearrange("p (s t) -> p s t", t=TS)[:, :, 7]),
                          reads=[Bhs], writes=[Bfin])
                def gen_srnn():
                    for c in range(4):
                        xv = xr[:, c, 0:176].rearrange("p (s j) -> p s j", j=11)
                        yield from rnn_chunk(c, 128, True, xv[:, :, 3:11], [xv[:, :, j:j + 8] for j in range(4)], scan_s, fin_s)
                    rnn_norm(128, 0)
                    yield
                    fin_out(64, [(j * 16, 16, ncs[:, j, :]) for j in range(3)], (48, 16), None, nhs)
                    yield
                srnn = gen_srnn()
                srnn_alive = [True]

                def srnn_step():
                    if srnn_alive[0]:
                        try:
                            next(srnn)
                        except StopIteration:
                            srnn_alive[0] = False
                for s in range(NSS):
                    sl = s % 2
                    ps, Bps = pbank[0], Bpb[0]
                    for kc in range(8):
                        fw.op("pe", lambda e, kc=kc, ps=ps, s=s: e.matmul(ps[0:8, :], lhsT=nT[:, kc, s * 8:(s + 1) * 8], rhs=w_in_bf[:, kc, 1024:1536],
                                                                         start=(kc == 0), stop=(kc == 7)), reads=[Bwin[2], BnT], writes=[Bps], pe_accum=True)
                    fw.op("act", lambda e, ps=ps: e.activation(out=Vn[:], in_=ps[0:8, :], func=AF.Copy), reads=[Bps], writes=[BVn])
                    accb, Baccb = pbank[4 + s % 2], Bpb[4 + s % 2]

                    def s_emit_S(kb, s=s, sl=sl):
                        npart = 128 if kb < NKB else 8
                        s_ps, Bs_ps = pbank[2 + kb % 2], Bpb[2 + kb % 2]
                        for c in range(4):
                            if kb < NKB:
                                lhs = KTs[sl][:, c, kb * 128:(kb + 1) * 128]
                                rdl = [BKTs[sl], BQbd]
                            else:
                                lhs = KTn[:, c, s * 8:(s + 1) * 8]
                                rdl = [BKTn, BQbd]
                            fw.op("pe", lambda e, s_ps=s_ps, lhs=lhs, c=c, s=s, npart=npart: e.matmul(
                                s_ps[0:npart, c * 16:(c + 1) * 16], lhsT=lhs, rhs=Qbd[:, c, s, :],
                                start=True, stop=True, skip_group_check=True), reads=rdl, writes=[Bs_ps], pe_accum=True)

                    def s_emit_rest(kb, s=s, sl=sl, accb=accb, Baccb=Baccb):
                        npart = 128 if kb < NKB else 8
                        s_ps, Bs_ps = pbank[2 + kb % 2], Bpb[2 + kb % 2]
                        E_, BE_ = Es[kb % 2], BEs[kb % 2]
                        P_, BP_ = Ps[kb % 2], BPs[kb % 2]
                        fw.op("act", lambda e, s_ps=s_ps, E_=E_, npart=npart: e.activation(out=E_[0:npart, :], in_=s_ps[0:npart, 0:64], func=AF.Exp),
                              reads=[Bs_ps], writes=[BE_])
                        fw.op("dve", lambda e, E_=E_, P_=P_, kb=kb, npart=npart: e.tensor_tensor(out=P_[0:npart, :], in0=E_[0:npart, :], in1=Mc[0:npart, kb, :], op=ALU.mult),
                              reads=[BE_, BMc], writes=[BP_])
                        for cp in range(4):
                            if kb < NKB:
                                lhs = Vs[sl][:, kb, cp * 128:(cp + 1) * 128]
                                rdv = BVs[sl]
                            else:
                                lhs = Vn[0:8, cp * 128:(cp + 1) * 128]
                                rdv = BVn
                            fw.op("pe", lambda e, P_=P_, cp=cp, lhs=lhs, npart=npart, f=(kb == 0 and cp == 0): e.matmul(
                                accb[:, cp * 64:(cp + 1) * 64], lhsT=lhs, rhs=P_[0:npart, :], start=f, stop=False, skip_group_check=True),
                                reads=[BP_, rdv], writes=[Baccb], pe_accum=True)
                        fw.op("pe", lambda e, P_=P_, npart=npart: e.matmul(
                            accb[:, 256:320], lhsT=onesbb[0:npart, :], rhs=P_[0:npart, :], start=False, stop=False, skip_group_check=True),
                            reads=[BP_, Bonesbb], writes=[Baccb], pe_accum=True)

                    s_emit_S(0)
                    for kb in range(NKB + 1):
                        if kb + 1 < NKB + 1:
                            s_emit_S(kb + 1)
                        s_emit_rest(kb)
                        if kb % 2 == 1:
                            srnn_step()
                    if s + 2 < NSS:
                        load_cache(s + 2)
                    fw.op("dve", lambda e, accb=accb: e.reciprocal(out=rd[:], in_=accb[:, 256:320]), reads=[Baccb], writes=[Brd])
                    for hp in range(2):
                        fw.op("dve", lambda e, accb=accb, hp=hp, s=s: e.tensor_tensor(
                            out=attT[hp * 64:(hp + 1) * 64, :, s * 8:(s + 1) * 8],
                            in0=accb[hp * 64:(hp + 1) * 64, 0:320].rearrange("p (c x) -> p c x", x=80)[:, :, hp * 8:hp * 8 + 8],
                            in1=rd[hp * 64:(hp + 1) * 64, :].rearrange("p (c x) -> p c x", x=16)[:, :, hp * 8:hp * 8 + 8], op=ALU.mult),
                            reads=[Baccb, Brd], writes=BattT)
                while srnn_alive[0]:
                    srnn_step()
                grp_norm(attT, BattT, ATG, 0, 128, 0)
                fw.dma("sp", lambda e: e.dma_start(out=mixscr[:, :, NTP:NTP + 128].rearrange("k p t -> p k t"), in_=mixT[:, :, 0:128]),
                       BmixT, reads=[BmixT], writes=[Bmix])
                fw.barrier()
                fw.run()

        stop_here('sample')
        with ExitStack() as p2:
            TB = 3
            NTK = TB * 128
            w_out_bf = sbt(p2, "w_out_bf", [128, 8, D], BF16); Bwo = Buf("w_out_bf")
            w_mi_bf = sbt(p2, "w_mi_bf", [128, 8, DFF], BF16); Bwmi = [Buf("w_mi_bf%d" % q) for q in range(4)]
            w_mo_bf = sbt(p2, "w_mo_bf", [128, 32, D], BF16); Bwmo = [Buf("w_mo_bf%d" % q) for q in range(4)]
            fgB = sbt(p2, "fgB", [128, D], F32); BfgB = Buf("fgB")
            mix2 = sbt(p2, "mix2", [128, 8, NTK], BF16); Bmix2 = Buf("mix2")
            xmid = sbt(p2, "xmid", [128, TB, D], F32); Bxmid = [Buf("xmid%d" % i) for i in range(TB)]
            ss2 = sbt(p2, "ss2", [128, 2], F32); Bss2 = Buf("ss2")
            xn2 = sbt(p2, "xn2", [128, D], BF16); Bxn2 = Buf("xn2")
            n2T = sbt(p2, "n2T", [128, 8, NTK], BF16); Bn2T = Buf("n2T")
            hT = sbt(p2, "hT", [128, 32, NTK], BF16); BhT = Buf("hT")
            rl2 = [sbt(p2, "rl%d" % i, [128, NTK], F32) for i in range(2)]; Brl2 = [Buf("rl%d" % i) for i in range(2)]
            yst = sbt(p2, "yst", [128, D], F32); Byst = Buf("yst")

            fw.dma("pool", lambda e: e.dma_start(out=w_out_bf[:], in_=w_out.rearrange("(kc p) n -> p kc n", p=128)), Bwo, writes=[Bwo])
            for q in range(4):
                fw.dma("pool", lambda e, q=q: e.dma_start(out=w_mi_bf[:, :, q * 1024:(q + 1) * 1024],
                                                          in_=w_mi.rearrange("(kc p) n -> p kc n", p=128)[:, :, q * 1024:(q + 1) * 1024]), Bwmi[q], writes=[Bwmi[q]])
            for q in range(4):
                fw.dma("pool", lambda e, q=q: e.dma_start(out=w_mo_bf[:, q * 8:(q + 1) * 8, :],
                                                          in_=w_mo.rearrange("(kc p) n -> p kc n", p=128)[:, q * 8:(q + 1) * 8, :]), Bwmo[q], writes=[Bwmo[q]])
            fw.dma("sp", lambda e: e.dma_start(out=fgB[:], in_=fg.partition_broadcast(128)), BfgB, writes=[BfgB])

            NBLK = NTOK // 128

            def xsrc_of(g):
                return xp[g * 128:(g + 1) * 128, :] if g < NTP // 128 else xs[:, :]

            def ydst_of(g):
                return yp[g * 128:(g + 1) * 128, :] if g < NTP // 128 else ys[:, :]

            def load_tile_inputs(i):
                fw.dma("sp", lambda e, i=i: e.dma_start(out=mix2[:], in_=mixscr[:, :, i * NTK:(i + 1) * NTK].rearrange("k p t -> p k t")),
                       Bmix2, reads=[Bmix], writes=[Bmix2])

            def load_x(i, j):
                g = i * TB + j
                fw.dma("sp", lambda e, g=g, j=j: e.dma_start(out=xmid[:, j, :], in_=xsrc_of(g)), Bxmid[j], writes=[Bxmid[j]])

            ntiles = NBLK // TB
            load_tile_inputs(0)
            for j in range(TB):
                load_x(0, j)
            for i in range(ntiles):
                for j in range(TB):
                    for hf in range(2):
                        ps, Bps = pbank[hf], Bpb[hf]
                        for kc in range(8):
                            fw.op("pe", lambda e, ps=ps, kc=kc, j=j, hf=hf: e.matmul(
                                ps[:], lhsT=mix2[:, kc, j * 128:(j + 1) * 128], rhs=w_out_bf[:, kc, hf * 512:(hf + 1) * 512],
                                start=(kc == 0), stop=(kc == 7)), reads=[Bmix2, Bwo], writes=[Bps], pe_accum=True)
                        fw.op("dve", lambda e, ps=ps, j=j, hf=hf: e.tensor_tensor(out=xmid[:, j, hf * 512:(hf + 1) * 512], in0=ps[:],
                                                                                  in1=xmid[:, j, hf * 512:(hf + 1) * 512], op=ALU.add),
                              reads=[Bps, Bxmid[j]], writes=[Bxmid[j]])
                if i + 1 < ntiles:
                    load_tile_inputs(i + 1)
                for j in range(TB):
                    rms_rstd(xmid[:, j, :], Bxmid[j], xn2[:], Bxn2, ss2[:], Bss2, D)
                    fw.op("dve", lambda e, j=j: e.tensor_scalar(out=xn2[:], in0=xmid[:, j, :], scalar1=ss2[:, 0:1], scalar2=None, op0=ALU.mult),
                          reads=[Bxmid[j], Bss2], writes=[Bxn2])
                    for kc in range(8):
                        fw.op("pe", lambda e, kc=kc: e.transpose(out=ptp[:, kc * 128:(kc + 1) * 128], in_=xn2[:, kc * 128:(kc + 1) * 128], identity=identb[:]),
                              reads=[Bxn2, Bidb], writes=[Bptp], pe_accum=True)
                    fw.op("dve", lambda e, j=j: e.tensor_tensor(
                        out=n2T[:, :, j * 128:(j + 1) * 128], in0=ptp[:].rearrange("p (k j) -> p k j", k=8),
                        in1=vec[:, G2:G2 + 8].unsqueeze(2).to_broadcast([128, 8, 128]), op=ALU.mult), reads=[Bptp, Bvec], writes=[Bn2T])
                for fc in range(32):
                    ps, Bps = pbank[2 + fc % 2], Bpb[2 + fc % 2]
                    for kc in range(8):
                        fw.op("pe", lambda e, ps=ps, kc=kc, fc=fc: e.matmul(ps[:, 0:NTK], lhsT=w_mi_bf[:, kc, fc * 128:(fc + 1) * 128], rhs=n2T[:, kc, :],
                                                                            start=(kc == 0), stop=(kc == 7)), reads=[Bwmi[fc // 8], Bn2T], writes=[Bps], pe_accum=True)
                    rl, Brl = rl2[fc % 2], Brl2[fc % 2]
                    fw.op("act", lambda e, ps=ps, rl=rl: e.activation(out=rl[:], in_=ps[:, 0:NTK], func=AF.Relu), reads=[Bps], writes=[Brl])
                    fw.op("dve", lambda e, fc=fc, rl=rl: e.tensor_tensor(out=hT[:, fc, :], in0=rl[:], in1=rl[:], op=ALU.mult), reads=[Brl], writes=[BhT])
                for j in range(TB):
                    g = i * TB + j
                    for hf in range(2):
                        ps, Bps = pbank[4 + hf], Bpb[4 + hf]
                        for fc in range(32):
                            fw.op("pe", lambda e, ps=ps, fc=fc, j=j, hf=hf: e.matmul(
                                ps[:], lhsT=hT[:, fc, j * 128:(j + 1) * 128], rhs=w_mo_bf[:, fc, hf * 512:(hf + 1) * 512],
                                start=(fc == 0), stop=(fc == 31)), reads=[BhT, Bwmo[fc // 8]], writes=[Bps], pe_accum=True)
                        fw.op("dve", lambda e, ps=ps, j=j, hf=hf: e.tensor_tensor(out=xmid[:, j, hf * 512:(hf + 1) * 512], in0=ps[:],
                                                                                  in1=xmid[:, j, hf * 512:(hf + 1) * 512], op=ALU.add),
                              reads=[Bps, Bxmid[j]], writes=[Bxmid[j]])
                    rms_rstd(xmid[:, j, :], Bxmid[j], xn2[:], Bxn2, ss2[:], Bss2, D)
                    fw.op("dve", lambda e, j=j: e.scalar_tensor_tensor(out=yst[:], in0=xmid[:, j, :], scalar=ss2[:, 0:1], in1=fgB[:],
                                                                      op0=ALU.mult, op1=ALU.mult), reads=[Bxmid[j], Bss2, BfgB], writes=[Byst])
                    fw.dma("sp", lambda e, g=g: e.dma_start(out=ydst_of(g), in_=yst[:]), Byst, reads=[Byst])
                    if i + 1 < ntiles:
                        load_x(i + 1, j)
            fw.barrier()
            fw.run()
    except _Stop:
        pass
    return nc


def _t5_bucket_np(dist):
    dist = np.asarray(dist, np.int32)
    d_f = np.maximum(dist, 16).astype(np.float32)
    large = 16 + (np.log(d_f / np.float32(16)) / np.float32(math.log(2048 / 16)) * np.float32(16)).astype(np.int32)
    large = np.minimum(large, 31)
    return np.where(dist < 16, dist, large)


def _consts():
    delta = np.arange(TBL) - 128
    valid = delta >= 0
    bucket = _t5_bucket_np(np.maximum(delta, 0))
    onehot = np.zeros((32, TBL), np.float32)
    onehot[bucket, np.arange(TBL)] = 1.0
    onehot[:, ~valid] = 0.0
    m = ((delta >= 0) & (delta <= 128)).astype(np.float32) \
        + ((delta >= 0) & (delta <= 512) & (delta % 4 == 0)).astype(np.float32) \
        + ((delta >= 0) & (delta <= 2048) & (delta % 16 == 0)).astype(np.float32)
    mult = np.broadcast_to(m[None, :], (8, TBL)).astype(np.float32).copy()
    return onehot, mult


_NC_CACHE = {}


def kernel(x_prompt, x_sample, cache_k, cache_v, state_conv, state_h, norm1_g, w_in, rel_bias,
           conv_w, conv_b, gate_a_w, gate_a_b, gate_x_w, gate_x_b, lru_lambda, att_out_g,
           rnn_out_g, w_out, norm2_g, w_mlp_in, w_mlp_out, final_g):
    f32 = np.float32
    x_prompt = np.asarray(x_prompt, f32)
    x_sample = np.asarray(x_sample, f32)
    cache_k = np.asarray(cache_k, f32)
    cache_v = np.asarray(cache_v, f32)
    state_conv = np.asarray(state_conv, f32)
    state_h = np.asarray(state_h, f32)

    def fm(v, n):
        return np.asarray(v, f32).reshape(n, 128).T

    vecs = np.zeros((128, NV), f32)
    vecs[:, 0:8] = fm(norm1_g[0], 8)
    vecs[:, 8:16] = fm(norm2_g[0], 8)
    vecs[:, 16:20] = fm(att_out_g[0], 4)
    vecs[:, 20:24] = fm(rnn_out_g[0], 4)
    cw = np.asarray(conv_w[0], f32)
    for c in range(4):
        for j in range(4):
            vecs[:, 24 + c * 4 + j] = cw[j, c * 128:(c + 1) * 128]
    vecs[:, 40:44] = fm(conv_b[0], 4)
    vecs[:, 44:48] = fm(np.asarray(gate_a_b[0], f32).reshape(512), 4)
    vecs[:, 48:52] = fm(np.asarray(gate_x_b[0], f32).reshape(512), 4)
    vecs[:, 52:56] = fm(lru_lambda[0], 4)
    onehot, mult = _consts()
    shared = dict(
        w_in=np.ascontiguousarray(np.asarray(w_in[0], f32)), w_out=np.ascontiguousarray(np.asarray(w_out[0], f32)),
        w_mi=np.ascontiguousarray(np.asarray(w_mlp_in[0], f32)), w_mo=np.ascontiguousarray(np.asarray(w_mlp_out[0], f32)),
        vecs=vecs, fg=np.asarray(final_g, f32), gaw=np.ascontiguousarray(np.asarray(gate_a_w[0], f32)),
        gxw=np.ascontiguousarray(np.asarray(gate_x_w[0], f32)), relb=np.ascontiguousarray(np.asarray(rel_bias, f32)),
        onehot=onehot, mult=mult, identb=np.eye(128).astype(ml_dtypes.bfloat16), identf=np.eye(128, dtype=f32),
        jmat=np.ascontiguousarray(np.eye(128)[::-1]).astype(ml_dtypes.bfloat16),
    )
    rows_sel = np.array([r for r in range(1536) if r % 16 < 8] + list(range(1536, SEQ)))
    assert len(rows_sel) == NKEEP
    sel0 = np.zeros((128, 128), f32)
    sel1 = np.zeros((128, 128), f32)
    for ik in range(128):
        if ik % 16 < 8:
            sel0[ik, (ik // 16) * 8 + ik % 16] = 1.0
            sel1[ik, (8 + ik // 16) * 8 + ik % 16] = 1.0
    shared["sel0"] = sel0.astype(ml_dtypes.bfloat16)
    shared["sel1"] = sel1.astype(ml_dtypes.bfloat16)
    in_maps = []
    for c in range(NCORES):
        ck = cache_k[0, c * NSS:(c + 1) * NSS][:, rows_sel]
        ckT = np.ascontiguousarray(ck.transpose(0, 2, 3, 1)).reshape(NSS, 4, 128, NKEEP)
        sc = state_conv[0, c * NSS:(c + 1) * NSS]
        sconvT = np.ascontiguousarray(sc.reshape(NSS, 3, 4, 128).transpose(3, 2, 0, 1))
        sh = state_h[0, c * NSS:(c + 1) * NSS]
        shT = np.ascontiguousarray(sh.reshape(NSS, 4, 128).transpose(2, 1, 0))
        m = dict(shared)
        m.update(
            xp=np.ascontiguousarray(x_prompt[c * NPS:(c + 1) * NPS].reshape(NTP, D)),
            xs=np.ascontiguousarray(x_sample[c * NSS:(c + 1) * NSS].reshape(NSS * TS, D)),
            ckT=ckT, cv=np.ascontiguousarray(cache_v[0, c * NSS:(c + 1) * NSS][:, rows_sel].reshape(NSS, NKEEP, 512)),
            sconvT=sconvT, shT=shT,
        )
        in_maps.append(m)
    if "nc" not in _NC_CACHE:
        _NC_CACHE["nc"] = build_program()
    nc = _NC_CACHE["nc"]
    res = run_bass_kernel_spmd(nc, in_maps, core_ids=list(range(NCORES)))
    R = res.results

    def cat(name, shape):
        return np.concatenate([np.asarray(r[name], f32).reshape(shape) for r in R], axis=0)

    y_prompt = cat("yp", (NPS, SEQ, D))
    y_sample = cat("ys", (NSS, TS, D))
    nk_p = cat("nkp", (NPS, SEQ, 8, 64))[None]
    nv_p = cat("nvp", (NPS, SEQ, 8, 64))[None]
    nc_p = cat("ncp", (NPS, 3, 512))[None]
    nh_p = cat("nhp", (NPS, 512))[None]
    nk_s = cat("nks", (NSS, TS, 8, 64))[None]
    nv_s = cat("nvs", (NSS, TS, 8, 64))[None]
    nc_s = cat("ncs", (NSS, 3, 512))[None]
    nh_s = cat("nhs", (NSS, 512))[None]
    return (y_prompt, y_sample, nk_p, nv_p, nc_p, nh_p, nk_s, nv_s, nc_s, nh_s)


if __name__ == "__main__":
    import time
    t0 = time.time()
    nc = build_program()
    print("built in", time.time() - t0, "n_instructions", nc.n_instructions())
```
